# Optimizing a Trainium2 kernel written in Bass

```python
import functools
import jax, jax.numpy as jnp
from jax import lax
import numpy as np

D_MODEL = 1024
BATCH = 32
SEQ = 256
DEPTH = 4
DEC_BATCH = 2
DEC_SEQ = 2048
PAST_LEN = 512

GRID_W = 64
N_HEADS = 8
HEAD_DIM = 64
ATTN_W = N_HEADS * HEAD_DIM
NA_KH = 8
NA_KW = 16
POOL_WINDOWS = (2, 4, 8, 16)
N_POOL = 4
POOL_GRP = 128
POOL_W = N_POOL * POOL_GRP
LRU_W = 512
LRU_BLOCKS = 8
LRU_BLK = LRU_W // LRU_BLOCKS
LRU_C = 8.0
CONV_W = 4
N_BRANCH = 3
FF_HIDDEN = -(-8 * D_MODEL // (3 * 256)) * 256
SPLITS = (POOL_W, POOL_W + ATTN_W, POOL_W + 2 * ATTN_W, POOL_W + 3 * ATTN_W,
          POOL_W + 3 * ATTN_W + LRU_W, POOL_W + 3 * ATTN_W + 2 * LRU_W)
IN_W = POOL_W + 3 * ATTN_W + 2 * LRU_W + N_BRANCH * D_MODEL
Q_BLOCK = 128
EPS = 1e-6
NEG_INF = -1e30
ATTN_SCALE = HEAD_DIM ** -0.5

kernel_name = "hybrid_flow_pool_na_rglru_step"


def rmsnorm(x, g):
    xf = x.astype(jnp.float32)
    y = xf * lax.rsqrt(jnp.mean(xf * xf, axis=-1, keepdims=True) + EPS)
    return (y * g.astype(jnp.float32)).astype(x.dtype)


def pool_branch(up, pool_w, pool_scale):
    b, l, _ = up.shape
    xg = up.astype(jnp.float32).reshape(b, l, N_POOL, POOL_GRP)
    s = jnp.concatenate([jnp.zeros((b, 1, N_POOL, POOL_GRP), jnp.float32),
                         jnp.cumsum(xg, axis=1)], axis=1)
    t = jnp.arange(l)[:, None]
    half = jnp.array(POOL_WINDOWS, jnp.int32)[None, :] // 2
    lo = jnp.clip(t - half, 0, l)
    hi = jnp.clip(t + half, 0, l)
    g = jnp.arange(N_POOL)[None, :]
    cnt = (hi - lo).astype(jnp.float32)[None, :, :, None]
    d = (s[:, hi, g] - s[:, lo, g]) / cnt - xg
    y = jnp.einsum('blgc,gcd->blgd', d, pool_w.astype(jnp.float32)).reshape(b, l, POOL_W)
    return (y * pool_scale.astype(jnp.float32)).astype(up.dtype)


def context_attention(q, k, v):
    b, l, h, d = q.shape
    nb = l // Q_BLOCK
    qb = q.reshape(b, nb, Q_BLOCK, h, d).transpose(1, 0, 2, 3, 4)

    def one_block(qi):
        s = jnp.einsum('bqhd,bkhd->bhqk', qi, k).astype(jnp.float32) * ATTN_SCALE
        p = jax.nn.softmax(s, axis=-1).astype(v.dtype)
        return jnp.einsum('bhqk,bkhd->bqhd', p, v)

    o = lax.map(one_block, qb)
    return o.transpose(1, 0, 2, 3, 4).reshape(b, l, h, d)


def neighborhood_attention(q, k, v, ck, cv, rpb):
    b, l, h, d = q.shape
    rows = l // GRID_W
    w = GRID_W
    kh = min(NA_KH, rows)
    qg = q.reshape(b, rows, w, h, d)
    kg = k.reshape(b, rows, w, h, d)
    vg = v.reshape(b, rows, w, h, d)
    r = jnp.arange(rows)
    row0 = jnp.clip(r - kh // 2, 0, rows - kh)
    ridx = row0[:, None] + jnp.arange(kh)[None, :]
    kb = kg[:, ridx].reshape(b, rows, kh * w, h, d)
    vb = vg[:, ridx].reshape(b, rows, kh * w, h, d)
    cidx = jnp.arange(w)
    col0 = jnp.clip(cidx - NA_KW // 2, 0, w - NA_KW)
    col_ok = (cidx[None, :] >= col0[:, None]) & (cidx[None, :] < col0[:, None] + NA_KW)
    mask = jnp.broadcast_to(col_ok[:, None, :], (w, kh, w)).reshape(w, kh * w)
    dr = ridx - r[:, None] + (NA_KH - 1)
    dc = jnp.clip(cidx[None, :] - cidx[:, None], -(NA_KW - 1), NA_KW - 1) + (NA_KW - 1)
    bias = rpb[:, dr[:, None, :, None], dc[None, :, None, :]]
    bias = bias.reshape(h, rows, w, kh * w).astype(jnp.float32)
    s_lat = jnp.einsum('brqhd,brkhd->bhrqk', qg, kb).astype(jnp.float32) * ATTN_SCALE + bias
    s_lat = jnp.where(mask, s_lat, NEG_INF)
    s_ctx = jnp.einsum('brqhd,bkhd->bhrqk', qg, ck).astype(jnp.float32) * ATTN_SCALE
    p = jax.nn.softmax(jnp.concatenate([s_lat, s_ctx], axis=-1), axis=-1).astype(v.dtype)
    p_lat, p_ctx = p[..., :kh * w], p[..., kh * w:]
    o = (jnp.einsum('bhrqk,brkhd->brqhd', p_lat, vb)
         + jnp.einsum('bhrqk,bkhd->brqhd', p_ctx, cv))
    return o.reshape(b, l, h, d)


def linear_scan(a, bx, h0, reverse):
    def comb(left, right):
        return (left[0] * right[0], right[0] * left[1] + right[1])
    a_cum, b_cum = lax.associative_scan(comb, (a, bx), axis=1, reverse=reverse)
    return a_cum * h0[:, None, :] + b_cum


def rglru_branch(ux, conv_w, conv_b, wa, ba, wx, bx, lam, h0f, h0b):
    b, l, _ = ux.shape
    lpad = CONV_W // 2
    xp = jnp.pad(ux, ((0, 0), (lpad, CONV_W - 1 - lpad), (0, 0)))
    xc = conv_b
    for j in range(CONV_W):
        xc = xc + conv_w[j] * xp[:, j:j + l]
    xf = xc.astype(jnp.float32)
    xb = xf.reshape(b, l, LRU_BLOCKS, LRU_BLK)
    rg = jax.nn.sigmoid(jnp.einsum('blnc,encd->eblnd', xb, wa.astype(jnp.float32)).reshape(2, b, l, LRU_W)
                        + ba.astype(jnp.float32)[:, None, None, :])
    ig = jax.nn.sigmoid(jnp.einsum('blnc,encd->eblnd', xb, wx.astype(jnp.float32)).reshape(2, b, l, LRU_W)
                        + bx.astype(jnp.float32)[:, None, None, :])
    log_a = LRU_C * rg * jax.nn.log_sigmoid(lam.astype(jnp.float32))[:, None, None, :]
    a = jnp.exp(log_a)
    inp = jnp.sqrt(-jnp.expm1(2.0 * log_a)) * ig * xf[None]
    hf = linear_scan(a[0], inp[0], h0f, False)
    hb = linear_scan(a[1], inp[1], h0b, True)
    return hf, hb


def block(x, mod, attn_fn, h0f, h0b, g1, g2, w_in, pool_w, pool_scale, conv_w, conv_b,
          wa, ba, wx, bx, lam, w_branch, w_out, w_ff_in, w_ff_out):
    b, l, _ = x.shape
    sh1, sc1, gt1, sh2, sc2, gt2 = jnp.split(mod, 6, axis=-1)
    h = rmsnorm(x, g1) * (1 + sc1) + sh1
    u = h @ w_in
    u_pool, u_q, u_k, u_v, u_x, u_y, u_g = jnp.split(u, SPLITS, axis=-1)
    q = u_q.reshape(b, l, N_HEADS, HEAD_DIM)
    k = u_k.reshape(b, l, N_HEADS, HEAD_DIM)
    v = u_v.reshape(b, l, N_HEADS, HEAD_DIM)
    y_a = pool_branch(u_pool, pool_w, pool_scale)
    y_b = attn_fn(q, k, v).reshape(b, l, ATTN_W)
    hf, hb = rglru_branch(u_x, conv_w, conv_b, wa, ba, wx, bx, lam, h0f, h0b)
    y_c = ((hf + hb) * jax.nn.gelu(u_y.astype(jnp.float32))).astype(x.dtype)
    br = jnp.stack([y_a, y_b.astype(x.dtype), y_c], axis=2)
    proj = jnp.einsum('blnw,nwd->blnd', br, w_branch)
    gates = jax.nn.sigmoid(u_g.astype(jnp.float32)).reshape(b, l, N_BRANCH, D_MODEL)
    merged = jnp.sum(gates * proj.astype(jnp.float32), axis=2).astype(x.dtype)
    x = x + gt1 * (merged @ w_out)
    h2 = rmsnorm(x, g2) * (1 + sc2) + sh2
    fg, fu = jnp.split(h2 @ w_ff_in, 2, axis=-1)
    x = x + gt2 * ((jax.nn.silu(fg) * fu) @ w_ff_out)
    return x, k, v, hf, hb


def setup_inputs(seed: int = 0) -> dict:
    key = jax.random.key(seed)
    ks = jax.random.split(key, 32)
    f32 = jnp.float32
    nrm = lambda k, shape, s: jax.random.normal(k, shape, f32) * s
    u = jax.random.uniform(ks[21], (DEPTH, 2, LRU_W), f32, minval=0.9, maxval=0.999)
    p = u ** (1.0 / LRU_C)
    lam = jnp.log(p) - jnp.log1p(-p)
    return {
        "x_prompt": nrm(ks[0], (BATCH, SEQ, D_MODEL), 1.0),
        "x_sample": nrm(ks[1], (DEC_BATCH, DEC_SEQ, D_MODEL), 1.0),
        "cache_k": nrm(ks[2], (DEC_BATCH, DEPTH, PAST_LEN, N_HEADS, HEAD_DIM), 1.0),
        "cache_v": nrm(ks[3], (DEC_BATCH, DEPTH, PAST_LEN, N_HEADS, HEAD_DIM), 1.0),
        "state_lru": nrm(ks[4], (DEC_BATCH, DEPTH, 2, LRU_W), 0.5),
        "c": nrm(ks[5], (DEC_BATCH, D_MODEL), 1.0),
        "c_ctx": nrm(ks[6], (D_MODEL,), 1.0),
        "w_mod": nrm(ks[7], (DEPTH, D_MODEL, 6 * D_MODEL), 0.5 * D_MODEL ** -0.5),
        "b_mod": nrm(ks[8], (DEPTH, 6 * D_MODEL), 0.01),
        "g_norm1": 1.0 + nrm(ks[9], (DEPTH, D_MODEL), 0.05),
        "g_norm2": 1.0 + nrm(ks[10], (DEPTH, D_MODEL), 0.05),
        "w_in": nrm(ks[11], (DEPTH, D_MODEL, IN_W), D_MODEL ** -0.5),
        "pool_w": nrm(ks[12], (DEPTH, N_POOL, POOL_GRP, POOL_GRP), POOL_GRP ** -0.5),
        "pool_scale": 1.0 + nrm(ks[13], (DEPTH, POOL_W), 0.05),
        "na_rpb": nrm(ks[14], (DEPTH, N_HEADS, 2 * NA_KH - 1, 2 * NA_KW - 1), 0.1),
        "lru_conv_w": nrm(ks[15], (DEPTH, CONV_W, LRU_W), CONV_W ** -0.5),
        "lru_conv_b": nrm(ks[16], (DEPTH, LRU_W), 0.01),
        "lru_wa": nrm(ks[17], (DEPTH, 2, LRU_BLOCKS, LRU_BLK, LRU_BLK), LRU_BLK ** -0.5),
        "lru_ba": nrm(ks[18], (DEPTH, 2, LRU_W), 0.01),
        "lru_wx": nrm(ks[19], (DEPTH, 2, LRU_BLOCKS, LRU_BLK, LRU_BLK), LRU_BLK ** -0.5),
        "lru_bx": nrm(ks[20], (DEPTH, 2, LRU_W), 0.01),
        "lru_lambda": lam,
        "w_branch": nrm(ks[22], (DEPTH, N_BRANCH, ATTN_W, D_MODEL), ATTN_W ** -0.5),
        "w_out": nrm(ks[23], (DEPTH, D_MODEL, D_MODEL), D_MODEL ** -0.5),
        "w_ff_in": nrm(ks[24], (DEPTH, D_MODEL, 2 * FF_HIDDEN), D_MODEL ** -0.5),
        "w_ff_out": nrm(ks[25], (DEPTH, FF_HIDDEN, D_MODEL), FF_HIDDEN ** -0.5),
        "g_final": 1.0 + nrm(ks[26], (D_MODEL,), 0.05),
    }


def reference(x_prompt, x_sample, cache_k, cache_v, state_lru, c, c_ctx, w_mod, b_mod,
              g_norm1, g_norm2, w_in, pool_w, pool_scale, na_rpb, lru_conv_w, lru_conv_b,
              lru_wa, lru_ba, lru_wx, lru_bx, lru_lambda, w_branch, w_out, w_ff_in, w_ff_out,
              g_final):
    lws = [(g_norm1[l], g_norm2[l], w_in[l], pool_w[l], pool_scale[l], lru_conv_w[l], lru_conv_b[l],
            lru_wa[l], lru_ba[l], lru_wx[l], lru_bx[l], lru_lambda[l], w_branch[l], w_out[l],
            w_ff_in[l], w_ff_out[l]) for l in range(DEPTH)]

    xc = x_prompt
    n_ctx = x_prompt.shape[0]
    h0 = jnp.zeros((n_ctx, LRU_W), jnp.float32)
    ks_out, vs_out, hs_out = [], [], []
    for l in range(DEPTH):
        mod_ctx = jax.nn.silu(c_ctx) @ w_mod[l] + b_mod[l]
        xc, k, v, hf, hb = block(xc, mod_ctx, context_attention, h0, h0, *lws[l])
        ks_out.append(k)
        vs_out.append(v)
        hs_out.append(jnp.stack([hf[:, -1], hb[:, 0]], axis=1).astype(x_prompt.dtype))

    xs = x_sample
    for l in range(DEPTH):
        mod_lat = (jax.nn.silu(c) @ w_mod[l] + b_mod[l])[:, None, :]
        attn = functools.partial(neighborhood_attention, ck=cache_k[:, l], cv=cache_v[:, l], rpb=na_rpb[l])
        xs, _, _, _, _ = block(xs, mod_lat, attn,
                               state_lru[:, l, 0].astype(jnp.float32),
                               state_lru[:, l, 1].astype(jnp.float32), *lws[l])

    y_prompt = rmsnorm(xc, g_final)
    y_sample = rmsnorm(xs, g_final)
    new_cache_k = jnp.stack(ks_out, axis=1)
    new_cache_v = jnp.stack(vs_out, axis=1)
    new_state_lru = jnp.stack(hs_out, axis=1)
    return (y_prompt, y_sample, new_cache_k, new_cache_v, new_state_lru)
```

```python
import numpy as np
from contextlib import ExitStack
import concourse.bass as bass
import concourse.mybir as mybir
from concourse.bass_utils import run_bass_kernel_spmd

F32 = mybir.dt.float32
BF16 = mybir.dt.bfloat16
AF = mybir.ActivationFunctionType
ALU = mybir.AluOpType

D = 1024
DEPTH = 4
NCORES = 8
TP = 1024
TS = 512
TT = TP + TS
NT = 3
INW = 6144
FFH = 2816
EPS = 1e-6
SCALE = 0.125
NV = 112
NEG = -30000.0
GROUPS = [[0, 1, 2, 3], [4, 5, 6, 7]]


class _Stop(Exception):
    pass


class Arena:
    def __enter__(self):
        self.es = ExitStack()
        return self.es

    def __exit__(self, et, ev, tb):
        if et is None or et is _Stop:
            self.es.close()
        return False


class Sem:
    def __init__(self, handle, name):
        self.h = handle
        self.name = name
        self.count = 0


class T:
    __slots__ = ("name", "w", "rs", "dsem_w", "dsem_r", "excl")

    def __init__(self, name, excl=False):
        self.name = name
        self.excl = excl
        self.w = None
        self.rs = []
        self.dsem_w = None
        self.dsem_r = None


class Eng:
    def __init__(self, name, h, sem):
        self.name = name
        self.h = h
        self.sem = sem
        self.waited = {}
        self.nwaits = 0
        self.nops = 0


class FW:
    def __init__(self, nc, stack, same_engine_sync=True):
        self.nc = nc
        self.stack = stack
        self.same_engine_sync = same_engine_sync
        self.engs = {}
        self.nsem = 0
        self.sems = []
        self.free_dma = []
        for name, h in (("pe", nc.tensor), ("act", nc.scalar), ("dve", nc.vector),
                        ("pool", nc.gpsimd), ("sp", nc.sync)):
            self.engs[name] = Eng(name, h, self.new_sem("e_" + name))

    def new_sem(self, name):
        self.nsem += 1
        sm = Sem(self.stack.enter_context(self.nc.semaphore(name)), name)
        self.sems.append(sm)
        return sm

    def dma_sem(self):
        if self.free_dma:
            return self.free_dma.pop()
        return self.new_sem(f"dq{self.nsem}")

    def _needs(self, reads, writes):
        needs = {}
        for t in reads:
            if t.w is not None:
                s, v = t.w
                if needs.get(s, 0) < v:
                    needs[s] = v
            if t.excl:
                for (s, v) in t.rs:
                    if needs.get(s, 0) < v:
                        needs[s] = v
        for t in writes:
            if t.w is not None:
                s, v = t.w
                if needs.get(s, 0) < v:
                    needs[s] = v
            for (s, v) in t.rs:
                if needs.get(s, 0) < v:
                    needs[s] = v
        return needs

    def _emit_waits(self, eng, needs):
        for s, v in needs.items():
            if s is eng.sem and not self.same_engine_sync:
                continue
            if eng.waited.get(s, 0) >= v:
                continue
            eng.h.wait_ge(s.h, v)
            eng.waited[s] = v
            eng.nwaits += 1

    def _record(self, sv, reads, writes):
        for t in reads:
            t.rs.append(sv)
            if len(t.rs) > 24:
                m = {}
                for (s, v) in t.rs:
                    if m.get(s, 0) < v:
                        m[s] = v
                t.rs = list(m.items())
        for t in writes:
            t.w = sv
            t.rs = []

    def op(self, ename, fn, reads=(), writes=()):
        eng = self.engs[ename]
        self._emit_waits(eng, self._needs(reads, writes))
        ins = fn(eng.h)
        eng.sem.count += 1
        ins.then_inc(eng.sem.h, 1)
        eng.nops += 1
        self._record((eng.sem, eng.sem.count), reads, writes)
        return ins

    def mm(self, fns, reads=(), writes=()):
        eng = self.engs["pe"]
        self._emit_waits(eng, self._needs(reads, writes))
        ins = None
        for fn in fns:
            ins = fn(eng.h)
            eng.nops += 1
        eng.sem.count += 1
        ins.then_inc(eng.sem.h, 1)
        self._record((eng.sem, eng.sem.count), reads, writes)

    def dma(self, qname, out_ap, in_ap, reads=(), writes=(), sem_tile=None):
        eng = self.engs[qname]
        self._emit_waits(eng, self._needs(reads, writes))
        if sem_tile is None:
            sem_tile = writes[0] if writes else reads[0]
        if writes:
            if sem_tile.dsem_w is None:
                sem_tile.dsem_w = self.dma_sem()
            s = sem_tile.dsem_w
        else:
            if sem_tile.dsem_r is None:
                sem_tile.dsem_r = self.dma_sem()
            s = sem_tile.dsem_r
        ins = eng.h.dma_start(out=out_ap, in_=in_ap)
        s.count += 16
        ins.then_inc(s.h, 16)
        eng.nops += 1
        self._record((s, s.count), reads, writes)
        return ins

    def collective(self, kind, groups, in_ap, out_ap, reads=(), writes=()):
        eng = self.engs["pool"]
        self._emit_waits(eng, self._needs(reads, writes))
        sem_tile = writes[0]
        if sem_tile.dsem_w is None:
            sem_tile.dsem_w = self.new_sem("cc_" + sem_tile.name)
        s = sem_tile.dsem_w
        ins = eng.h.collective_compute(kind, ALU.bypass, replica_groups=groups, ins=[in_ap], outs=[out_ap])
        s.count += 1
        ins.then_inc(s.h, 1)
        self._record((s, s.count), reads, writes)

    def release(self, tiles, engines=("pe", "act", "dve", "pool", "sp")):
        needs = self._needs((), tiles)
        for e in engines:
            self._emit_waits(self.engs[e], needs)
        if len(engines) == 5:
            for t in tiles:
                for s_ in (t.dsem_w, t.dsem_r):
                    if s_ is not None and s_.name.startswith("dq"):
                        self.free_dma.append(s_)
                t.dsem_w = None
                t.dsem_r = None
                t.w = None
                t.rs = []

    def stats(self):
        return {k: (e.nops, e.nwaits) for k, e in self.engs.items()}, self.nsem


def build_program(depth=DEPTH, same_engine_sync=True, debug=False, stop_after=None):
    nc = bass.Bass("TRN2", target_bir_lowering=False)
    dbg_list = []

    def din(name, shape, dt=F32):
        return nc.dram_tensor(name, list(shape), dt, kind="ExternalInput").ap()

    def dout(name, shape, dt=F32):
        return nc.dram_tensor(name, list(shape), dt, kind="ExternalOutput").ap()

    def dint(name, shape, dt=F32):
        return nc.dram_tensor(name, list(shape), dt, kind="Internal").ap()

    xT = din("xT", [D, TT])
    xh0 = din("xh0", [128, 8, 16])
    ckT = din("ckT", [depth, 512, 512])
    cv = din("cv", [depth, 512, 512])
    st0 = din("st0", [128, depth * 2 * 4])
    cvec = din("cvec", [128, 8, 2])
    pvec = din("pvec", [128, depth, NV])
    gfin = din("gfin", [128, 8])
    w_mod = din("w_mod", [depth, D, INW])
    w_in = din("w_in", [depth, D, INW])
    w_branch = din("w_branch", [depth, 3, 512, D])
    w_out = din("w_out", [depth, D, D])
    w_ff_in = din("w_ff_in", [depth, D, 2 * FFH])
    w_ff_out = din("w_ff_out", [depth, FFH, D])
    pool_w = din("pool_w", [depth, 128, 4 * 128])
    lru_bd = din("lru_bd", [depth, 128, 16 * 128])
    rpbG = din("rpbG", [depth, 64, 8, 15, 64])
    slotoh = din("slotoh", [128, 1024])
    rowb = din("rowb", [128, 512])
    sel = din("sel", [128, 12])
    flags = din("flags", [128, 2])
    icp = din("icp", [128, 4, 2, 8])
    ics = din("ics", [128, 4, 2, 8])

    yT = dout("yT", [D, TT])
    kT_out = dout("kT_out", [depth, 512, TP])
    v_out = dout("v_out", [depth, TP, 512])
    st_out = dout("st_out", [depth, 128, 32])

    ckv_in = [dint(f"ckv_in{i}", [2, 128, 512], BF16) for i in range(2)]
    ckv_out = [dint(f"ckv_out{i}", [4, 2, 128, 512], BF16) for i in range(2)]
    cx_in = dint("cx_in", [128, 128])
    cx_out = dint("cx_out", [4, 128, 128])
    cs_in = [dint(f"cs_in{i}", [128, 4]) for i in range(2)]
    cs_out = [dint(f"cs_out{i}", [4, 128, 4]) for i in range(2)]

    with ExitStack() as st:
        fw = FW(nc, st, same_engine_sync=same_engine_sync)

        uid = {"i": 0}

        def sb(name, shape, dt, stack=st):
            uid["i"] += 1
            return stack.enter_context(nc.sbuf_tensor(f"{name}_{uid['i']}", list(shape), dt))

        X = sb("X", [128, 8, TT], F32)
        H = sb("H", [128, 8, TT], BF16)
        YB = sb("YB", [128, 3, 4, TT], BF16)
        WST = [sb(f"WST{i}", [128, 2048], F32) for i in range(3)]
        WRG = [sb(f"WRG{i}", [128, 2048], BF16) for i in range(4)]
        PV = sb("PV", [128, depth, NV], F32)
        GF = sb("GF", [128, 8], F32)
        CV = sb("CV", [128, 8, 2], F32)
        SC = sb("SC", [128, 8, 2], BF16)
        MODV = [sb(f"MODV{i}", [128, 48, 2], F32) for i in range(2)]
        AB = sb("AB", [128, 2, 8, 2], F32)
        CL = sb("CL", [128, 8], F32)
        CL2 = sb("CL2", [128, 8], F32)
        CLH = sb("CLH", [128, 8], F32)
        CL256 = sb("CL256", [128, 8], F32)
        HBV = sb("HBV", [128, 16], F32)
        ONES = sb("ONES", [128, 128], BF16)
        ONE1 = sb("ONE1", [128, 64], BF16)
        EPSC = sb("EPSC", [128, 1], F32)
        SEL = sb("SEL", [128, 12], F32)
        FLG = sb("FLG", [128, 2], F32)
        ICP = sb("ICP", [128, 4, 2, 8], F32)
        ICS = sb("ICS", [128, 4, 2, 8], F32)
        ST0 = sb("ST0", [128, depth * 8], F32)
        SOH = sb("SOH", [128, 1024], BF16)
        RWB = sb("RWB", [128, 512], BF16)
        XH = sb("XH", [128, 8, 16], F32)
        HH = sb("HH", [128, 8, 16], BF16)
        STO = sb("STO", [128, 32], F32)

        PS = [st.enter_context(nc.psum_tensor(f"PS{i}", [128, 512], F32)) for i in range(8)]
        tPS = [T(f"PS{i}", excl=True) for i in range(8)]

        tX = [[T(f"X{k}_{t}") for t in range(NT)] for k in range(8)]
        tH = [[T(f"H{k}_{t}") for t in range(NT)] for k in range(8)]
        tYB = [[[T(f"Y{n}_{c}_{t}") for t in range(NT)] for c in range(4)] for n in range(3)]
        tWST = [T(f"WST{i}") for i in range(3)]
        tWRG = [T(f"WRG{i}") for i in range(4)]
        tC = T("consts")
        tMODV = [T("MODV0"), T("MODV1")]
        tAB = T("AB")
        tXH = T("XH")
        tHH = T("HH")
        tSTO = T("STO")
        tT2 = [T("T2r0"), T("T2r1")]

        def tsl(t):
            return slice(t * 512, (t + 1) * 512)

        def chk(name):
            if stop_after == name:
                raise _Stop()

        grp_of_tile = [0, 0, 1]

        fw.dma("sp", PV[:], pvec[:, :, :], writes=[tC])
        fw.dma("sp", GF[:], gfin[:, :], writes=[tC])
        fw.dma("sp", CV[:], cvec[:, :, :], writes=[tC])
        fw.dma("sp", SEL[:], sel[:, :], writes=[tC])
        fw.dma("sp", FLG[:], flags[:, :], writes=[tC])
        fw.dma("sp", ICP[:], icp[:, :, :, :], writes=[tC])
        fw.dma("sp", ICS[:], ics[:, :, :, :], writes=[tC])
        fw.dma("sp", ST0[:], st0[:, :], writes=[tC])
        fw.dma("sp", XH[:], xh0[:, :, :], writes=[tXH])
        tSF = T("SMALLF")
        a0 = ExitStack()
        SMALLF = sb("SMALLF", [128, 1024], F32, a0)
        fw.dma("sp", SMALLF[:], slotoh[:, :], writes=[tSF])
        fw.op("dve", lambda e: e.tensor_copy(out=SOH[:], in_=SMALLF[:]), reads=[tSF], writes=[tC])
        fw.dma("sp", SMALLF[:, 0:512], rowb[:, :], writes=[tSF])
        fw.op("dve", lambda e: e.tensor_copy(out=RWB[:], in_=SMALLF[:, 0:512]), reads=[tSF], writes=[tC])
        fw.release([tSF])
        a0.close()
        fw.op("dve", lambda e: e.memset(ONES[:], 1.0 / 1024.0), writes=[tC])
        fw.op("dve", lambda e: e.memset(ONE1[:], 1.0), writes=[tC])
        fw.op("dve", lambda e: e.memset(EPSC[:], EPS), writes=[tC])
        fw.op("act", lambda e: e.activation(out=SC[:], in_=CV[:], func=AF.Silu), reads=[tC], writes=[tC])
        xv = xT.rearrange("(k p) t -> p k t", p=128)
        for k in range(8):
            for t in range(NT):
                fw.dma("sp", X[:, k, tsl(t)], xv[:, k, tsl(t)], writes=[tX[k][t]])

        plan = []
        state = {"issued": 0}

        def add_block(segs, kc):
            ncols = sum(s[1] for s in segs)
            assert kc * ncols <= 2048
            plan.append(dict(segs=segs, kc=kc, ncols=ncols))
            return len(plan) - 1

        def issue_block(i):
            b = plan[i]
            s_i, r_i = i % 3, i % 4
            kc, ncols = b["kc"], b["ncols"]
            dst = WST[s_i][:, 0:kc * ncols].rearrange("p (k n) -> p k n", k=kc)
            c0 = 0
            for (ap, n) in b["segs"]:
                fw.dma("sp", dst[:, :, c0:c0 + n], ap, writes=[tWST[s_i]])
                c0 += n
            fw.op("act", lambda e: e.activation(out=WRG[r_i][:, 0:kc * ncols], in_=WST[s_i][:, 0:kc * ncols], func=AF.Identity),
                  reads=[tWST[s_i]], writes=[tWRG[r_i]])

        def get_block(i, lookahead=2):
            while state["issued"] <= min(i + lookahead, len(plan) - 1):
                issue_block(state["issued"])
                state["issued"] += 1
            b = plan[i]
            r_i = i % 4
            w = WRG[r_i][:, 0:b["kc"] * b["ncols"]].rearrange("p (k n) -> p k n", k=b["kc"])
            return w, tWRG[r_i]

        def wcols(wm, l, c0, n):
            return wm[l].rearrange("(k p) n -> p k n", p=128)[:, :, c0:c0 + n]

        dps = {"i": 0, "n": 4}

        def next_dps():
            i = dps["i"] % dps["n"]
            dps["i"] += 1
            return PS[i], tPS[i]

        def plan_mod(l):
            return [add_block([(wcols(w_mod, l, c * 256, 256), 256)], 8) for c in range(24)]

        def emit_mod_block(l, c, bi):
            w, tw = get_block(bi)
            for mm_ in range(2):
                m = c * 2 + mm_
                fns = []
                for k in range(8):
                    fns.append(lambda e, k=k, m=m, mm_=mm_: e.matmul(
                        PS[7][:, 2 * m:2 * m + 2], lhsT=w[:, k, mm_ * 128:(mm_ + 1) * 128], rhs=SC[:, k, :],
                        start=(k == 0), stop=(k == 7)))
                fw.mm(fns, reads=[tw, tC], writes=[tPS[7]])

        def finish_mod(l):
            mv = MODV[l % 2]
            psv = PS[7][:, 0:96].rearrange("p (m g) -> p m g", g=2)
            for g in range(2):
                fw.op("dve", lambda e, g=g: e.tensor_tensor(out=mv[:, :, g], in0=psv[:, :, g], in1=PV[:, l, 16:64], op=ALU.add),
                      reads=[tPS[7], tC], writes=[tMODV[l % 2]])

        def emit_ab(l):
            mv = MODV[l % 2]
            for which, (g0, sc0) in enumerate(((0, 8), (8, 32))):
                for g in range(2):
                    fw.op("dve", lambda e, which=which, g=g, g0=g0, sc0=sc0: e.scalar_tensor_tensor(
                        out=AB[:, which, :, g], in0=mv[:, sc0:sc0 + 8, g], scalar=1.0, in1=PV[:, l, g0:g0 + 8],
                        op0=ALU.add, op1=ALU.mult), reads=[tMODV[l % 2], tC], writes=[tAB])
            fw.op("act", lambda e: e.activation(out=CL[:], in_=PV[:, l, 104:112], func=AF.Exp, scale=-1.0), reads=[tC], writes=[tAB])
            fw.op("act", lambda e: e.activation(out=CL[:], in_=CL[:], func=AF.Ln, bias=1.0), reads=[tAB], writes=[tAB])
            fw.op("dve", lambda e: e.tensor_scalar(out=CL2[:], in0=CL[:], scalar1=-16.0, scalar2=None, op0=ALU.mult), reads=[tAB], writes=[tAB])
            fw.op("dve", lambda e: e.tensor_scalar(out=CL[:], in0=CL[:], scalar1=-8.0, scalar2=None, op0=ALU.mult), reads=[tAB], writes=[tAB])
            fw.op("dve", lambda e: e.tensor_scalar(out=CLH[:], in0=CL[:], scalar1=0.5, scalar2=None, op0=ALU.mult), reads=[tAB], writes=[tAB])
            fw.op("dve", lambda e: e.tensor_scalar(out=CL256[:], in0=CL[:], scalar1=256.0, scalar2=None, op0=ALU.mult), reads=[tAB], writes=[tAB])
            fw.op("dve", lambda e: e.tensor_scalar(out=HBV[:], in0=PV[:, l, 88:104], scalar1=0.5, scalar2=None, op0=ALU.mult), reads=[tC], writes=[tAB])

        def emit_norm(l, which, arena):
            SQ, tSQ, RS, tRS, TMP, tTMP = arena
            mv = MODV[l % 2]
            b0 = 0 if which == 0 else 24
            pss = []
            qi = 0
            for t in range(NT):
                ps, tps = next_dps()
                pss.append((ps, tps))
                for k in range(8):
                    j = qi % len(SQ)
                    qi += 1
                    fw.op("pool", lambda e, k=k, j=j, t=t: e.tensor_tensor(out=SQ[j][:], in0=X[:, k, tsl(t)], in1=X[:, k, tsl(t)], op=ALU.mult),
                          reads=[tX[k][t]], writes=[tSQ[j]])
                    fw.mm([lambda e, k=k, j=j, ps=ps: e.matmul(ps[:], lhsT=ONES[:], rhs=SQ[j][:], start=(k == 0), stop=(k == 7))],
                          reads=[tSQ[j], tC], writes=[tps])
            for t in range(NT):
                ps, tps = pss[t]
                fw.op("act", lambda e, t=t, ps=ps: e.activation(out=RS[t][:], in_=ps[:], func=AF.Ln, bias=EPSC[:, 0:1]), reads=[tps, tC], writes=[tRS[t]])
                fw.op("act", lambda e, t=t: e.activation(out=RS[t][:], in_=RS[t][:], func=AF.Exp, scale=-0.5), reads=[tRS[t]], writes=[tRS[t]])
            qi = 0
            for t in range(NT):
                g = grp_of_tile[t]
                for k in range(8):
                    j = qi % len(TMP)
                    qi += 1
                    fw.op("dve", lambda e, k=k, j=j, t=t: e.tensor_tensor(out=TMP[j][:], in0=X[:, k, tsl(t)], in1=RS[t][:], op=ALU.mult),
                          reads=[tX[k][t], tRS[t]], writes=[tTMP[j]])
                    fw.op("act", lambda e, k=k, j=j, g=g, t=t: e.activation(
                        out=H[:, k, tsl(t)], in_=TMP[j][:], func=AF.Identity,
                        scale=AB[:, which, k, g:g + 1], bias=mv[:, b0 + k, g:g + 1]),
                        reads=[tTMP[j], tAB, tMODV[l % 2]], writes=[tH[k][t]])

        def emit_norm_halo(l, arena):
            SQ, tSQ, RS, tRS, TMP, tTMP = arena
            mv = MODV[l % 2]
            ps, tps = next_dps()
            fw.op("act", lambda e: e.activation(out=SQ[0][:, 0:128], in_=XH[:].rearrange("p k t -> p (k t)"), func=AF.Square),
                  reads=[tXH], writes=[tSQ[0]])
            sqv = SQ[0][:, 0:128].rearrange("p (k t) -> p k t", k=8)
            fw.mm([lambda e, k=k: e.matmul(ps[:, 0:16], lhsT=ONES[:], rhs=sqv[:, k, :], start=(k == 0), stop=(k == 7)) for k in range(8)],
                  reads=[tSQ[0], tC], writes=[tps])
            fw.op("act", lambda e: e.activation(out=RS[0][:, 0:16], in_=ps[:, 0:16], func=AF.Ln, bias=EPSC[:, 0:1]), reads=[tps, tC], writes=[tRS[0]])
            fw.op("act", lambda e: e.activation(out=RS[0][:, 0:16], in_=RS[0][:, 0:16], func=AF.Exp, scale=-0.5), reads=[tRS[0]], writes=[tRS[0]])
            for k in range(8):
                fw.op("dve", lambda e, k=k: e.tensor_tensor(out=TMP[0][:, 0:16], in0=XH[:, k, :], in1=RS[0][:, 0:16], op=ALU.mult),
                      reads=[tXH, tRS[0]], writes=[tTMP[0]])
                fw.op("act", lambda e, k=k: e.activation(out=HH[:, k, :], in_=TMP[0][:, 0:16], func=AF.Identity,
                                                          scale=AB[:, 0, k, 1:2], bias=mv[:, k, 1:2]),
                      reads=[tTMP[0], tAB, tMODV[l % 2]], writes=[tHH])

        def dense(w, tw, mcol, kcn, rhs_fn, rhs_tiles_fn, t, ps, tps, pcols=slice(0, 512)):
            fns = []
            rt = []
            for k in range(kcn):
                fns.append(lambda e, k=k: e.matmul(ps[:, pcols], lhsT=w[:, k, mcol:mcol + 128], rhs=rhs_fn(k, t),
                                                   start=(k == 0), stop=(k == kcn - 1)))
                rt.append(rhs_tiles_fn(k, t))
            fw.mm(fns, reads=[tw] + rt, writes=[tps])

        hrhs = lambda k, t: H[:, k, tsl(t)]
        hrt = lambda k, t: tH[k][t]

        def plan_layer(l):
            P = {}
            P["qk"] = [add_block([(wcols(w_in, l, 512 + m * 128, 128), 128), (wcols(w_in, l, 1024 + m * 128, 128), 128)], 8) for m in range(4)]
            P["v"] = [add_block([(wcols(w_in, l, 1536 + m * 128, 128), 128)], 8) for m in range(4)]
            return P

        order = []

        LP = []
        mod_blocks = {}
        mod_blocks[0] = plan_mod(0)
        for l in range(depth):
            P = {}
            P["qkv"] = []
            for m in range(4):
                P["qkv"].append((
                    add_block([(wcols(w_in, l, 512 + m * 128, 128), 128), (wcols(w_in, l, 1024 + m * 128, 128), 128)], 8),
                    add_block([(wcols(w_in, l, 1536 + m * 128, 128), 128)], 8)))
            P["poolw"] = add_block([(pool_w[l].rearrange("c (o n) -> c o n", o=1), 512)], 1)
            P["pool"] = [add_block([(wcols(w_in, l, gp * 256, 256), 256)], 8) for gp in range(2)]
            P["bd"] = add_block([(lru_bd[l].rearrange("c (o n) -> c o n", o=1), 2048)], 1)
            P["lru"] = [add_block([(wcols(w_in, l, 2048 + c * 128, 128), 128), (wcols(w_in, l, 2560 + c * 128, 128), 128)], 8) for c in range(4)]
            P["gate"] = []
            for jp in range(4):
                for n in range(3):
                    P["gate"].append((
                        add_block([(wcols(w_in, l, 3072 + n * 1024 + jp * 256, 256), 256)], 8),
                        add_block([(w_branch[l, n].rearrange("(k p) n -> p k n", p=128)[:, :, jp * 256:(jp + 1) * 256], 256)], 4)))
            P["out"] = [add_block([(wcols(w_out, l, mp * 256, 256), 256)], 8) for mp in range(4)]
            P["ffseq"] = []
            halves = ((0, 12), (12, 22))
            nmod = 0
            for hf, (c0, c1) in enumerate(halves):
                for i in range(c0, c1):
                    P["ffseq"].append(("ffin", (hf, i - c0), add_block(
                        [(wcols(w_ff_in, l, i * 128, 128), 128), (wcols(w_ff_in, l, FFH + i * 128, 128), 128)], 8)))
                    if l + 1 < depth:
                        for _ in range(2 if i in (5, 16) else 1):
                            if nmod < 24:
                                P["ffseq"].append(("mod", nmod, add_block([(wcols(w_mod, l + 1, nmod * 256, 256), 256)], 8)))
                                nmod += 1
                nk = c1 - c0
                for j in range(8):
                    P["ffseq"].append(("ffout", (hf, j, nk), add_block(
                        [(w_ff_out[l].rearrange("(k p) n -> p k n", p=128)[:, c0:c1, j * 128:(j + 1) * 128], 128)], nk)))
            assert nmod == (24 if l + 1 < depth else 0)
            LP.append(P)

        cons = {"i": 0}

        def take(bi):
            assert bi == cons["i"], (bi, cons["i"])
            cons["i"] += 1
            return get_block(bi)

        try:
            for c in range(24):
                bi = mod_blocks[0][c]
                assert bi == cons["i"]
                cons["i"] += 1
                emit_mod_block(0, c, bi)
            finish_mod(0)
            chk("mod")

            tKVI = [T("kvin0"), T("kvin1")]
            tKVO = [T("kvout0"), T("kvout1")]
            tCXI, tCXO = T("cxin"), T("cxout")
            tCSI = [T("csin0"), T("csin1")]
            tCSO = [T("csout0"), T("csout1")]

            for l in range(depth):
                P = LP[l]
                emit_ab(l)
                mv = MODV[l % 2]
                tmv = tMODV[l % 2]

                with Arena() as a1:
                    SQ = [sb(f"SQ{j}", [128, 512], BF16, a1) for j in range(4)]
                    RS = [sb(f"RS{j}", [128, 512], F32, a1) for j in range(3)]
                    TMP = [sb(f"TMP{j}", [128, 512], F32, a1) for j in range(4)]
                    tSQ = [T(f"SQ{j}") for j in range(4)]
                    tRS = [T(f"RS{j}") for j in range(3)]
                    tTMP = [T(f"TMP{j}") for j in range(4)]
                    arena = (SQ, tSQ, RS, tRS, TMP, tTMP)
                    emit_norm(l, 0, arena)
                    if l >= 1:
                        XCAND = sb("XCAND", [128, 4, 128], F32, a1)
                        XSEL = sb("XSEL", [128, 8, 8], F32, a1)
                        tXCAND, tXSEL = T("XCAND"), T("XSEL")
                        fw.dma("sp", XCAND[:], cx_out.rearrange("r p n -> p r n"), reads=[tCXO], writes=[tXCAND])
                        for (d0, s0, c0) in ((0, 0, 8), (8, 4, 0)):
                            cv_ = lambda r, c0=c0: XCAND[:, r, :].rearrange("p (k t) -> p k t", k=8)[:, :, c0:c0 + 8]
                            fw.op("dve", lambda e, d0=d0, s0=s0, cv_=cv_: e.tensor_scalar(out=XH[:, :, d0:d0 + 8], in0=cv_(0), scalar1=SEL[:, s0:s0 + 1],
                                                                                         scalar2=None, op0=ALU.mult), reads=[tXCAND, tC], writes=[tXH])
                            for r in range(1, 4):
                                fw.op("dve", lambda e, d0=d0, s0=s0, cv_=cv_, r=r: e.scalar_tensor_tensor(
                                    out=XH[:, :, d0:d0 + 8], in0=cv_(r), scalar=SEL[:, s0 + r:s0 + r + 1], in1=XH[:, :, d0:d0 + 8],
                                    op0=ALU.mult, op1=ALU.add), reads=[tXCAND, tC, tXH], writes=[tXH])
                    emit_norm_halo(l, arena)
                    chk("norm")
                    fw.release(tSQ + tRS + tTMP + ([tXCAND, tXSEL] if l >= 1 else []))

                with Arena() as a2:
                    QT = sb("QT", [128, TT], BF16, a2)
                    KT = sb("KT", [128, TT], BF16, a2)
                    VT = sb("VT", [128, 12, 128], BF16, a2)
                    KWs = [sb(f"KW{j}", [128, 1024], BF16, a2) for j in range(2)]
                    VWs = [sb(f"VW{j}", [128, 8, 128], BF16, a2) for j in range(2)]
                    CKs = [sb(f"CK{j}", [128, 512], BF16, a2) for j in range(2)]
                    CVBs = [sb(f"CVB{j}", [128, 4, 128], BF16, a2) for j in range(2)]
                    QSs = [sb(f"QS{j}", [128, 512], BF16, a2) for j in range(2)]
                    CAND = sb("CAND", [128, 4, 512], BF16, a2)
                    CSEL = sb("CSEL", [128, 256], BF16, a2)
                    tCSEL = T("CSEL")
                    T2 = [sb(f"T2r{j}", [128, 22, 64], F32, a2) for j in range(2)]
                    FS = [sb(f"FS{j}", [128, 512], F32, a2) for j in range(3)]
                    PT = [sb(f"PT{j}", [128, 512], BF16, a2) for j in range(4)]
                    RC = sb("RC", [128, 512], F32, a2)
                    tQT = [T(f"QT{t}") for t in range(NT)]
                    tKT = [T(f"KT{t}") for t in range(NT)]
                    tVT = [T(f"VT{t}") for t in range(NT)]
                    tKWs, tVWs, tCKs, tCVBs, tQSs = ([T(f"{n_}{j}") for j in range(2)] for n_ in ("KW", "VW", "CK", "CVB", "QS"))
                    tCAND = T("CAND")
                    tFS = [T(f"FS{j}") for j in range(3)]
                    tPT = [T("PT0"), T("PT1"), T("PT2"), T("PT3")]
                    tRC = T("RC")
                    arena_tiles = tQT + tKT + tVT + tKWs + tVWs + tCKs + tCVBs + tQSs + [tCAND, tCSEL] + tFS + tPT + [tRC] + tT2
                    for j in range(2):
                        fw.op("pool", lambda e, j=j: e.memset(T2[j][:], 0.0), writes=[tT2[j]])
                    fsc = {"i": 0}

                    def next_fs():
                        i = fsc["i"] % 3
                        fsc["i"] += 1
                        return FS[i], tFS[i]

                    cnt = {"sb": 0, "pt": 0, "sps": 0, "t2": 0}
                    dps["n"] = 3

                    def exp_tile(src_ps, tsrc, width, bias_ap=None, bias_tiles=()):
                        j = cnt["pt"] % 4
                        cnt["pt"] += 1
                        if bias_ap is None:
                            fw.op("act", lambda e: e.activation(out=PT[j][:, 0:width], in_=src_ps, func=AF.Exp, scale=SCALE),
                                  reads=[tsrc], writes=[tPT[j]])
                        else:
                            sbf, tsbf = next_fs()
                            fw.op("dve", lambda e: e.scalar_tensor_tensor(out=sbf[:, 0:width], in0=src_ps, scalar=SCALE, in1=bias_ap,
                                                                           op0=ALU.mult, op1=ALU.add),
                                  reads=[tsrc] + list(bias_tiles), writes=[tsbf])
                            fw.op("act", lambda e: e.activation(out=PT[j][:, 0:width], in_=sbf[:, 0:width], func=AF.Exp),
                                  reads=[tsbf], writes=[tPT[j]])
                        return PT[j], tPT[j]

                    def next_sps():
                        i = 3 + cnt["sps"] % 3
                        cnt["sps"] += 1
                        return PS[i], tPS[i]

                    def run_pipeline(steps, depth_):
                        G = 2
                        n = len(steps)
                        outs = [None] * n
                        ngr = (n + G - 1) // G
                        for g in range(ngr + 1):
                            if g < ngr:
                                for i in range(g * G, min(n, (g + 1) * G)):
                                    outs[i] = steps[i][0]()
                            if g >= 1:
                                for i in range((g - 1) * G, min(n, g * G)):
                                    steps[i][1](outs[i])

                    QRANGE = {0: (0, 128), 1: (0, 256), 6: (320, 512), 7: (448, 512)}

                    def sample_attention(m):
                        pb = m % 2
                        KW, VW, CK, CVB, QS = KWs[pb], VWs[pb], CKs[pb], CVBs[pb], QSs[pb]
                        tKW, tVW, tCK, tCVB, tQS = tKWs[pb], tVWs[pb], tCKs[pb], tCVBs[pb], tQSs[pb]
                        steps = []
                        nchunks = 12
                        for hh in range(2):
                            h = 2 * m + hh
                            hs = slice(hh * 64, (hh + 1) * 64)
                            for c in range(nchunks):
                                def s1(hh=hh, h=h, hs=hs, c=c):
                                    if c == 0:
                                        j2 = cnt["t2"] % 2
                                        cnt["t2"] += 1
                                        src = rpbG[l, :, h, :, :]
                                        fw.dma("act", T2[j2][0:64, 3:18, :], src, writes=[tT2[j2]])
                                        fw.dma("act", T2[j2][64:128, 4:19, :], src, writes=[tT2[j2]])
                                        cnt["t2cur"] = j2
                                    j2 = cnt["t2cur"]
                                    t2flat = T2[j2][:].rearrange("p e w -> p (e w)")
                                    sps, tsps = next_sps()
                                    if c < 8:
                                        q0, q1 = QRANGE.get(c, (0, 512))
                                        fw.mm([
                                            lambda e: e.matmul(sps[:, q0:q1], lhsT=KW[hs, c * 128:(c + 1) * 128], rhs=QS[hs, q0:q1], start=True, stop=False),
                                            lambda e: e.matmul(sps[:, q0:q1], lhsT=SOH[hs, c * 128:(c + 1) * 128], rhs=RWB[hs, q0:q1], start=False, stop=True),
                                        ], reads=[tKW, tQS, tC], writes=[tsps])
                                        e0 = (14 - 2 * c) * 64
                                        pt, tpt = exp_tile(sps[:, q0:q1], tsps, q1 - q0, bias_ap=t2flat[:, e0 + q0:e0 + q1], bias_tiles=[tT2[j2]])
                                        return pt, tpt, q0, q1
                                    cc = c - 8
                                    fw.mm([lambda e: e.matmul(sps[:], lhsT=CK[hs, cc * 128:(cc + 1) * 128], rhs=QS[hs, :], start=True, stop=True)],
                                          reads=[tCK, tQS], writes=[tsps])
                                    pt, tpt = exp_tile(sps[:], tsps, 512)
                                    return pt, tpt, 0, 512

                                def s2(o, hh=hh, hs=hs, c=c):
                                    pt, tpt, q0, q1 = o
                                    if c < 8:
                                        vl, vt = VW[:, c, hs], tVW
                                    else:
                                        vl, vt = CVB[:, c - 8, hs], tCVB
                                    first, last = (c == 0), (c == nchunks - 1)
                                    fw.mm([
                                        lambda e: e.matmul(PS[6][hs, q0:q1], lhsT=vl, rhs=pt[:, 0:q1 - q0], start=first, stop=last),
                                        lambda e: e.matmul(PS[7][hs, q0:q1], lhsT=ONE1[:, :], rhs=pt[:, 0:q1 - q0], start=first, stop=last),
                                    ], reads=[vt, tpt, tC], writes=[tPS[6], tPS[7]])
                                    if hh == 1 and last:
                                        fw.op("act", lambda e: e.activation(out=RC[:], in_=PS[7][:], func=AF.Ln), reads=[tPS[7]], writes=[tRC])
                                        fw.op("act", lambda e: e.activation(out=RC[:], in_=RC[:], func=AF.Exp, scale=-1.0), reads=[tRC], writes=[tRC])
                                        fw.op("dve", lambda e: e.tensor_tensor(out=YB[:, 1, m, 1024:1536], in0=PS[6][:], in1=RC[:], op=ALU.mult),
                                              reads=[tPS[6], tRC], writes=[tYB[1][m][2]])
                                steps.append((s1, s2))
                        run_pipeline(steps, 2)

                    def prompt_attention(m, mid_hook=None):
                        steps = []
                        for tp in range(2):
                            for hh in range(2):
                                hs = slice(hh * 64, (hh + 1) * 64)
                                for s2_ in range(2):
                                    s = tp * 2 + s2_
                                    q0 = s * 256

                                    def s1(tp=tp, hs=hs, q0=q0):
                                        sps, tsps = next_sps()
                                        fw.mm([lambda e, kc=kc: e.matmul(sps[:, kc * 256:(kc + 1) * 256], lhsT=KT[hs, q0 + kc * 128:q0 + (kc + 1) * 128],
                                                                         rhs=QT[hs, q0:q0 + 256], start=True, stop=True) for kc in range(2)],
                                              reads=[tKT[tp], tQT[tp]], writes=[tsps])
                                        return exp_tile(sps[:], tsps, 512)

                                    def s2(o, tp=tp, hh=hh, hs=hs, s2_=s2_, s=s):
                                        pt, tpt = o
                                        oc = slice(s2_ * 256, (s2_ + 1) * 256)
                                        fns = []
                                        for kc in range(2):
                                            fns.append(lambda e, kc=kc: e.matmul(PS[6][hs, oc], lhsT=VT[:, s * 2 + kc, hs], rhs=pt[:, kc * 256:(kc + 1) * 256],
                                                                                 start=(kc == 0), stop=(kc == 1)))
                                            fns.append(lambda e, kc=kc: e.matmul(PS[7][hs, oc], lhsT=ONE1[:, :], rhs=pt[:, kc * 256:(kc + 1) * 256],
                                                                                 start=(kc == 0), stop=(kc == 1)))
                                        fw.mm(fns, reads=[tVT[tp], tpt, tC], writes=[tPS[6], tPS[7]])
                                        if hh == 1 and s2_ == 1:
                                            fw.op("act", lambda e: e.activation(out=RC[:], in_=PS[7][:], func=AF.Ln), reads=[tPS[7]], writes=[tRC])
                                            fw.op("act", lambda e: e.activation(out=RC[:], in_=RC[:], func=AF.Exp, scale=-1.0), reads=[tRC], writes=[tRC])
                                            fw.op("dve", lambda e: e.tensor_tensor(out=YB[:, 1, m, tsl(tp)], in0=PS[6][:], in1=RC[:], op=ALU.mult),
                                                  reads=[tPS[6], tRC], writes=[tYB[1][m][tp]])
                                            if tp == 0 and mid_hook is not None:
                                                mid_hook()
                                    steps.append((s1, s2))
                        run_pipeline(steps, 2)

                    for m in range(4):
                        bqk, bv = P["qkv"][m]
                        wqk, twqk = take(bqk)
                        for t in range(NT):
                            ps, tps = next_dps()
                            dense(wqk, twqk, 0, 8, hrhs, hrt, t, ps, tps)
                            fw.op("act", lambda e, t=t, ps=ps: e.activation(out=QT[:, tsl(t)], in_=ps[:], func=AF.Identity),
                                  reads=[tps], writes=[tQT[t]])
                            chk(f"q{m}_{t}")
                            ps, tps = next_dps()
                            dense(wqk, twqk, 128, 8, hrhs, hrt, t, ps, tps)
                            if t < 2:
                                kf, tkf = next_fs()
                                fw.op("act", lambda e, kf=kf, ps=ps: e.activation(out=kf[:], in_=ps[:], func=AF.Identity),
                                      reads=[tps], writes=[tkf])
                                fw.op("dve", lambda e, t=t, kf=kf: e.tensor_copy(out=KT[:, tsl(t)], in_=kf[:]), reads=[tkf], writes=[tKT[t]])
                                chk(f"kc{m}_{t}")
                                fw.dma("act", kT_out[l, m * 128:(m + 1) * 128, tsl(t)], kf[:], reads=[tkf])
                                chk(f"kd{m}_{t}")
                            else:
                                fw.op("dve", lambda e, t=t, ps=ps: e.tensor_copy(out=KT[:, tsl(t)], in_=ps[:]), reads=[tps], writes=[tKT[t]])
                        chk(f"qk{m}")
                        wv, twv = take(bv)
                        for t in range(NT):
                            ps, tps = next_dps()
                            for q4 in range(4):
                                tt = t * 4 + q4
                                fw.mm([lambda e, k=k, tt=tt, q4=q4: e.matmul(ps[:, q4 * 128:(q4 + 1) * 128], lhsT=H[:, k, tt * 128:(tt + 1) * 128],
                                                                             rhs=wv[:, k, :], start=(k == 0), stop=(k == 7)) for k in range(8)],
                                      reads=[twv] + [tH[k][t] for k in range(8)], writes=[tps])
                            vf, tvf = next_fs()
                            fw.op("dve", lambda e, vf=vf, ps=ps: e.tensor_copy(out=vf[:], in_=ps[:]), reads=[tps], writes=[tvf])
                            fw.op("pool", lambda e, vf=vf, t=t: e.tensor_copy(out=VT[:, t * 4:(t + 1) * 4, :].rearrange("p a b -> p (a b)"), in_=vf[:]),
                                  reads=[tvf], writes=[tVT[t]])
                            if t < 2:
                                fw.dma("act", v_out[l, t * 512:(t + 1) * 512, m * 128:(m + 1) * 128].rearrange("(a p) n -> p a n", p=128),
                                       vf[:].rearrange("p (a n) -> p a n", a=4), reads=[tvf])
                        chk(f"v{m}")
                        bb = m % 2
                        fw.dma("pool", ckv_in[bb][0], KT[:, 1024:1536], reads=[tKT[2]], writes=[tKVI[bb]])
                        fw.dma("pool", ckv_in[bb][1], VT[:, 8:12, :].rearrange("p a b -> p (a b)"), reads=[tVT[2]], writes=[tKVI[bb]])
                        fw.collective("AllGather", GROUPS, ckv_in[bb].rearrange("a p n -> (a p) n"),
                                      ckv_out[bb].rearrange("r a p n -> (r a p) n"), reads=[tKVI[bb]], writes=[tKVO[bb]])
                        pb = m % 2
                        fw.op("pool", lambda e, pb=pb: e.tensor_copy(out=QSs[pb][:], in_=QT[:, 1024:1536]), reads=[tQT[2]], writes=[tQSs[pb]])
                        fw.op("pool", lambda e, pb=pb: e.tensor_copy(out=KWs[pb][:, 256:768], in_=KT[:, 1024:1536]), reads=[tKT[2]], writes=[tKWs[pb]])
                        fw.op("pool", lambda e, pb=pb: e.tensor_copy(out=VWs[pb][:, 2:6, :], in_=VT[:, 8:12, :]), reads=[tVT[2]], writes=[tVWs[pb]])
                        ckf, tckf = next_fs()
                        fw.dma("pool", ckf[:], ckT[l, m * 128:(m + 1) * 128, :], writes=[tckf])
                        fw.op("pool", lambda e, ckf=ckf, pb=pb: e.tensor_copy(out=CKs[pb][:], in_=ckf[:]), reads=[tckf], writes=[tCKs[pb]])
                        cvf, tcvf = next_fs()
                        fw.dma("pool", cvf[:].rearrange("p (a n) -> p a n", a=4), cv[l, :, m * 128:(m + 1) * 128].rearrange("(a p) n -> p a n", p=128), writes=[tcvf])
                        fw.op("pool", lambda e, cvf=cvf, pb=pb: e.tensor_copy(out=CVBs[pb][:].rearrange("p a n -> p (a n)"), in_=cvf[:]), reads=[tcvf], writes=[tCVBs[pb]])
                        chk(f"cc{m}")

                        def window(mm_):
                            pb_ = mm_ % 2
                            KW, VW = KWs[pb_], VWs[pb_]
                            for a in range(2):
                                fw.dma("sp", CAND[:], ckv_out[pb_][:, a].rearrange("r p n -> p r n"), reads=[tKVO[pb_]], writes=[tCAND])
                                if a == 0:
                                    dsts = ((KW[:, 0:256], 256, 0), (KW[:, 768:1024], 0, 4))
                                    tw_ = tKWs[pb_]
                                else:
                                    dsts = ((VW[:, 0:2, :].rearrange("p a b -> p (a b)"), 256, 0), (VW[:, 6:8, :].rearrange("p a b -> p (a b)"), 0, 4))
                                    tw_ = tVWs[pb_]
                                for (dst, c0, s0) in dsts:
                                    fw.op("dve", lambda e, dst=dst, c0=c0, s0=s0: e.tensor_scalar(
                                        out=dst, in0=CAND[:, 0, c0:c0 + 256], scalar1=SEL[:, s0:s0 + 1], scalar2=None, op0=ALU.mult),
                                        reads=[tCAND, tC], writes=[tw_])
                                    for r in range(1, 4):
                                        fw.op("dve", lambda e, dst=dst, c0=c0, s0=s0, r=r: e.scalar_tensor_tensor(
                                            out=dst, in0=CAND[:, r, c0:c0 + 256], scalar=SEL[:, s0 + r:s0 + r + 1], in1=dst,
                                            op0=ALU.mult, op1=ALU.add), reads=[tCAND, tC], writes=[tw_])

                        prompt_attention(m, mid_hook=(lambda: window(m - 1)) if m >= 1 else None)
                        chk(f"pattn{m}")
                        if m >= 1:
                            sample_attention(m - 1)
                        if m == 3:
                            window(3)
                            sample_attention(3)
                        chk(f"sattn{m}")
                    dps["n"] = 4
                    fw.release(arena_tiles)

                if debug and l == 0:
                    dbg_list.append(("H", H[:], [tH[k][t] for k in range(8) for t in range(NT)]))
                    dbg_list.append(("YB", YB[:], [tYB[n][c][t] for n in range(3) for c in range(4) for t in range(NT)]))
                    dbg_list.append(("MODV", MODV[0][:], [tMODV[0]]))
                chk("attn")

                with Arena() as a3:
                    LP_ = 1616
                    UPB = [sb(f"UP{j}", [128, LP_], F32, a3) for j in range(2)]
                    PB = sb("PB", [128, LP_], F32, a3)
                    PC = sb("PC", [128, LP_], F32, a3)
                    DG = [sb(f"DG{j}", [128, TT], BF16, a3) for j in range(2)]
                    ETMP = sb("ETMP", [128, 64], F32, a3)
                    tUP = [T("UP0"), T("UP1")]
                    tPB, tPC, tETMP = T("PB"), T("PC"), T("ETMP")
                    tDG = [T("DG0"), T("DG1")]
                    wpw0, twpw0 = take(P["poolw"])
                    POOLW = sb("POOLW", [128, 1, 512], BF16, a3)
                    twpw = T("POOLW")
                    fw.op("dve", lambda e: e.tensor_copy(out=POOLW[:], in_=wpw0), reads=[twpw0], writes=[twpw])
                    wpw = POOLW
                    for gp in range(2):
                        w, tw = take(P["pool"][gp])
                        for gg in range(2):
                            g = gp * 2 + gg
                            jb = g % 2
                            UP, tup = UPB[jb], tUP[jb]
                            fw.op("pool", lambda e, UP=UP: e.memset(UP[:], 0.0), writes=[tup])
                            for t in range(NT):
                                ps, tps = next_dps()
                                dense(w, tw, gg * 128, 8, hrhs, hrt, t, ps, tps)
                                if t < 2:
                                    dstv = UP[:, t * 544:(t + 1) * 544].rearrange("p (s n) -> p s n", s=2)[:, :, 8:264]
                                    srcv = ps[:].rearrange("p (s n) -> p s n", s=2)
                                else:
                                    dstv = UP[:, 1096:1608]
                                    srcv = ps[:]
                                fw.op("act", lambda e, dstv=dstv, srcv=srcv: e.activation(out=dstv, in_=srcv, func=AF.Identity),
                                      reads=[tps], writes=[tup])
                            ps, tps = next_dps()
                            fw.mm([lambda e, k=k: e.matmul(ps[:, 0:16], lhsT=w[:, k, gg * 128:(gg + 1) * 128], rhs=HH[:, k, :],
                                                           start=(k == 0), stop=(k == 7)) for k in range(8)],
                                  reads=[tw, tHH], writes=[tps])
                            fw.op("dve", lambda e, UP=UP, ps=ps: e.tensor_scalar(out=UP[:, 1088:1096], in0=ps[:, 0:8], scalar1=FLG[:, 0:1],
                                                                               scalar2=None, op0=ALU.mult), reads=[tps, tC], writes=[tup])
                            fw.op("dve", lambda e, UP=UP, ps=ps: e.tensor_scalar(out=UP[:, 1608:1616], in0=ps[:, 8:16], scalar1=FLG[:, 1:2],
                                                                               scalar2=None, op0=ALU.mult), reads=[tps, tC], writes=[tup])
                            L_ = LP_
                            fw.op("dve", lambda e, UP=UP: e.tensor_tensor(out=PB[:, 1:L_], in0=UP[:, 0:L_ - 1], in1=UP[:, 1:L_], op=ALU.add),
                                  reads=[tup], writes=[tPB])
                            S, tS = PB, tPB
                            if g >= 1:
                                fw.op("dve", lambda e: e.tensor_tensor(out=PC[:, 2:L_ - 1], in0=PB[:, 1:L_ - 2], in1=PB[:, 3:L_], op=ALU.add),
                                      reads=[tPB], writes=[tPC])
                                S, tS = PC, tPC
                            if g >= 2:
                                fw.op("dve", lambda e: e.tensor_tensor(out=PB[:, 4:L_ - 3], in0=PC[:, 2:L_ - 5], in1=PC[:, 6:L_ - 1], op=ALU.add),
                                      reads=[tPC], writes=[tPB])
                                S, tS = PB, tPB
                            if g >= 3:
                                fw.op("dve", lambda e: e.tensor_tensor(out=PC[:, 8:L_ - 7], in0=PB[:, 4:L_ - 11], in1=PB[:, 12:L_ - 3], op=ALU.add),
                                      reads=[tPB], writes=[tPC])
                                S, tS = PC, tPC
                            dg, tdg = DG[jb], tDG[jb]
                            invw = 1.0 / (2 ** (g + 1))

                            def pv_(buf):
                                return buf[:, 0:1088].rearrange("p (s n) -> p s n", s=4)
                            fw.op("dve", lambda e, S=S, UP=UP, dg=dg: e.scalar_tensor_tensor(
                                out=dg[:, 0:1024].rearrange("p (s n) -> p s n", s=4), in0=pv_(S)[:, :, 8:264], scalar=invw,
                                in1=pv_(UP)[:, :, 8:264], op0=ALU.mult, op1=ALU.subtract), reads=[tS, tup], writes=[tdg])
                            fw.op("dve", lambda e, S=S, UP=UP, dg=dg: e.scalar_tensor_tensor(
                                out=dg[:, 1024:1536], in0=S[:, 1096:1608], scalar=invw, in1=UP[:, 1096:1608],
                                op0=ALU.mult, op1=ALU.subtract), reads=[tS, tup], writes=[tdg])
                            for side in range(2):
                                o = 8 if side == 0 else 256
                                od = 0 if side == 0 else 248
                                ic = ICP[:, g, side, :]
                                icb = bass.AP(ic.tensor, ic.offset, [list(ic.ap[0]), [0, 4], [1, 8]])
                                fw.op("dve", lambda e, S=S, o=o, icb=icb: e.tensor_tensor(
                                    out=ETMP[:, 0:32].rearrange("p (s n) -> p s n", s=4), in0=pv_(S)[:, :, o:o + 8], in1=icb, op=ALU.mult),
                                    reads=[tS, tC], writes=[tETMP])
                                fw.op("dve", lambda e, UP=UP, dg=dg, o=o, od=od: e.tensor_tensor(
                                    out=dg[:, 0:1024].rearrange("p (s n) -> p s n", s=4)[:, :, od:od + 8],
                                    in0=ETMP[:, 0:32].rearrange("p (s n) -> p s n", s=4), in1=pv_(UP)[:, :, o:o + 8], op=ALU.subtract),
                                    reads=[tETMP, tup], writes=[tdg])
                                o2 = 1096 if side == 0 else 1600
                                od2 = 1024 if side == 0 else 1528
                                fw.op("dve", lambda e, S=S, o2=o2, side=side: e.tensor_tensor(
                                    out=ETMP[:, 32:40], in0=S[:, o2:o2 + 8], in1=ICS[:, g, side, :], op=ALU.mult),
                                    reads=[tS, tC], writes=[tETMP])
                                fw.op("dve", lambda e, UP=UP, dg=dg, o2=o2, od2=od2: e.tensor_tensor(
                                    out=dg[:, od2:od2 + 8], in0=ETMP[:, 32:40], in1=UP[:, o2:o2 + 8], op=ALU.subtract),
                                    reads=[tETMP, tup], writes=[tdg])
                            for t in range(NT):
                                ps, tps = next_dps()
                                fw.mm([lambda e, dg=dg, t=t, ps=ps: e.matmul(ps[:], lhsT=wpw[:, 0, g * 128:(g + 1) * 128], rhs=dg[:, tsl(t)],
                                                                              start=True, stop=True)], reads=[twpw, tdg], writes=[tps])
                                fw.op("act", lambda e, t=t, ps=ps: e.activation(out=YB[:, 0, g, tsl(t)], in_=ps[:], func=AF.Identity,
                                                                               scale=PV[:, l, 64 + g:65 + g]),
                                      reads=[tps, tC], writes=[tYB[0][g][t]])
                    fw.release(tUP + [tPB, tPC, tETMP, twpw] + tDG)
                chk("pool")

                with Arena() as a4:
                    LU = 1556
                    UX = sb("UX", [128, LU], F32, a4)
                    GY = sb("GY", [128, TT], BF16, a4)
                    XC = sb("XC", [128, TT], F32, a4)
                    XCB = sb("XCB", [128, TT], BF16, a4)
                    HS = [sb(f"HS{e_}", [128, TT], F32, a4) for e_ in range(2)]
                    GT = [sb(f"GT{j}", [128, 512], F32, a4) for j in range(8)]
                    CAR = sb("CAR", [128, 4], F32, a4)
                    CARG = sb("CARG", [128, 4, 4], F32, a4)
                    HINS = sb("HINS", [128, 16], F32, a4)
                    SRG = sb("SRG", [128, 2], F32, a4)
                    tUX, tGY = T("UX"), [T(f"GY{t}") for t in range(NT)]
                    tXC, tXCB = [T(f"XC{t}") for t in range(NT)], [T(f"XCB{t}") for t in range(NT)]
                    tHS = [[T(f"HS{e_}_{t}") for t in range(NT)] for e_ in range(2)]
                    tGT = [T(f"GT{j}") for j in range(8)]
                    tCAR, tCARG, tHINS, tSRG = T("CAR"), T("CARG"), T("HINS"), T("SRG")
                    gtc = {"i": 0}

                    def next_gt():
                        i = 4 + gtc["i"] % 4
                        gtc["i"] += 1
                        return GT[i], tGT[i]

                    def rev(ap, n):
                        return bass.AP(ap.tensor, ap.offset + (n - 1), [list(ap.ap[0]), [-1, n]])

                    wbd0, twbd0 = take(P["bd"])
                    BDW = sb("BDW", [128, 1, 2048], BF16, a4)
                    twbd = T("BDW")
                    fw.op("dve", lambda e: e.tensor_copy(out=BDW[:], in_=wbd0), reads=[twbd0], writes=[twbd])
                    wbd = BDW
                    for c in range(4):
                        w, tw = take(P["lru"][c])
                        fw.op("pool", lambda e: e.memset(UX[:], 0.0), writes=[tUX])
                        for t in range(NT):
                            ps, tps = next_dps()
                            dense(w, tw, 0, 8, hrhs, hrt, t, ps, tps)
                            if t < 2:
                                dstv = UX[:, t * 520:(t + 1) * 520].rearrange("p (s n) -> p s n", s=2)[:, :, 2:258]
                                srcv = ps[:].rearrange("p (s n) -> p s n", s=2)
                            else:
                                dstv = UX[:, 1042:1554]
                                srcv = ps[:]
                            fw.op("act", lambda e, dstv=dstv, srcv=srcv: e.activation(out=dstv, in_=srcv, func=AF.Identity),
                                  reads=[tps], writes=[tUX])
                        ps, tps = next_dps()
                        fw.mm([lambda e, k=k: e.matmul(ps[:, 0:16], lhsT=w[:, k, 0:128], rhs=HH[:, k, :], start=(k == 0), stop=(k == 7))
                               for k in range(8)], reads=[tw, tHH], writes=[tps])
                        fw.op("dve", lambda e, ps=ps: e.tensor_scalar(out=UX[:, 1040:1042], in0=ps[:, 6:8], scalar1=FLG[:, 0:1], scalar2=None,
                                                                     op0=ALU.mult), reads=[tps, tC], writes=[tUX])
                        fw.op("dve", lambda e, ps=ps: e.tensor_scalar(out=UX[:, 1554:1555], in0=ps[:, 8:9], scalar1=FLG[:, 1:2], scalar2=None,
                                                                     op0=ALU.mult), reads=[tps, tC], writes=[tUX])
                        for t in range(NT):
                            def tap(j, t=t):
                                if t < 2:
                                    return UX[:, t * 520:(t + 1) * 520].rearrange("p (s n) -> p s n", s=2)[:, :, j:j + 256]
                                return UX[:, 1040 + j:1040 + j + 512]
                            xo = XC[:, tsl(t)].rearrange("p (s n) -> p s n", s=2) if t < 2 else XC[:, tsl(t)]
                            fw.op("act", lambda e, xo=xo, tap=tap: e.activation(out=xo, in_=tap(0), func=AF.Identity, scale=PV[:, l, 68 + c:69 + c],
                                                                               bias=PV[:, l, 84 + c:85 + c]),
                                  reads=[tUX, tC], writes=[tXC[t]])
                            for j in range(1, 4):
                                fw.op("dve", lambda e, xo=xo, tap=tap, j=j: e.scalar_tensor_tensor(
                                    out=xo, in0=tap(j), scalar=PV[:, l, 68 + j * 4 + c:69 + j * 4 + c], in1=xo, op0=ALU.mult, op1=ALU.add),
                                    reads=[tUX, tC, tXC[t]], writes=[tXC[t]])
                            fw.op("pool", lambda e, t=t: e.tensor_copy(out=XCB[:, tsl(t)], in_=XC[:, tsl(t)]), reads=[tXC[t]], writes=[tXCB[t]])
                        AT = {}

                        def gate_scan(e_, t):
                            cl = CL[:, e_ * 4 + c:e_ * 4 + c + 1]
                            cl2 = CL2[:, e_ * 4 + c:e_ * 4 + c + 1]
                            if True:
                                ma = (e_ * 2 + 0) * 4 + c
                                mx = (e_ * 2 + 1) * 4 + c
                                idx = e_ * 4 + c
                                ps, tps = next_dps()
                                fw.mm([lambda e, ps=ps, t=t, ma=ma: e.matmul(ps[:], lhsT=wbd[:, 0, ma * 128:(ma + 1) * 128], rhs=XCB[:, tsl(t)],
                                                                              start=True, stop=True)], reads=[twbd, tXCB[t]], writes=[tps])
                                ps2, tps2 = next_dps()
                                fw.mm([lambda e, ps2=ps2, t=t, mx=mx: e.matmul(ps2[:], lhsT=wbd[:, 0, mx * 128:(mx + 1) * 128], rhs=XCB[:, tsl(t)],
                                                                                start=True, stop=True)], reads=[twbd, tXCB[t]], writes=[tps2])
                                rg, trg = next_gt()
                                if t == 2:
                                    aa, taa = GT[e_ * 2], tGT[e_ * 2]
                                    bb_, tbb = GT[e_ * 2 + 1], tGT[e_ * 2 + 1]
                                    fw.op("act", lambda e, rg=rg, ps=ps: e.activation(out=rg[:], in_=ps[:], func=AF.Tanh, scale=0.5,
                                                                                    bias=HBV[:, idx:idx + 1], accum_out=SRG[:, e_:e_ + 1]),
                                          reads=[tps, tAB], writes=[trg, tSRG])
                                else:
                                    aa, taa = next_gt()
                                    bb_, tbb = next_gt()
                                    fw.op("act", lambda e, rg=rg, ps=ps: e.activation(out=rg[:], in_=ps[:], func=AF.Tanh, scale=0.5,
                                                                                    bias=HBV[:, idx:idx + 1]),
                                          reads=[tps, tAB], writes=[trg])
                                fw.op("act", lambda e, bb_=bb_, ps2=ps2: e.activation(out=bb_[:], in_=ps2[:], func=AF.Tanh, scale=0.5,
                                                                                     bias=HBV[:, 8 + idx:9 + idx]),
                                      reads=[tps2, tAB], writes=[tbb])
                                fw.op("act", lambda e, rg=rg, aa=aa: e.activation(out=aa[:], in_=rg[:], func=AF.Exp, scale=CLH[:, idx:idx + 1],
                                                                                bias=CLH[:, idx:idx + 1]),
                                      reads=[trg, tAB], writes=[taa])
                                fw.op("dve", lambda e, rg=rg, aa=aa: e.tensor_tensor(out=rg[:], in0=aa[:], in1=aa[:], op=ALU.mult),
                                      reads=[taa], writes=[trg])
                                fw.op("act", lambda e, rg=rg: e.activation(out=rg[:], in_=rg[:], func=AF.Sqrt, scale=-1.0, bias=1.0),
                                      reads=[trg], writes=[trg])
                                fw.op("dve", lambda e, bb_=bb_, t=t: e.scalar_tensor_tensor(out=bb_[:], in0=bb_[:], scalar=1.0, in1=XC[:, tsl(t)],
                                                                                            op0=ALU.add, op1=ALU.mult),
                                      reads=[tbb, tXC[t]], writes=[tbb])
                                fw.op("dve", lambda e, bb_=bb_, rg=rg: e.scalar_tensor_tensor(out=bb_[:], in0=bb_[:], scalar=0.5, in1=rg[:],
                                                                                              op0=ALU.mult, op1=ALU.mult),
                                      reads=[tbb, trg], writes=[tbb])
                                if t < 2:
                                    for s2 in range(2):
                                        sl = slice(s2 * 256, (s2 + 1) * 256)
                                        o_ = HS[e_][:, t * 512 + s2 * 256:t * 512 + (s2 + 1) * 256]
                                        if e_ == 0:
                                            fw.op("dve", lambda e, o_=o_, aa=aa, bb_=bb_, sl=sl: e.tensor_tensor_scan(
                                                out=o_, data0=aa[:, sl], data1=bb_[:, sl], initial=0.0, op0=ALU.mult, op1=ALU.add),
                                                reads=[taa, tbb], writes=[tHS[e_][t]])
                                        else:
                                            fw.op("dve", lambda e, o_=o_, aa=aa, bb_=bb_, sl=sl: e.tensor_tensor_scan(
                                                out=rev(o_, 256), data0=rev(aa[:, sl], 256), data1=rev(bb_[:, sl], 256), initial=0.0,
                                                op0=ALU.mult, op1=ALU.add), reads=[taa, tbb], writes=[tHS[e_][t]])
                                else:
                                    AT[e_] = (aa, taa, bb_, tbb)
                                    o_ = HS[e_][:, 1024:1536]
                                    if e_ == 0:
                                        fw.op("dve", lambda e, o_=o_, aa=aa, bb_=bb_: e.tensor_tensor_scan(
                                            out=o_, data0=aa[:], data1=bb_[:], initial=0.0, op0=ALU.mult, op1=ALU.add),
                                            reads=[taa, tbb], writes=[tHS[e_][t]])
                                        fw.op("dve", lambda e: e.tensor_copy(out=CAR[:, 1:2], in_=HS[0][:, 1535:1536]), reads=[tHS[0][2]], writes=[tCAR])
                                    else:
                                        fw.op("dve", lambda e, o_=o_, aa=aa, bb_=bb_: e.tensor_tensor_scan(
                                            out=rev(o_, 512), data0=rev(aa[:], 512), data1=rev(bb_[:], 512), initial=0.0,
                                            op0=ALU.mult, op1=ALU.add), reads=[taa, tbb], writes=[tHS[e_][t]])
                                        fw.op("dve", lambda e: e.tensor_copy(out=CAR[:, 3:4], in_=HS[1][:, 1024:1025]), reads=[tHS[1][2]], writes=[tCAR])
                                    fw.op("act", lambda e, e_=e_: e.activation(out=CAR[:, 2 * e_:2 * e_ + 1], in_=SRG[:, e_:e_ + 1], func=AF.Exp,
                                                                               scale=CLH[:, e_ * 4 + c:e_ * 4 + c + 1], bias=CL256[:, e_ * 4 + c:e_ * 4 + c + 1]),
                                          reads=[tSRG, tAB], writes=[tCAR])
                        gate_scan(0, 2)
                        gate_scan(1, 2)
                        for t in range(NT):
                            ps, tps = next_dps()
                            dense(w, tw, 128, 8, hrhs, hrt, t, ps, tps)
                            fw.op("act", lambda e, t=t, ps=ps: e.activation(out=GY[:, tsl(t)], in_=ps[:], func=AF.Gelu_apprx_tanh),
                                  reads=[tps], writes=[tGY[t]])
                        cb = c % 2
                        fw.dma("pool", cs_in[cb], CAR[:], reads=[tCAR], writes=[tCSI[cb]])
                        fw.collective("AllGather", GROUPS, cs_in[cb], cs_out[cb].rearrange("r p n -> (r p) n"), reads=[tCSI[cb]], writes=[tCSO[cb]])
                        for e_ in range(2):
                            for t in range(2):
                                gate_scan(e_, t)
                        for t in range(2):
                            tm, ttm = next_gt()
                            fw.op("pool", lambda e, tm=tm, t=t: e.tensor_tensor(out=tm[:], in0=HS[0][:, tsl(t)], in1=HS[1][:, tsl(t)], op=ALU.add),
                                  reads=[tHS[0][t], tHS[1][t]], writes=[ttm])
                            fw.op("dve", lambda e, tm=tm, t=t: e.tensor_tensor(out=YB[:, 2, c, tsl(t)], in0=tm[:], in1=GY[:, tsl(t)], op=ALU.mult),
                                  reads=[ttm, tGY[t]], writes=[tYB[2][c][t]])
                        stv = STO[:].rearrange("p (s x) -> p s x", s=4)
                        fw.op("act", lambda e: e.activation(out=stv[:, :, c], in_=HS[0][:, 0:1024].rearrange("p (s n) -> p s n", s=4)[:, :, 255],
                                                            func=AF.Identity), reads=[tHS[0][0], tHS[0][1]], writes=[tSTO])
                        fw.op("act", lambda e: e.activation(out=stv[:, :, 4 + c], in_=HS[1][:, 0:1024].rearrange("p (s n) -> p s n", s=4)[:, :, 0],
                                                            func=AF.Identity), reads=[tHS[1][0], tHS[1][1]], writes=[tSTO])
                        fw.dma("pool", CARG[:], cs_out[cb].rearrange("r p n -> p r n"), reads=[tCSO[cb]], writes=[tCARG])
                        fw.op("dve", lambda e: e.tensor_copy(out=HINS[:, 0:1], in_=ST0[:, l * 8 + c:l * 8 + c + 1]), reads=[tC], writes=[tHINS])
                        for r in range(3):
                            fw.op("dve", lambda e, r=r: e.scalar_tensor_tensor(out=HINS[:, r + 1:r + 2], in0=HINS[:, r:r + 1], scalar=CARG[:, r, 0:1],
                                                                               in1=CARG[:, r, 1:2], op0=ALU.mult, op1=ALU.add),
                                  reads=[tHINS, tCARG], writes=[tHINS])
                        fw.op("dve", lambda e: e.tensor_copy(out=HINS[:, 7:8], in_=ST0[:, l * 8 + 4 + c:l * 8 + 4 + c + 1]), reads=[tC], writes=[tHINS])
                        for r in (3, 2, 1):
                            fw.op("dve", lambda e, r=r: e.scalar_tensor_tensor(out=HINS[:, 4 + r - 1:4 + r], in0=HINS[:, 4 + r:4 + r + 1],
                                                                               scalar=CARG[:, r, 2:3], in1=CARG[:, r, 3:4], op0=ALU.mult, op1=ALU.add),
                                  reads=[tHINS, tCARG], writes=[tHINS])
                        for e_ in range(2):
                            fw.op("dve", lambda e, e_=e_: e.tensor_scalar(out=HINS[:, 8 + e_:9 + e_], in0=HINS[:, 4 * e_:4 * e_ + 1], scalar1=SEL[:, 8:9],
                                                                         scalar2=None, op0=ALU.mult), reads=[tHINS, tC], writes=[tHINS])
                            for r in range(1, 4):
                                fw.op("dve", lambda e, e_=e_, r=r: e.scalar_tensor_tensor(
                                    out=HINS[:, 8 + e_:9 + e_], in0=HINS[:, 4 * e_ + r:4 * e_ + r + 1], scalar=SEL[:, 8 + r:9 + r],
                                    in1=HINS[:, 8 + e_:9 + e_], op0=ALU.mult, op1=ALU.add), reads=[tHINS, tC], writes=[tHINS])
                        for e_ in range(2):
                            aa, taa, bb_, tbb = AT[e_]
                            o_ = HS[e_][:, 1024:1536]
                            if e_ == 0:
                                fw.op("dve", lambda e, o_=o_, aa=aa, bb_=bb_: e.tensor_tensor_scan(
                                    out=o_, data0=aa[:], data1=bb_[:], initial=HINS[:, 8:9], op0=ALU.mult, op1=ALU.add),
                                    reads=[taa, tbb, tHINS], writes=[tHS[0][2]])
                            else:
                                fw.op("dve", lambda e, o_=o_, aa=aa, bb_=bb_: e.tensor_tensor_scan(
                                    out=rev(o_, 512), data0=rev(aa[:], 512), data1=rev(bb_[:], 512), initial=HINS[:, 9:10],
                                    op0=ALU.mult, op1=ALU.add), reads=[taa, tbb, tHINS], writes=[tHS[1][2]])
                        tm, ttm = next_gt()
                        fw.op("pool", lambda e, tm=tm: e.tensor_tensor(out=tm[:], in0=HS[0][:, 1024:1536], in1=HS[1][:, 1024:1536], op=ALU.add),
                              reads=[tHS[0][2], tHS[1][2]], writes=[ttm])
                        fw.op("dve", lambda e, tm=tm: e.tensor_tensor(out=YB[:, 2, c, 1024:1536], in0=tm[:], in1=GY[:, 1024:1536], op=ALU.mult),
                              reads=[ttm, tGY[2]], writes=[tYB[2][c][2]])
                        chk(f"lru{c}")
                    fw.dma("act", st_out[l], STO[:], reads=[tSTO])
                    fw.release([tUX] + tGY + tXC + tXCB + tHS[0] + tHS[1] + tGT + [tCAR, tCARG, tHINS, tSRG, twbd])
                if debug and l == 0:
                    dbg_list.append(("YB2", YB[:], [tYB[n][c][t] for n in range(3) for c in range(4) for t in range(NT)]))
                chk("lru")

                with Arena() as a5:
                    MG = sb("MG", [128, 8, TT], BF16, a5)
                    MF = sb("MF", [128, 2, TT], F32, a5)
                    GG = [sb(f"GG{j}", [128, 512], F32, a5) for j in range(2)]
                    GM = [sb(f"GM{j}", [128, 512], F32, a5) for j in range(2)]
                    tMG = [[T(f"MG{k}_{t}") for t in range(NT)] for k in range(8)]
                    tMF = [[T(f"MF{k}_{t}") for t in range(NT)] for k in range(2)]
                    tGG, tGM = [T("GG0"), T("GG1")], [T("GM0"), T("GM1")]
                    gi = 0
                    bi_ = 0
                    for jp in range(4):
                        for n in range(3):
                            bg, bb2 = P["gate"][bi_]
                            bi_ += 1
                            wg, twg = take(bg)
                            wb, twb = take(bb2)
                            for jj in range(2):
                                j = jp * 2 + jj
                                for t in range(NT):
                                    psg, tpsg = next_dps()
                                    dense(wg, twg, jj * 128, 8, hrhs, hrt, t, psg, tpsg)
                                    gq = gi % 2
                                    gi += 1
                                    fw.op("act", lambda e, gq=gq, psg=psg: e.activation(out=GG[gq][:], in_=psg[:], func=AF.Sigmoid),
                                          reads=[tpsg], writes=[tGG[gq]])
                                    psp, tpsp = next_dps()
                                    dense(wb, twb, jj * 128, 4, lambda k, t, n=n: YB[:, n, k, tsl(t)], lambda k, t, n=n: tYB[n][k][t], t, psp, tpsp)
                                    if n == 0:
                                        fw.op("dve", lambda e, gq=gq, psp=psp, jj=jj, t=t: e.tensor_tensor(out=MF[:, jj, tsl(t)], in0=GG[gq][:], in1=psp[:], op=ALU.mult),
                                              reads=[tGG[gq], tpsp], writes=[tMF[jj][t]])
                                    else:
                                        fw.op("dve", lambda e, gq=gq, psp=psp: e.tensor_tensor(out=GM[gq][:], in0=GG[gq][:], in1=psp[:], op=ALU.mult),
                                              reads=[tGG[gq], tpsp], writes=[tGM[gq]])
                                        if n == 1:
                                            fw.op("pool", lambda e, gq=gq, jj=jj, t=t: e.tensor_tensor(out=MF[:, jj, tsl(t)], in0=MF[:, jj, tsl(t)], in1=GM[gq][:], op=ALU.add),
                                                  reads=[tGM[gq], tMF[jj][t]], writes=[tMF[jj][t]])
                                        else:
                                            fw.op("pool", lambda e, gq=gq, jj=jj, t=t, j=j: e.tensor_tensor(out=MG[:, j, tsl(t)], in0=MF[:, jj, tsl(t)], in1=GM[gq][:], op=ALU.add),
                                                  reads=[tGM[gq], tMF[jj][t]], writes=[tMG[j][t]])
                    chk("merge")
                    if debug and l == 0:
                        pass
                    for mp in range(4):
                        w, tw = take(P["out"][mp])
                        for jj in range(2):
                            j = mp * 2 + jj
                            for t in range(NT):
                                g = grp_of_tile[t]
                                ps, tps = next_dps()
                                dense(w, tw, jj * 128, 8, lambda k, t: MG[:, k, tsl(t)], lambda k, t: tMG[k][t], t, ps, tps)
                                fw.op("dve", lambda e, ps=ps, j=j, t=t, g=g: e.scalar_tensor_tensor(
                                    out=X[:, j, tsl(t)], in0=ps[:], scalar=mv[:, 16 + j, g:g + 1], in1=X[:, j, tsl(t)], op0=ALU.mult, op1=ALU.add),
                                    reads=[tps, tmv, tX[j][t]], writes=[tX[j][t]])
                    fw.release([x for r_ in tMG for x in r_] + [x for r_ in tMF for x in r_] + tGG + tGM)
                chk("wout")
                if debug and l == 0:
                    dbg_list.append(("X1", X[:], [tX[k][t] for k in range(8) for t in range(NT)]))
                    if stop_after == "wout":
                        raise _Stop()

                with Arena() as a1:
                    SQ = [sb(f"SQ{j}", [128, 512], BF16, a1) for j in range(4)]
                    RS = [sb(f"RS{j}", [128, 512], F32, a1) for j in range(3)]
                    TMP = [sb(f"TMP{j}", [128, 512], F32, a1) for j in range(4)]
                    tSQ = [T(f"SQ{j}") for j in range(4)]
                    tRS = [T(f"RS{j}") for j in range(3)]
                    tTMP = [T(f"TMP{j}") for j in range(4)]
                    emit_norm(l, 1, (SQ, tSQ, RS, tRS, TMP, tTMP))
                    fw.release(tSQ + tRS + tTMP)

                with Arena() as a6:
                    AV = sb("AV", [128, 12, TT], BF16, a6)
                    SG = [sb(f"SG{j}", [128, 512], F32, a6) for j in range(2)]
                    tAV = [[T(f"AV{k}_{t}") for t in range(NT)] for k in range(12)]
                    tSG = [T("SG0"), T("SG1")]
                    si = 0
                    for (kind, idx, bid) in P["ffseq"]:
                        if kind == "ffin":
                            hf_, i = idx
                            w, tw = take(bid)
                            for t in range(NT):
                                psg, tpsg = next_dps()
                                dense(w, tw, 0, 8, hrhs, hrt, t, psg, tpsg)
                                sq_ = si % 2
                                si += 1
                                fw.op("act", lambda e, sq_=sq_, psg=psg: e.activation(out=SG[sq_][:], in_=psg[:], func=AF.Silu),
                                      reads=[tpsg], writes=[tSG[sq_]])
                                psu, tpsu = next_dps()
                                dense(w, tw, 128, 8, hrhs, hrt, t, psu, tpsu)
                                fw.op("dve", lambda e, sq_=sq_, psu=psu, i=i, t=t: e.tensor_tensor(out=AV[:, i, tsl(t)], in0=SG[sq_][:], in1=psu[:], op=ALU.mult),
                                      reads=[tSG[sq_], tpsu], writes=[tAV[i][t]])
                        elif kind == "mod":
                            assert bid == cons["i"]
                            cons["i"] += 1
                            emit_mod_block(l + 1, idx, bid)
                        else:
                            hf_, j, nk = idx
                            w, tw = take(bid)
                            for t in range(NT):
                                g = grp_of_tile[t]
                                ps, tps = next_dps()
                                fw.mm([lambda e, k=k, ps=ps, t=t: e.matmul(ps[:], lhsT=w[:, k, :], rhs=AV[:, k, tsl(t)], start=(k == 0), stop=(k == nk - 1))
                                       for k in range(nk)], reads=[tw] + [tAV[k][t] for k in range(nk)], writes=[tps])
                                fw.op("dve", lambda e, ps=ps, j=j, t=t, g=g: e.scalar_tensor_tensor(
                                    out=X[:, j, tsl(t)], in0=ps[:], scalar=mv[:, 40 + j, g:g + 1], in1=X[:, j, tsl(t)], op0=ALU.mult, op1=ALU.add),
                                    reads=[tps, tmv, tX[j][t]], writes=[tX[j][t]])
                    if l + 1 < depth:
                        finish_mod(l + 1)
                    fw.release([x for r_ in tAV for x in r_] + tSG)
                if debug and l == 0:
                    dbg_list.append(("X2", X[:], [tX[k][t] for k in range(8) for t in range(NT)]))
                chk(f"layer{l}")

                if l + 1 < depth:
                    with Arena() as a7:
                        XHO = sb("XHO", [128, 8, 16], F32, a7)
                        tXHO = T("XHO")
                        xrd = [tX[k][2] for k in range(8)]
                        fw.op("pool", lambda e: e.tensor_copy(out=XHO[:, :, 0:8], in_=X[:, :, 1024:1032]), reads=xrd, writes=[tXHO])
                        fw.op("pool", lambda e: e.tensor_copy(out=XHO[:, :, 8:16], in_=X[:, :, 1528:1536]), reads=xrd, writes=[tXHO])
                        fw.dma("pool", cx_in, XHO[:].rearrange("p k t -> p (k t)"), reads=[tXHO], writes=[tCXI])
                        fw.collective("AllGather", GROUPS, cx_in, cx_out.rearrange("r p n -> (r p) n"), reads=[tCXI], writes=[tCXO])
                        fw.release([tXHO])

            with Arena() as a8:
                SQ = [sb(f"SQ{j}", [128, 512], BF16, a8) for j in range(2)]
                RS = sb("RS", [128, 512], F32, a8)
                YO = [sb(f"YO{j}", [128, 512], F32, a8) for j in range(4)]
                tSQ, tRS, tYO = [T("SQ0"), T("SQ1")], T("RS"), [T(f"YO{j}") for j in range(4)]
                yv = yT.rearrange("(k p) t -> p k t", p=128)
                yi = 0
                for t in range(NT):
                    ps, tps = next_dps()
                    for k in range(8):
                        j = k % 2
                        fw.op("act", lambda e, k=k, j=j, t=t: e.activation(out=SQ[j][:], in_=X[:, k, tsl(t)], func=AF.Square),
                              reads=[tX[k][t]], writes=[tSQ[j]])
                        fw.mm([lambda e, k=k, j=j, ps=ps: e.matmul(ps[:], lhsT=ONES[:], rhs=SQ[j][:], start=(k == 0), stop=(k == 7))],
                              reads=[tSQ[j], tC], writes=[tps])
                    fw.op("act", lambda e, ps=ps: e.activation(out=RS[:], in_=ps[:], func=AF.Ln, bias=EPSC[:, 0:1]), reads=[tps, tC], writes=[tRS])
                    fw.op("act", lambda e: e.activation(out=RS[:], in_=RS[:], func=AF.Exp, scale=-0.5), reads=[tRS], writes=[tRS])
                    for k in range(8):
                        j = yi % 4
                        yi += 1
                        fw.op("dve", lambda e, k=k, j=j, t=t: e.scalar_tensor_tensor(out=YO[j][:], in0=X[:, k, tsl(t)], scalar=GF[:, k:k + 1], in1=RS[:],
                                                                                     op0=ALU.mult, op1=ALU.mult),
                              reads=[tX[k][t], tRS, tC], writes=[tYO[j]])
                        fw.dma("act", yv[:, k, tsl(t)], YO[j][:], reads=[tYO[j]])
                fw.release(tSQ + [tRS] + tYO)

        except _Stop:
            pass
        for (nm, ap_sb, tl) in dbg_list:
            dd = nc.dram_tensor("dbg_" + nm, list(ap_sb.shape), ap_sb.dtype, kind="ExternalOutput").ap()
            fw.dma("sp", dd, ap_sb, reads=tl)
        eng = fw.engs["sp"]
        for s_ in fw.sems:
            if s_.count > 0:
                eng.h.wait_ge(s_.h, s_.count)
    return nc


def _chunked(v):
    v = np.asarray(v, np.float32)
    n = v.shape[-1] // 128
    return np.ascontiguousarray(np.moveaxis(v.reshape(v.shape[:-1] + (n, 128)), -1, 0))


def prepare_inputs(inp, depth=DEPTH):
    f32 = np.float32
    LAYERED = ("w_mod", "b_mod", "g_norm1", "g_norm2", "w_in", "pool_w", "pool_scale", "na_rpb", "lru_conv_w", "lru_conv_b",
               "lru_wa", "lru_ba", "lru_wx", "lru_bx", "lru_lambda", "w_branch", "w_out", "w_ff_in", "w_ff_out")

    def g(k):
        a = np.asarray(inp[k], f32)
        if k in LAYERED:
            a = a[:depth]
        elif k in ("cache_k", "cache_v", "state_lru"):
            a = a[:, :depth]
        return a
    DEPTH = depth
    x_prompt, x_sample = g("x_prompt"), g("x_sample")
    cache_k, cache_v, state_lru = g("cache_k"), g("cache_v"), g("state_lru")
    c, c_ctx = g("c"), g("c_ctx")
    pv = np.zeros((128, DEPTH, NV), f32)
    pv[:, :, 0:8] = _chunked(g("g_norm1"))
    pv[:, :, 8:16] = _chunked(g("g_norm2"))
    pv[:, :, 16:64] = _chunked(g("b_mod"))
    pv[:, :, 64:68] = _chunked(g("pool_scale"))
    cw = _chunked(g("lru_conv_w"))
    pv[:, :, 68:84] = cw.reshape(128, DEPTH, 16)
    pv[:, :, 84:88] = _chunked(g("lru_conv_b"))
    pv[:, :, 88:96] = _chunked(g("lru_ba")).reshape(128, DEPTH, 8)
    pv[:, :, 96:104] = _chunked(g("lru_bx")).reshape(128, DEPTH, 8)
    pv[:, :, 104:112] = _chunked(g("lru_lambda")).reshape(128, DEPTH, 8)
    gf = _chunked(g("g_final"))
    poolw = np.ascontiguousarray(g("pool_w").transpose(0, 2, 1, 3).reshape(DEPTH, 128, 512))
    wa, wx = g("lru_wa"), g("lru_wx")
    bd = np.zeros((DEPTH, 128, 16, 128), f32)
    for e in range(2):
        for which, w in enumerate((wa, wx)):
            for ch in range(4):
                mat = (e * 2 + which) * 4 + ch
                for hf in range(2):
                    bd[:, hf * 64:(hf + 1) * 64, mat, hf * 64:(hf + 1) * 64] = w[:, e, ch * 2 + hf]
    bd = bd.reshape(DEPTH, 128, 2048)
    rpb = g("na_rpb")
    wk = np.arange(64)[:, None]
    wq = np.arange(64)[None, :]
    dc = np.clip(wk - wq, -15, 15) + 15
    col0 = np.clip(np.arange(64) - 8, 0, 48)
    col_ok = (wk >= col0[None, :]) & (wk < col0[None, :] + 16)
    G = np.empty((DEPTH, 64, 8, 15, 64), f32)
    for ep in range(15):
        gath = rpb[:, :, 14 - ep][:, :, dc]
        gath = np.where(col_ok[None, None], gath, f32(NEG))
        G[:, :, :, ep, :] = gath.transpose(0, 2, 1, 3)
    slot16 = (np.arange(1024)[None, :] // 64 == np.arange(16)[:, None]).astype(f32)
    slot = np.zeros((128, 1024), f32)
    slot[0:16] = slot16
    slot[64:80] = slot16
    icp = np.zeros((128, 4, 2, 8), f32)
    for gi in range(4):
        half = 2 ** gi
        for i in range(8):
            t = i
            icp[:, gi, 0, i] = 1.0 / (min(t + half, 256) - max(t - half, 0))
            t = 248 + i
            icp[:, gi, 1, i] = 1.0 / (min(t + half, 256) - max(t - half, 0))
    shared = dict(pvec=pv, gfin=gf, w_mod=g("w_mod"), w_in=g("w_in"), w_branch=g("w_branch"), w_out=g("w_out"),
                  w_ff_in=g("w_ff_in"), w_ff_out=g("w_ff_out"), pool_w=poolw, lru_bd=bd, rpbG=G, slotoh=slot, icp=icp)
    in_maps = []
    for core in range(NCORES):
        b, j = core // 4, core % 4
        xp = x_prompt[4 * core:4 * core + 4].reshape(TP, D)
        xs = x_sample[b, j * TS:(j + 1) * TS]
        xT = np.ascontiguousarray(np.concatenate([xp, xs], 0).T)
        halo = np.zeros((16, D), f32)
        if j > 0:
            halo[0:8] = x_sample[b, j * TS - 8:j * TS]
        if j < 3:
            halo[8:16] = x_sample[b, (j + 1) * TS:(j + 1) * TS + 8]
        xh0 = np.ascontiguousarray(halo.T.reshape(8, 128, 16).transpose(1, 0, 2))
        ckT = np.ascontiguousarray(cache_k[b].reshape(DEPTH, 512, 512).transpose(0, 2, 1))
        cvv = np.ascontiguousarray(cache_v[b].reshape(DEPTH, 512, 512))
        st0 = _chunked(state_lru[b]).reshape(128, DEPTH * 8)
        cvec = np.stack([_chunked(c_ctx), _chunked(c[b])], -1)
        R0 = 8 * j
        rowb16 = np.zeros((16, 512), f32)
        for i in range(8):
            r = R0 + i
            row0 = min(max(r - 4, 0), 24)
            for m in range(16):
                rr = R0 - 4 + m
                ok = (row0 <= rr < row0 + 8) and (0 <= rr < 32)
                if not ok:
                    rowb16[m, i * 64:(i + 1) * 64] = -240000.0
        rowb = np.zeros((128, 512), f32)
        rowb[0:16] = rowb16
        rowb[64:80] = rowb16
        sel = np.zeros((128, 12), f32)
        if j > 0:
            sel[:, j - 1] = 1
        if j < 3:
            sel[:, 4 + j + 1] = 1
        sel[:, 8 + j] = 1
        flags = np.zeros((128, 2), f32)
        flags[:, 0] = float(j > 0)
        flags[:, 1] = float(j < 3)
        ics = np.zeros((128, 4, 2, 8), f32)
        for gi in range(4):
            half = 2 ** gi
            for i in range(8):
                t = j * TS + i
                ics[:, gi, 0, i] = 1.0 / (min(t + half, 2048) - max(t - half, 0))
                t = j * TS + 504 + i
                ics[:, gi, 1, i] = 1.0 / (min(t + half, 2048) - max(t - half, 0))
        m = dict(shared)
        m.update(xT=xT, xh0=xh0, ckT=ckT, cv=cvv, st0=np.ascontiguousarray(st0), cvec=np.ascontiguousarray(cvec),
                 rowb=rowb, sel=sel, flags=flags, ics=ics)
        in_maps.append(m)
    return in_maps


def assemble_outputs(results):
    f32 = np.float32
    y_prompt = np.empty((32, 256, D), f32)
    y_sample = np.empty((2, 2048, D), f32)
    nk = np.empty((32, DEPTH, 256, 8, 64), f32)
    nv = np.empty((32, DEPTH, 256, 8, 64), f32)
    ns = np.empty((32, DEPTH, 2, 512), f32)
    for core in range(NCORES):
        r = results[core]
        b, j = core // 4, core % 4
        yT = np.asarray(r["yT"])
        for s in range(4):
            y_prompt[4 * core + s] = yT[:, s * 256:(s + 1) * 256].T
        y_sample[b, j * TS:(j + 1) * TS] = yT[:, TP:TT].T
        kT = np.asarray(r["kT_out"])
        vv = np.asarray(r["v_out"])
        so = np.asarray(r["st_out"])
        for s in range(4):
            nk[4 * core + s] = kT[:, :, s * 256:(s + 1) * 256].transpose(0, 2, 1).reshape(DEPTH, 256, 8, 64)
            nv[4 * core + s] = vv[:, s * 256:(s + 1) * 256, :].reshape(DEPTH, 256, 8, 64)
            blk = so[:, :, s * 8:(s + 1) * 8].reshape(DEPTH, 128, 2, 4)
            ns[4 * core + s] = blk.transpose(0, 2, 3, 1).reshape(DEPTH, 2, 512)
    return y_prompt, y_sample, nk, nv, ns


_NC_CACHE = {}


def kernel(**inputs):
    if "nc" not in _NC_CACHE:
        _NC_CACHE["nc"] = build_program()
    nc = _NC_CACHE["nc"]
    in_maps = prepare_inputs(inputs)
    res = run_bass_kernel_spmd(nc, in_maps, core_ids=list(range(NCORES)))
    return assemble_outputs(res.results)
```

```python
import numpy as np
from contextlib import ExitStack
import concourse.bass as bass
import concourse.mybir as mybir
from concourse.bass_utils import run_bass_kernel_spmd

F32 = mybir.dt.float32
BF16 = mybir.dt.bfloat16
AF = mybir.ActivationFunctionType
ALU = mybir.AluOpType

D = 1024
DEPTH = 4
NCORES = 8
TP = 1024
TS = 512
TT = TP + TS
NT = 3
INW = 6144
FFH = 2816
EPS = 1e-6
SCALE = 0.125
NV = 112
NEG = -30000.0
GROUPS = [[0, 1, 2, 3], [4, 5, 6, 7]]


class _Stop(Exception):
    pass


class Arena:
    def __enter__(self):
        self.es = ExitStack()
        return self.es

    def __exit__(self, et, ev, tb):
        if et is None or et is _Stop:
            self.es.close()
        return False


class Sem:
    def __init__(self, handle, name):
        self.h = handle
        self.name = name
        self.count = 0


class T:
    __slots__ = ("name", "w", "rs", "dsem_w", "dsem_r", "excl")

    def __init__(self, name, excl=False):
        self.name = name
        self.excl = excl
        self.w = None
        self.rs = []
        self.dsem_w = None
        self.dsem_r = None


class Eng:
    def __init__(self, name, h, sem):
        self.name = name
        self.h = h
        self.sem = sem
        self.waited = {}
        self.nwaits = 0
        self.nops = 0


class FW:
    def __init__(self, nc, stack, same_engine_sync=True):
        self.nc = nc
        self.stack = stack
        self.same_engine_sync = same_engine_sync
        self.engs = {}
        self.nsem = 0
        self.sems = []
        self.free_dma = []
        for name, h in (("pe", nc.tensor), ("act", nc.scalar), ("dve", nc.vector),
                        ("pool", nc.gpsimd), ("sp", nc.sync)):
            self.engs[name] = Eng(name, h, self.new_sem("e_" + name))

    def new_sem(self, name):
        self.nsem += 1
        sm = Sem(self.stack.enter_context(self.nc.semaphore(name)), name)
        self.sems.append(sm)
        return sm

    def dma_sem(self):
        if self.free_dma:
            return self.free_dma.pop()
        return self.new_sem(f"dq{self.nsem}")

    def _needs(self, reads, writes):
        needs = {}
        for t in reads:
            if t.w is not None:
                s, v = t.w
                if needs.get(s, 0) < v:
                    needs[s] = v
            if t.excl:
                for (s, v) in t.rs:
                    if needs.get(s, 0) < v:
                        needs[s] = v
        for t in writes:
            if t.w is not None:
                s, v = t.w
                if needs.get(s, 0) < v:
                    needs[s] = v
            for (s, v) in t.rs:
                if needs.get(s, 0) < v:
                    needs[s] = v
        return needs

    def _emit_waits(self, eng, needs):
        for s, v in needs.items():
            if s is eng.sem and not self.same_engine_sync:
                continue
            if eng.waited.get(s, 0) >= v:
                continue
            eng.h.wait_ge(s.h, v)
            eng.waited[s] = v
            eng.nwaits += 1

    def _record(self, sv, reads, writes):
        for t in reads:
            t.rs.append(sv)
            if len(t.rs) > 24:
                m = {}
                for (s, v) in t.rs:
                    if m.get(s, 0) < v:
                        m[s] = v
                t.rs = list(m.items())
        for t in writes:
            t.w = sv
            t.rs = []

    def op(self, ename, fn, reads=(), writes=()):
        eng = self.engs[ename]
        self._emit_waits(eng, self._needs(reads, writes))
        ins = fn(eng.h)
        eng.sem.count += 1
        ins.then_inc(eng.sem.h, 1)
        eng.nops += 1
        self._record((eng.sem, eng.sem.count), reads, writes)
        return ins

    def mm(self, fns, reads=(), writes=()):
        eng = self.engs["pe"]
        self._emit_waits(eng, self._needs(reads, writes))
        ins = None
        for fn in fns:
            ins = fn(eng.h)
            eng.nops += 1
        eng.sem.count += 1
        ins.then_inc(eng.sem.h, 1)
        self._record((eng.sem, eng.sem.count), reads, writes)

    def dma(self, qname, out_ap, in_ap, reads=(), writes=(), sem_tile=None):
        eng = self.engs[qname]
        self._emit_waits(eng, self._needs(reads, writes))
        if sem_tile is None:
            sem_tile = writes[0] if writes else reads[0]
        if writes:
            if sem_tile.dsem_w is None:
                sem_tile.dsem_w = self.dma_sem()
            s = sem_tile.dsem_w
        else:
            if sem_tile.dsem_r is None:
                sem_tile.dsem_r = self.dma_sem()
            s = sem_tile.dsem_r
        ins = eng.h.dma_start(out=out_ap, in_=in_ap)
        s.count += 16
        ins.then_inc(s.h, 16)
        eng.nops += 1
        self._record((s, s.count), reads, writes)
        return ins

    def collective(self, kind, groups, in_ap, out_ap, reads=(), writes=()):
        eng = self.engs["pool"]
        self._emit_waits(eng, self._needs(reads, writes))
        sem_tile = writes[0]
        if sem_tile.dsem_w is None:
            sem_tile.dsem_w = self.new_sem("cc_" + sem_tile.name)
        s = sem_tile.dsem_w
        ins = eng.h.collective_compute(kind, ALU.bypass, replica_groups=groups, ins=[in_ap], outs=[out_ap])
        s.count += 1
        ins.then_inc(s.h, 1)
        self._record((s, s.count), reads, writes)

    def release(self, tiles, engines=("pe", "act", "dve", "pool", "sp")):
        needs = self._needs((), tiles)
        for e in engines:
            self._emit_waits(self.engs[e], needs)
        if len(engines) == 5:
            for t in tiles:
                for s_ in (t.dsem_w, t.dsem_r):
                    if s_ is not None and s_.name.startswith("dq"):
                        self.free_dma.append(s_)
                t.dsem_w = None
                t.dsem_r = None
                t.w = None
                t.rs = []

    def stats(self):
        return {k: (e.nops, e.nwaits) for k, e in self.engs.items()}, self.nsem


def build_program(depth=DEPTH, same_engine_sync=True, debug=False, stop_after=None):
    nc = bass.Bass("TRN2", target_bir_lowering=False)
    dbg_list = []

    def din(name, shape, dt=F32):
        return nc.dram_tensor(name, list(shape), dt, kind="ExternalInput").ap()

    def dout(name, shape, dt=F32):
        return nc.dram_tensor(name, list(shape), dt, kind="ExternalOutput").ap()

    def dint(name, shape, dt=F32):
        return nc.dram_tensor(name, list(shape), dt, kind="Internal").ap()

    xT = din("xT", [D, TT])
    xh0 = din("xh0", [128, 8, 16])
    ckT = din("ckT", [depth, 512, 512])
    cv = din("cv", [depth, 512, 512])
    st0 = din("st0", [128, depth * 2 * 4])
    cvec = din("cvec", [128, 8, 2])
    pvec = din("pvec", [128, depth, NV])
    gfin = din("gfin", [128, 8])
    w_mod = din("w_mod", [depth, D, INW])
    w_in = din("w_in", [depth, D, INW])
    w_branch = din("w_branch", [depth, 3, 512, D])
    w_out = din("w_out", [depth, D, D])
    w_ff_in = din("w_ff_in", [depth, D, 2 * FFH])
    w_ff_out = din("w_ff_out", [depth, FFH, D])
    pool_w = din("pool_w", [depth, 128, 4 * 128])
    lru_bd = din("lru_bd", [depth, 128, 16 * 128])
    rpbG = din("rpbG", [depth, 64, 8, 15, 64])
    slotoh = din("slotoh", [128, 1024])
    rowb = din("rowb", [128, 512])
    sel = din("sel", [128, 12])
    flags = din("flags", [128, 2])
    icp = din("icp", [128, 4, 2, 8])
    ics = din("ics", [128, 4, 2, 8])

    yT = dout("yT", [D, TT])
    kT_out = dout("kT_out", [depth, 512, TP])
    v_out = dout("v_out", [depth, TP, 512])
    st_out = dout("st_out", [depth, 128, 32])

    ckv_in = [dint(f"ckv_in{i}", [2, 128, 512], BF16) for i in range(2)]
    ckv_out = [dint(f"ckv_out{i}", [4, 2, 128, 512], BF16) for i in range(2)]
    cx_in = dint("cx_in", [128, 128])
    cx_out = dint("cx_out", [4, 128, 128])
    cs_in = [dint(f"cs_in{i}", [128, 4]) for i in range(2)]
    cs_out = [dint(f"cs_out{i}", [4, 128, 4]) for i in range(2)]

    with ExitStack() as st:
        fw = FW(nc, st, same_engine_sync=same_engine_sync)

        uid = {"i": 0}

        def sb(name, shape, dt, stack=st):
            uid["i"] += 1
            return stack.enter_context(nc.sbuf_tensor(f"{name}_{uid['i']}", list(shape), dt))

        X = sb("X", [128, 8, TT], F32)
        H = sb("H", [128, 8, TT], BF16)
        YB = sb("YB", [128, 3, 4, TT], BF16)
        WST = [sb(f"WST{i}", [128, 2048], F32) for i in range(3)]
        WRG = [sb(f"WRG{i}", [128, 2048], BF16) for i in range(4)]
        PV = sb("PV", [128, depth, NV], F32)
        GF = sb("GF", [128, 8], F32)
        CV = sb("CV", [128, 8, 2], F32)
        SC = sb("SC", [128, 8, 2], BF16)
        MODV = [sb(f"MODV{i}", [128, 48, 2], F32) for i in range(2)]
        AB = sb("AB", [128, 2, 8, 2], F32)
        CL = sb("CL", [128, 8], F32)
        CL2 = sb("CL2", [128, 8], F32)
        CLH = sb("CLH", [128, 8], F32)
        CL256 = sb("CL256", [128, 8], F32)
        HBV = sb("HBV", [128, 16], F32)
        ONES = sb("ONES", [128, 128], BF16)
        ONE1 = sb("ONE1", [128, 64], BF16)
        EPSC = sb("EPSC", [128, 1], F32)
        SEL = sb("SEL", [128, 12], F32)
        FLG = sb("FLG", [128, 2], F32)
        ICP = sb("ICP", [128, 4, 2, 8], F32)
        ICS = sb("ICS", [128, 4, 2, 8], F32)
        ST0 = sb("ST0", [128, depth * 8], F32)
        SOH = sb("SOH", [128, 1024], BF16)
        RWB = sb("RWB", [128, 512], BF16)
        XH = sb("XH", [128, 8, 16], F32)
        HH = sb("HH", [128, 8, 16], BF16)
        STO = sb("STO", [128, 32], F32)

        PS = [st.enter_context(nc.psum_tensor(f"PS{i}", [128, 512], F32)) for i in range(8)]
        tPS = [T(f"PS{i}", excl=True) for i in range(8)]

        tX = [[T(f"X{k}_{t}") for t in range(NT)] for k in range(8)]
        tH = [[T(f"H{k}_{t}") for t in range(NT)] for k in range(8)]
        tYB = [[[T(f"Y{n}_{c}_{t}") for t in range(NT)] for c in range(4)] for n in range(3)]
        tWST = [T(f"WST{i}") for i in range(3)]
        tWRG = [T(f"WRG{i}") for i in range(4)]
        tC = T("consts")
        tMODV = [T("MODV0"), T("MODV1")]
        tAB = T("AB")
        tXH = T("XH")
        tHH = T("HH")
        tSTO = T("STO")
        tT2 = [T("T2r0"), T("T2r1")]

        def tsl(t):
            return slice(t * 512, (t + 1) * 512)

        def chk(name):
            if stop_after == name:
                raise _Stop()

        grp_of_tile = [0, 0, 1]

        fw.dma("sp", PV[:], pvec[:, :, :], writes=[tC])
        fw.dma("sp", GF[:], gfin[:, :], writes=[tC])
        fw.dma("sp", CV[:], cvec[:, :, :], writes=[tC])
        fw.dma("sp", SEL[:], sel[:, :], writes=[tC])
        fw.dma("sp", FLG[:], flags[:, :], writes=[tC])
        fw.dma("sp", ICP[:], icp[:, :, :, :], writes=[tC])
        fw.dma("sp", ICS[:], ics[:, :, :, :], writes=[tC])
        fw.dma("sp", ST0[:], st0[:, :], writes=[tC])
        fw.dma("sp", XH[:], xh0[:, :, :], writes=[tXH])
        tSF = T("SMALLF")
        a0 = ExitStack()
        SMALLF = sb("SMALLF", [128, 1024], F32, a0)
        fw.dma("sp", SMALLF[:], slotoh[:, :], writes=[tSF])
        fw.op("dve", lambda e: e.tensor_copy(out=SOH[:], in_=SMALLF[:]), reads=[tSF], writes=[tC])
        fw.dma("sp", SMALLF[:, 0:512], rowb[:, :], writes=[tSF])
        fw.op("dve", lambda e: e.tensor_copy(out=RWB[:], in_=SMALLF[:, 0:512]), reads=[tSF], writes=[tC])
        fw.release([tSF])
        a0.close()
        fw.op("dve", lambda e: e.memset(ONES[:], 1.0 / 1024.0), writes=[tC])
        fw.op("dve", lambda e: e.memset(ONE1[:], 1.0), writes=[tC])
        fw.op("dve", lambda e: e.memset(EPSC[:], EPS), writes=[tC])
        fw.op("act", lambda e: e.activation(out=SC[:], in_=CV[:], func=AF.Silu), reads=[tC], writes=[tC])
        xv = xT.rearrange("(k p) t -> p k t", p=128)
        for k in range(8):
            for t in range(NT):
                fw.dma("sp", X[:, k, tsl(t)], xv[:, k, tsl(t)], writes=[tX[k][t]])

        plan = []
        state = {"issued": 0}

        def add_block(segs, kc):
            ncols = sum(s[1] for s in segs)
            assert kc * ncols <= 2048
            plan.append(dict(segs=segs, kc=kc, ncols=ncols))
            return len(plan) - 1

        def issue_block(i):
            b = plan[i]
            s_i, r_i = i % 3, i % 4
            kc, ncols = b["kc"], b["ncols"]
            dst = WST[s_i][:, 0:kc * ncols].rearrange("p (k n) -> p k n", k=kc)
            c0 = 0
            for (ap, n) in b["segs"]:
                fw.dma("sp", dst[:, :, c0:c0 + n], ap, writes=[tWST[s_i]])
                c0 += n
            fw.op("act", lambda e: e.activation(out=WRG[r_i][:, 0:kc * ncols], in_=WST[s_i][:, 0:kc * ncols], func=AF.Identity),
                  reads=[tWST[s_i]], writes=[tWRG[r_i]])

        def get_block(i, lookahead=2):
            while state["issued"] <= min(i + lookahead, len(plan) - 1):
                issue_block(state["issued"])
                state["issued"] += 1
            b = plan[i]
            r_i = i % 4
            w = WRG[r_i][:, 0:b["kc"] * b["ncols"]].rearrange("p (k n) -> p k n", k=b["kc"])
            return w, tWRG[r_i]

        def wcols(wm, l, c0, n):
            return wm[l].rearrange("(k p) n -> p k n", p=128)[:, :, c0:c0 + n]

        dps = {"i": 0, "n": 4}

        def next_dps():
            i = dps["i"] % dps["n"]
            dps["i"] += 1
            return PS[i], tPS[i]

        def plan_mod(l):
            return [add_block([(wcols(w_mod, l, c * 256, 256), 256)], 8) for c in range(24)]

        def emit_mod_block(l, c, bi):
            w, tw = get_block(bi)
            for mm_ in range(2):
                m = c * 2 + mm_
                fns = []
                for k in range(8):
                    fns.append(lambda e, k=k, m=m, mm_=mm_: e.matmul(
                        PS[7][:, 2 * m:2 * m + 2], lhsT=w[:, k, mm_ * 128:(mm_ + 1) * 128], rhs=SC[:, k, :],
                        start=(k == 0), stop=(k == 7)))
                fw.mm(fns, reads=[tw, tC], writes=[tPS[7]])

        def finish_mod(l):
            mv = MODV[l % 2]
            psv = PS[7][:, 0:96].rearrange("p (m g) -> p m g", g=2)
            for g in range(2):
                fw.op("dve", lambda e, g=g: e.tensor_tensor(out=mv[:, :, g], in0=psv[:, :, g], in1=PV[:, l, 16:64], op=ALU.add),
                      reads=[tPS[7], tC], writes=[tMODV[l % 2]])

        def emit_ab(l):
            mv = MODV[l % 2]
            for which, (g0, sc0) in enumerate(((0, 8), (8, 32))):
                for g in range(2):
                    fw.op("dve", lambda e, which=which, g=g, g0=g0, sc0=sc0: e.scalar_tensor_tensor(
                        out=AB[:, which, :, g], in0=mv[:, sc0:sc0 + 8, g], scalar=1.0, in1=PV[:, l, g0:g0 + 8],
                        op0=ALU.add, op1=ALU.mult), reads=[tMODV[l % 2], tC], writes=[tAB])
            fw.op("act", lambda e: e.activation(out=CL[:], in_=PV[:, l, 104:112], func=AF.Exp, scale=-1.0), reads=[tC], writes=[tAB])
            fw.op("act", lambda e: e.activation(out=CL[:], in_=CL[:], func=AF.Ln, bias=1.0), reads=[tAB], writes=[tAB])
            fw.op("dve", lambda e: e.tensor_scalar(out=CL2[:], in0=CL[:], scalar1=-16.0, scalar2=None, op0=ALU.mult), reads=[tAB], writes=[tAB])
            fw.op("dve", lambda e: e.tensor_scalar(out=CL[:], in0=CL[:], scalar1=-8.0, scalar2=None, op0=ALU.mult), reads=[tAB], writes=[tAB])
            fw.op("dve", lambda e: e.tensor_scalar(out=CLH[:], in0=CL[:], scalar1=0.5, scalar2=None, op0=ALU.mult), reads=[tAB], writes=[tAB])
            fw.op("dve", lambda e: e.tensor_scalar(out=CL256[:], in0=CL[:], scalar1=256.0, scalar2=None, op0=ALU.mult), reads=[tAB], writes=[tAB])
            fw.op("dve", lambda e: e.tensor_scalar(out=HBV[:], in0=PV[:, l, 88:104], scalar1=0.5, scalar2=None, op0=ALU.mult), reads=[tC], writes=[tAB])

        def emit_norm(l, which, arena):
            SQ, tSQ, RS, tRS, TMP, tTMP = arena
            mv = MODV[l % 2]
            b0 = 0 if which == 0 else 24
            pss = []
            qi = 0
            for t in range(NT):
                ps, tps = next_dps()
                pss.append((ps, tps))
                for k in range(8):
                    j = qi % len(SQ)
                    qi += 1
                    fw.op("pool", lambda e, k=k, j=j, t=t: e.tensor_tensor(out=SQ[j][:], in0=X[:, k, tsl(t)], in1=X[:, k, tsl(t)], op=ALU.mult),
                          reads=[tX[k][t]], writes=[tSQ[j]])
                    fw.mm([lambda e, k=k, j=j, ps=ps: e.matmul(ps[:], lhsT=ONES[:], rhs=SQ[j][:], start=(k == 0), stop=(k == 7))],
                          reads=[tSQ[j], tC], writes=[tps])
            for t in range(NT):
                ps, tps = pss[t]
                fw.op("act", lambda e, t=t, ps=ps: e.activation(out=RS[t][:], in_=ps[:], func=AF.Ln, bias=EPSC[:, 0:1]), reads=[tps, tC], writes=[tRS[t]])
                fw.op("act", lambda e, t=t: e.activation(out=RS[t][:], in_=RS[t][:], func=AF.Exp, scale=-0.5), reads=[tRS[t]], writes=[tRS[t]])
            qi = 0
            for t in range(NT):
                g = grp_of_tile[t]
                for k in range(8):
                    j = qi % len(TMP)
                    qi += 1
                    fw.op("dve", lambda e, k=k, j=j, t=t: e.tensor_tensor(out=TMP[j][:], in0=X[:, k, tsl(t)], in1=RS[t][:], op=ALU.mult),
                          reads=[tX[k][t], tRS[t]], writes=[tTMP[j]])
                    fw.op("act", lambda e, k=k, j=j, g=g, t=t: e.activation(
                        out=H[:, k, tsl(t)], in_=TMP[j][:], func=AF.Identity,
                        scale=AB[:, which, k, g:g + 1], bias=mv[:, b0 + k, g:g + 1]),
                        reads=[tTMP[j], tAB, tMODV[l % 2]], writes=[tH[k][t]])

        def emit_norm_halo(l, arena):
            SQ, tSQ, RS, tRS, TMP, tTMP = arena
            mv = MODV[l % 2]
            ps, tps = next_dps()
            fw.op("act", lambda e: e.activation(out=SQ[0][:, 0:128], in_=XH[:].rearrange("p k t -> p (k t)"), func=AF.Square),
                  reads=[tXH], writes=[tSQ[0]])
            sqv = SQ[0][:, 0:128].rearrange("p (k t) -> p k t", k=8)
            fw.mm([lambda e, k=k: e.matmul(ps[:, 0:16], lhsT=ONES[:], rhs=sqv[:, k, :], start=(k == 0), stop=(k == 7)) for k in range(8)],
                  reads=[tSQ[0], tC], writes=[tps])
            fw.op("act", lambda e: e.activation(out=RS[0][:, 0:16], in_=ps[:, 0:16], func=AF.Ln, bias=EPSC[:, 0:1]), reads=[tps, tC], writes=[tRS[0]])
            fw.op("act", lambda e: e.activation(out=RS[0][:, 0:16], in_=RS[0][:, 0:16], func=AF.Exp, scale=-0.5), reads=[tRS[0]], writes=[tRS[0]])
            for k in range(8):
                fw.op("dve", lambda e, k=k: e.tensor_tensor(out=TMP[0][:, 0:16], in0=XH[:, k, :], in1=RS[0][:, 0:16], op=ALU.mult),
                      reads=[tXH, tRS[0]], writes=[tTMP[0]])
                fw.op("act", lambda e, k=k: e.activation(out=HH[:, k, :], in_=TMP[0][:, 0:16], func=AF.Identity,
                                                          scale=AB[:, 0, k, 1:2], bias=mv[:, k, 1:2]),
                      reads=[tTMP[0], tAB, tMODV[l % 2]], writes=[tHH])

        def dense(w, tw, mcol, kcn, rhs_fn, rhs_tiles_fn, t, ps, tps, pcols=slice(0, 512)):
            fns = []
            rt = []
            for k in range(kcn):
                fns.append(lambda e, k=k: e.matmul(ps[:, pcols], lhsT=w[:, k, mcol:mcol + 128], rhs=rhs_fn(k, t),
                                                   start=(k == 0), stop=(k == kcn - 1)))
                rt.append(rhs_tiles_fn(k, t))
            fw.mm(fns, reads=[tw] + rt, writes=[tps])

        hrhs = lambda k, t: H[:, k, tsl(t)]
        hrt = lambda k, t: tH[k][t]

        def plan_layer(l):
            P = {}
            P["qk"] = [add_block([(wcols(w_in, l, 512 + m * 128, 128), 128), (wcols(w_in, l, 1024 + m * 128, 128), 128)], 8) for m in range(4)]
            P["v"] = [add_block([(wcols(w_in, l, 1536 + m * 128, 128), 128)], 8) for m in range(4)]
            return P

        order = []

        LP = []
        mod_blocks = {}
        mod_blocks[0] = plan_mod(0)
        for l in range(depth):
            P = {}
            P["qkv"] = []
            for m in range(4):
                P["qkv"].append((
                    add_block([(wcols(w_in, l, 512 + m * 128, 128), 128), (wcols(w_in, l, 1024 + m * 128, 128), 128)], 8),
                    add_block([(wcols(w_in, l, 1536 + m * 128, 128), 128)], 8)))
            P["poolw"] = add_block([(pool_w[l].rearrange("c (o n) -> c o n", o=1), 512)], 1)
            P["pool"] = [add_block([(wcols(w_in, l, gp * 256, 256), 256)], 8) for gp in range(2)]
            P["bd"] = add_block([(lru_bd[l].rearrange("c (o n) -> c o n", o=1), 2048)], 1)
            P["lru"] = [add_block([(wcols(w_in, l, 2048 + c * 128, 128), 128), (wcols(w_in, l, 2560 + c * 128, 128), 128)], 8) for c in range(4)]
            P["gate"] = []
            for jp in range(4):
                for n in range(3):
                    P["gate"].append((
                        add_block([(wcols(w_in, l, 3072 + n * 1024 + jp * 256, 256), 256)], 8),
                        add_block([(w_branch[l, n].rearrange("(k p) n -> p k n", p=128)[:, :, jp * 256:(jp + 1) * 256], 256)], 4)))
            P["out"] = [add_block([(wcols(w_out, l, mp * 256, 256), 256)], 8) for mp in range(4)]
            P["ffseq"] = []
            halves = ((0, 12), (12, 22))
            nmod = 0
            for hf, (c0, c1) in enumerate(halves):
                for i in range(c0, c1):
                    P["ffseq"].append(("ffin", (hf, i - c0), add_block(
                        [(wcols(w_ff_in, l, i * 128, 128), 128), (wcols(w_ff_in, l, FFH + i * 128, 128), 128)], 8)))
                    if l + 1 < depth:
                        for _ in range(2 if i in (5, 16) else 1):
                            if nmod < 24:
                                P["ffseq"].append(("mod", nmod, add_block([(wcols(w_mod, l + 1, nmod * 256, 256), 256)], 8)))
                                nmod += 1
                nk = c1 - c0
                for j in range(8):
                    P["ffseq"].append(("ffout", (hf, j, nk), add_block(
                        [(w_ff_out[l].rearrange("(k p) n -> p k n", p=128)[:, c0:c1, j * 128:(j + 1) * 128], 128)], nk)))
            assert nmod == (24 if l + 1 < depth else 0)
            LP.append(P)

        cons = {"i": 0}

        def take(bi):
            assert bi == cons["i"], (bi, cons["i"])
            cons["i"] += 1
            return get_block(bi)

        try:
            for c in range(24):
                bi = mod_blocks[0][c]
                assert bi == cons["i"]
                cons["i"] += 1
                emit_mod_block(0, c, bi)
            finish_mod(0)
            chk("mod")

            tKVI = [T("kvin0"), T("kvin1")]
            tKVO = [T("kvout0"), T("kvout1")]
            tCXI, tCXO = T("cxin"), T("cxout")
            tCSI = [T("csin0"), T("csin1")]
            tCSO = [T("csout0"), T("csout1")]

            for l in range(depth):
                P = LP[l]
                emit_ab(l)
                mv = MODV[l % 2]
                tmv = tMODV[l % 2]

                with Arena() as a1:
                    SQ = [sb(f"SQ{j}", [128, 512], BF16, a1) for j in range(4)]
                    RS = [sb(f"RS{j}", [128, 512], F32, a1) for j in range(3)]
                    TMP = [sb(f"TMP{j}", [128, 512], F32, a1) for j in range(4)]
                    tSQ = [T(f"SQ{j}") for j in range(4)]
                    tRS = [T(f"RS{j}") for j in range(3)]
                    tTMP = [T(f"TMP{j}") for j in range(4)]
                    arena = (SQ, tSQ, RS, tRS, TMP, tTMP)
                    emit_norm(l, 0, arena)
                    if l >= 1:
                        XCAND = sb("XCAND", [128, 4, 128], F32, a1)
                        XSEL = sb("XSEL", [128, 8, 8], F32, a1)
                        tXCAND, tXSEL = T("XCAND"), T("XSEL")
                        fw.dma("sp", XCAND[:], cx_out.rearrange("r p n -> p r n"), reads=[tCXO], writes=[tXCAND])
                        for (d0, s0, c0) in ((0, 0, 8), (8, 4, 0)):
                            cv_ = lambda r, c0=c0: XCAND[:, r, :].rearrange("p (k t) -> p k t", k=8)[:, :, c0:c0 + 8]
                            fw.op("dve", lambda e, d0=d0, s0=s0, cv_=cv_: e.tensor_scalar(out=XH[:, :, d0:d0 + 8], in0=cv_(0), scalar1=SEL[:, s0:s0 + 1],
                                                                                         scalar2=None, op0=ALU.mult), reads=[tXCAND, tC], writes=[tXH])
                            for r in range(1, 4):
                                fw.op("dve", lambda e, d0=d0, s0=s0, cv_=cv_, r=r: e.scalar_tensor_tensor(
                                    out=XH[:, :, d0:d0 + 8], in0=cv_(r), scalar=SEL[:, s0 + r:s0 + r + 1], in1=XH[:, :, d0:d0 + 8],
                                    op0=ALU.mult, op1=ALU.add), reads=[tXCAND, tC, tXH], writes=[tXH])
                    emit_norm_halo(l, arena)
                    chk("norm")
                    fw.release(tSQ + tRS + tTMP + ([tXCAND, tXSEL] if l >= 1 else []))

                with Arena() as a2:
                    QT = sb("QT", [128, TT], BF16, a2)
                    KT = sb("KT", [128, TT], BF16, a2)
                    VT = sb("VT", [128, 12, 128], BF16, a2)
                    KWs = [sb(f"KW{j}", [128, 1024], BF16, a2) for j in range(2)]
                    VWs = [sb(f"VW{j}", [128, 8, 128], BF16, a2) for j in range(2)]
                    CKs = [sb(f"CK{j}", [128, 512], BF16, a2) for j in range(2)]
                    CVBs = [sb(f"CVB{j}", [128, 4, 128], BF16, a2) for j in range(2)]
                    QSs = [sb(f"QS{j}", [128, 512], BF16, a2) for j in range(2)]
                    CAND = sb("CAND", [128, 4, 512], BF16, a2)
                    CSEL = sb("CSEL", [128, 256], BF16, a2)
                    tCSEL = T("CSEL")
                    T2 = [sb(f"T2r{j}", [128, 22, 64], F32, a2) for j in range(2)]
                    FS = [sb(f"FS{j}", [128, 512], F32, a2) for j in range(3)]
                    PT = [sb(f"PT{j}", [128, 512], BF16, a2) for j in range(4)]
                    RC = sb("RC", [128, 512], F32, a2)
                    tQT = [T(f"QT{t}") for t in range(NT)]
                    tKT = [T(f"KT{t}") for t in range(NT)]
                    tVT = [T(f"VT{t}") for t in range(NT)]
                    tKWs, tVWs, tCKs, tCVBs, tQSs = ([T(f"{n_}{j}") for j in range(2)] for n_ in ("KW", "VW", "CK", "CVB", "QS"))
                    tCAND = T("CAND")
                    tFS = [T(f"FS{j}") for j in range(3)]
                    tPT = [T("PT0"), T("PT1"), T("PT2"), T("PT3")]
                    tRC = T("RC")
                    arena_tiles = tQT + tKT + tVT + tKWs + tVWs + tCKs + tCVBs + tQSs + [tCAND, tCSEL] + tFS + tPT + [tRC] + tT2
                    for j in range(2):
                        fw.op("pool", lambda e, j=j: e.memset(T2[j][:], 0.0), writes=[tT2[j]])
                    fsc = {"i": 0}

                    def next_fs():
                        i = fsc["i"] % 3
                        fsc["i"] += 1
                        return FS[i], tFS[i]

                    cnt = {"sb": 0, "pt": 0, "sps": 0, "t2": 0}
                    dps["n"] = 3

                    def exp_tile(src_ps, tsrc, width, bias_ap=None, bias_tiles=()):
                        j = cnt["pt"] % 4
                        cnt["pt"] += 1
                        if bias_ap is None:
                            fw.op("act", lambda e: e.activation(out=PT[j][:, 0:width], in_=src_ps, func=AF.Exp, scale=SCALE),
                                  reads=[tsrc], writes=[tPT[j]])
                        else:
                            sbf, tsbf = next_fs()
                            fw.op("dve", lambda e: e.scalar_tensor_tensor(out=sbf[:, 0:width], in0=src_ps, scalar=SCALE, in1=bias_ap,
                                                                           op0=ALU.mult, op1=ALU.add),
                                  reads=[tsrc] + list(bias_tiles), writes=[tsbf])
                            fw.op("act", lambda e: e.activation(out=PT[j][:, 0:width], in_=sbf[:, 0:width], func=AF.Exp),
                                  reads=[tsbf], writes=[tPT[j]])
                        return PT[j], tPT[j]

                    def next_sps():
                        i = 3 + cnt["sps"] % 3
                        cnt["sps"] += 1
                        return PS[i], tPS[i]

                    def run_pipeline(steps, depth_):
                        G = 2
                        n = len(steps)
                        outs = [None] * n
                        ngr = (n + G - 1) // G
                        for g in range(ngr + 1):
                            if g < ngr:
                                for i in range(g * G, min(n, (g + 1) * G)):
                                    outs[i] = steps[i][0]()
                            if g >= 1:
                                for i in range((g - 1) * G, min(n, g * G)):
                                    steps[i][1](outs[i])

                    QRANGE = {0: (0, 128), 1: (0, 256), 6: (320, 512), 7: (448, 512)}

                    def sample_attention(m):
                        pb = m % 2
                        KW, VW, CK, CVB, QS = KWs[pb], VWs[pb], CKs[pb], CVBs[pb], QSs[pb]
                        tKW, tVW, tCK, tCVB, tQS = tKWs[pb], tVWs[pb], tCKs[pb], tCVBs[pb], tQSs[pb]
                        steps = []
                        nchunks = 12
                        for hh in range(2):
                            h = 2 * m + hh
                            hs = slice(hh * 64, (hh + 1) * 64)
                            for c in range(nchunks):
                                def s1(hh=hh, h=h, hs=hs, c=c):
                                    if c == 0:
                                        j2 = cnt["t2"] % 2
                                        cnt["t2"] += 1
                                        src = rpbG[l, :, h, :, :]
                                        fw.dma("act", T2[j2][0:64, 3:18, :], src, writes=[tT2[j2]])
                                        fw.dma("act", T2[j2][64:128, 4:19, :], src, writes=[tT2[j2]])
                                        cnt["t2cur"] = j2
                                    j2 = cnt["t2cur"]
                                    t2flat = T2[j2][:].rearrange("p e w -> p (e w)")
                                    sps, tsps = next_sps()
                                    if c < 8:
                                        q0, q1 = QRANGE.get(c, (0, 512))
                                        fw.mm([
                                            lambda e: e.matmul(sps[:, q0:q1], lhsT=KW[hs, c * 128:(c + 1) * 128], rhs=QS[hs, q0:q1], start=True, stop=False),
                                            lambda e: e.matmul(sps[:, q0:q1], lhsT=SOH[hs, c * 128:(c + 1) * 128], rhs=RWB[hs, q0:q1], start=False, stop=True),
                                        ], reads=[tKW, tQS, tC], writes=[tsps])
                                        e0 = (14 - 2 * c) * 64
                                        pt, tpt = exp_tile(sps[:, q0:q1], tsps, q1 - q0, bias_ap=t2flat[:, e0 + q0:e0 + q1], bias_tiles=[tT2[j2]])
                                        return pt, tpt, q0, q1
                                    cc = c - 8
                                    fw.mm([lambda e: e.matmul(sps[:], lhsT=CK[hs, cc * 128:(cc + 1) * 128], rhs=QS[hs, :], start=True, stop=True)],
                                          reads=[tCK, tQS], writes=[tsps])
                                    pt, tpt = exp_tile(sps[:], tsps, 512)
                                    return pt, tpt, 0, 512

                                def s2(o, hh=hh, hs=hs, c=c):
                                    pt, tpt, q0, q1 = o
                                    if c < 8:
                                        vl, vt = VW[:, c, hs], tVW
                                    else:
                                        vl, vt = CVB[:, c - 8, hs], tCVB
                                    first, last = (c == 0), (c == nchunks - 1)
                                    fw.mm([
                                        lambda e: e.matmul(PS[6][hs, q0:q1], lhsT=vl, rhs=pt[:, 0:q1 - q0], start=first, stop=last),
                                        lambda e: e.matmul(PS[7][hs, q0:q1], lhsT=ONE1[:, :], rhs=pt[:, 0:q1 - q0], start=first, stop=last),
                                    ], reads=[vt, tpt, tC], writes=[tPS[6], tPS[7]])
                                    if hh == 1 and last:
                                        fw.op("act", lambda e: e.activation(out=RC[:], in_=PS[7][:], func=AF.Ln), reads=[tPS[7]], writes=[tRC])
                                        fw.op("act", lambda e: e.activation(out=RC[:], in_=RC[:], func=AF.Exp, scale=-1.0), reads=[tRC], writes=[tRC])
                                        fw.op("dve", lambda e: e.tensor_tensor(out=YB[:, 1, m, 1024:1536], in0=PS[6][:], in1=RC[:], op=ALU.mult),
                                              reads=[tPS[6], tRC], writes=[tYB[1][m][2]])
                                steps.append((s1, s2))
                        run_pipeline(steps, 2)

                    def prompt_attention(m, mid_hook=None):
                        steps = []
                        for tp in range(2):
                            for hh in range(2):
                                hs = slice(hh * 64, (hh + 1) * 64)
                                for s2_ in range(2):
                                    s = tp * 2 + s2_
                                    q0 = s * 256

                                    def s1(tp=tp, hs=hs, q0=q0):
                                        sps, tsps = next_sps()
                                        fw.mm([lambda e, kc=kc: e.matmul(sps[:, kc * 256:(kc + 1) * 256], lhsT=KT[hs, q0 + kc * 128:q0 + (kc + 1) * 128],
                                                                         rhs=QT[hs, q0:q0 + 256], start=True, stop=True) for kc in range(2)],
                                              reads=[tKT[tp], tQT[tp]], writes=[tsps])
                                        return exp_tile(sps[:], tsps, 512)

                                    def s2(o, tp=tp, hh=hh, hs=hs, s2_=s2_, s=s):
                                        pt, tpt = o
                                        oc = slice(s2_ * 256, (s2_ + 1) * 256)
                                        fns = []
                                        for kc in range(2):
                                            fns.append(lambda e, kc=kc: e.matmul(PS[6][hs, oc], lhsT=VT[:, s * 2 + kc, hs], rhs=pt[:, kc * 256:(kc + 1) * 256],
                                                                                 start=(kc == 0), stop=(kc == 1)))
                                            fns.append(lambda e, kc=kc: e.matmul(PS[7][hs, oc], lhsT=ONE1[:, :], rhs=pt[:, kc * 256:(kc + 1) * 256],
                                                                                 start=(kc == 0), stop=(kc == 1)))
                                        fw.mm(fns, reads=[tVT[tp], tpt, tC], writes=[tPS[6], tPS[7]])
                                        if hh == 1 and s2_ == 1:
                                            fw.op("act", lambda e: e.activation(out=RC[:], in_=PS[7][:], func=AF.Ln), reads=[tPS[7]], writes=[tRC])
                                            fw.op("act", lambda e: e.activation(out=RC[:], in_=RC[:], func=AF.Exp, scale=-1.0), reads=[tRC], writes=[tRC])
                                            fw.op("dve", lambda e: e.tensor_tensor(out=YB[:, 1, m, tsl(tp)], in0=PS[6][:], in1=RC[:], op=ALU.mult),
                                                  reads=[tPS[6], tRC], writes=[tYB[1][m][tp]])
                                            if tp == 0 and mid_hook is not None:
                                                mid_hook()
                                    steps.append((s1, s2))
                        run_pipeline(steps, 2)

                    for m in range(4):
                        bqk, bv = P["qkv"][m]
                        wqk, twqk = take(bqk)
                        for t in range(NT):
                            ps, tps = next_dps()
                            dense(wqk, twqk, 0, 8, hrhs, hrt, t, ps, tps)
                            fw.op("act", lambda e, t=t, ps=ps: e.activation(out=QT[:, tsl(t)], in_=ps[:], func=AF.Identity),
                                  reads=[tps], writes=[tQT[t]])
                            chk(f"q{m}_{t}")
                            ps, tps = next_dps()
                            dense(wqk, twqk, 128, 8, hrhs, hrt, t, ps, tps)
                            if t < 2:
                                kf, tkf = next_fs()
                                fw.op("act", lambda e, kf=kf, ps=ps: e.activation(out=kf[:], in_=ps[:], func=AF.Identity),
                                      reads=[tps], writes=[tkf])
                                fw.op("dve", lambda e, t=t, kf=kf: e.tensor_copy(out=KT[:, tsl(t)], in_=kf[:]), reads=[tkf], writes=[tKT[t]])
                                chk(f"kc{m}_{t}")
                                fw.dma("act", kT_out[l, m * 128:(m + 1) * 128, tsl(t)], kf[:], reads=[tkf])
                                chk(f"kd{m}_{t}")
                            else:
                                fw.op("dve", lambda e, t=t, ps=ps: e.tensor_copy(out=KT[:, tsl(t)], in_=ps[:]), reads=[tps], writes=[tKT[t]])
                        chk(f"qk{m}")
                        wv, twv = take(bv)
                        for t in range(NT):
                            ps, tps = next_dps()
                            for q4 in range(4):
                                tt = t * 4 + q4
                                fw.mm([lambda e, k=k, tt=tt, q4=q4: e.matmul(ps[:, q4 * 128:(q4 + 1) * 128], lhsT=H[:, k, tt * 128:(tt + 1) * 128],
                                                                             rhs=wv[:, k, :], start=(k == 0), stop=(k == 7)) for k in range(8)],
                                      reads=[twv] + [tH[k][t] for k in range(8)], writes=[tps])
                            vf, tvf = next_fs()
                            fw.op("dve", lambda e, vf=vf, ps=ps: e.tensor_copy(out=vf[:], in_=ps[:]), reads=[tps], writes=[tvf])
                            fw.op("pool", lambda e, vf=vf, t=t: e.tensor_copy(out=VT[:, t * 4:(t + 1) * 4, :].rearrange("p a b -> p (a b)"), in_=vf[:]),
                                  reads=[tvf], writes=[tVT[t]])
                            if t < 2:
                                fw.dma("act", v_out[l, t * 512:(t + 1) * 512, m * 128:(m + 1) * 128].rearrange("(a p) n -> p a n", p=128),
                                       vf[:].rearrange("p (a n) -> p a n", a=4), reads=[tvf])
                        chk(f"v{m}")
                        bb = m % 2
                        fw.dma("pool", ckv_in[bb][0], KT[:, 1024:1536], reads=[tKT[2]], writes=[tKVI[bb]])
                        fw.dma("pool", ckv_in[bb][1], VT[:, 8:12, :].rearrange("p a b -> p (a b)"), reads=[tVT[2]], writes=[tKVI[bb]])
                        fw.collective("AllGather", GROUPS, ckv_in[bb].rearrange("a p n -> (a p) n"),
                                      ckv_out[bb].rearrange("r a p n -> (r a p) n"), reads=[tKVI[bb]], writes=[tKVO[bb]])
                        pb = m % 2
                        fw.op("pool", lambda e, pb=pb: e.tensor_copy(out=QSs[pb][:], in_=QT[:, 1024:1536]), reads=[tQT[2]], writes=[tQSs[pb]])
                        fw.op("pool", lambda e, pb=pb: e.tensor_copy(out=KWs[pb][:, 256:768], in_=KT[:, 1024:1536]), reads=[tKT[2]], writes=[tKWs[pb]])
                        fw.op("pool", lambda e, pb=pb: e.tensor_copy(out=VWs[pb][:, 2:6, :], in_=VT[:, 8:12, :]), reads=[tVT[2]], writes=[tVWs[pb]])
                        ckf, tckf = next_fs()
                        fw.dma("pool", ckf[:], ckT[l, m * 128:(m + 1) * 128, :], writes=[tckf])
                        fw.op("pool", lambda e, ckf=ckf, pb=pb: e.tensor_copy(out=CKs[pb][:], in_=ckf[:]), reads=[tckf], writes=[tCKs[pb]])
                        cvf, tcvf = next_fs()
                        fw.dma("pool", cvf[:].rearrange("p (a n) -> p a n", a=4), cv[l, :, m * 128:(m + 1) * 128].rearrange("(a p) n -> p a n", p=128), writes=[tcvf])
                        fw.op("pool", lambda e, cvf=cvf, pb=pb: e.tensor_copy(out=CVBs[pb][:].rearrange("p a n -> p (a n)"), in_=cvf[:]), reads=[tcvf], writes=[tCVBs[pb]])
                        chk(f"cc{m}")

                        def window(mm_):
                            pb_ = mm_ % 2
                            KW, VW = KWs[pb_], VWs[pb_]
                            for a in range(2):
                                fw.dma("sp", CAND[:], ckv_out[pb_][:, a].rearrange("r p n -> p r n"), reads=[tKVO[pb_]], writes=[tCAND])
                                if a == 0:
                                    dsts = ((KW[:, 0:256], 256, 0), (KW[:, 768:1024], 0, 4))
                                    tw_ = tKWs[pb_]
                                else:
                                    dsts = ((VW[:, 0:2, :].rearrange("p a b -> p (a b)"), 256, 0), (VW[:, 6:8, :].rearrange("p a b -> p (a b)"), 0, 4))
                                    tw_ = tVWs[pb_]
                                for (dst, c0, s0) in dsts:
                                    fw.op("dve", lambda e, dst=dst, c0=c0, s0=s0: e.tensor_scalar(
                                        out=dst, in0=CAND[:, 0, c0:c0 + 256], scalar1=SEL[:, s0:s0 + 1], scalar2=None, op0=ALU.mult),
                                        reads=[tCAND, tC], writes=[tw_])
                                    for r in range(1, 4):
                                        fw.op("dve", lambda e, dst=dst, c0=c0, s0=s0, r=r: e.scalar_tensor_tensor(
                                            out=dst, in0=CAND[:, r, c0:c0 + 256], scalar=SEL[:, s0 + r:s0 + r + 1], in1=dst,
                                            op0=ALU.mult, op1=ALU.add), reads=[tCAND, tC], writes=[tw_])

                        prompt_attention(m, mid_hook=(lambda: window(m - 1)) if m >= 1 else None)
                        chk(f"pattn{m}")
                        if m >= 1:
                            sample_attention(m - 1)
                        if m == 3:
                            window(3)
                            sample_attention(3)
                        chk(f"sattn{m}")
                    dps["n"] = 4
                    fw.release(arena_tiles)

                if debug and l == 0:
                    dbg_list.append(("H", H[:], [tH[k][t] for k in range(8) for t in range(NT)]))
                    dbg_list.append(("YB", YB[:], [tYB[n][c][t] for n in range(3) for c in range(4) for t in range(NT)]))
                    dbg_list.append(("MODV", MODV[0][:], [tMODV[0]]))
                chk("attn")

                with Arena() as a3:
                    LP_ = 1616
                    UPB = [sb(f"UP{j}", [128, LP_], F32, a3) for j in range(2)]
                    PB = sb("PB", [128, LP_], F32, a3)
                    PC = sb("PC", [128, LP_], F32, a3)
                    DG = [sb(f"DG{j}", [128, TT], BF16, a3) for j in range(2)]
                    ETMP = sb("ETMP", [128, 64], F32, a3)
                    tUP = [T("UP0"), T("UP1")]
                    tPB, tPC, tETMP = T("PB"), T("PC"), T("ETMP")
                    tDG = [T("DG0"), T("DG1")]
                    wpw0, twpw0 = take(P["poolw"])
                    POOLW = sb("POOLW", [128, 1, 512], BF16, a3)
                    twpw = T("POOLW")
                    fw.op("dve", lambda e: e.tensor_copy(out=POOLW[:], in_=wpw0), reads=[twpw0], writes=[twpw])
                    wpw = POOLW
                    for gp in range(2):
                        w, tw = take(P["pool"][gp])
                        for gg in range(2):
                            g = gp * 2 + gg
                            jb = g % 2
                            UP, tup = UPB[jb], tUP[jb]
                            fw.op("pool", lambda e, UP=UP: e.memset(UP[:], 0.0), writes=[tup])
                            for t in range(NT):
                                ps, tps = next_dps()
                                dense(w, tw, gg * 128, 8, hrhs, hrt, t, ps, tps)
                                if t < 2:
                                    dstv = UP[:, t * 544:(t + 1) * 544].rearrange("p (s n) -> p s n", s=2)[:, :, 8:264]
                                    srcv = ps[:].rearrange("p (s n) -> p s n", s=2)
                                else:
                                    dstv = UP[:, 1096:1608]
                                    srcv = ps[:]
                                fw.op("act", lambda e, dstv=dstv, srcv=srcv: e.activation(out=dstv, in_=srcv, func=AF.Identity),
                                      reads=[tps], writes=[tup])
                            ps, tps = next_dps()
                            fw.mm([lambda e, k=k: e.matmul(ps[:, 0:16], lhsT=w[:, k, gg * 128:(gg + 1) * 128], rhs=HH[:, k, :],
                                                           start=(k == 0), stop=(k == 7)) for k in range(8)],
                                  reads=[tw, tHH], writes=[tps])
                            fw.op("dve", lambda e, UP=UP, ps=ps: e.tensor_scalar(out=UP[:, 1088:1096], in0=ps[:, 0:8], scalar1=FLG[:, 0:1],
                                                                               scalar2=None, op0=ALU.mult), reads=[tps, tC], writes=[tup])
                            fw.op("dve", lambda e, UP=UP, ps=ps: e.tensor_scalar(out=UP[:, 1608:1616], in0=ps[:, 8:16], scalar1=FLG[:, 1:2],
                                                                               scalar2=None, op0=ALU.mult), reads=[tps, tC], writes=[tup])
                            L_ = LP_
                            fw.op("dve", lambda e, UP=UP: e.tensor_tensor(out=PB[:, 1:L_], in0=UP[:, 0:L_ - 1], in1=UP[:, 1:L_], op=ALU.add),
                                  reads=[tup], writes=[tPB])
                            S, tS = PB, tPB
                            if g >= 1:
                                fw.op("dve", lambda e: e.tensor_tensor(out=PC[:, 2:L_ - 1], in0=PB[:, 1:L_ - 2], in1=PB[:, 3:L_], op=ALU.add),
                                      reads=[tPB], writes=[tPC])
                                S, tS = PC, tPC
                            if g >= 2:
                                fw.op("dve", lambda e: e.tensor_tensor(out=PB[:, 4:L_ - 3], in0=PC[:, 2:L_ - 5], in1=PC[:, 6:L_ - 1], op=ALU.add),
                                      reads=[tPC], writes=[tPB])
                                S, tS = PB, tPB
                            if g >= 3:
                                fw.op("dve", lambda e: e.tensor_tensor(out=PC[:, 8:L_ - 7], in0=PB[:, 4:L_ - 11], in1=PB[:, 12:L_ - 3], op=ALU.add),
                                      reads=[tPB], writes=[tPC])
                                S, tS = PC, tPC
                            dg, tdg = DG[jb], tDG[jb]
                            invw = 1.0 / (2 ** (g + 1))

                            def pv_(buf):
                                return buf[:, 0:1088].rearrange("p (s n) -> p s n", s=4)
                            fw.op("dve", lambda e, S=S, UP=UP, dg=dg: e.scalar_tensor_tensor(
                                out=dg[:, 0:1024].rearrange("p (s n) -> p s n", s=4), in0=pv_(S)[:, :, 8:264], scalar=invw,
                                in1=pv_(UP)[:, :, 8:264], op0=ALU.mult, op1=ALU.subtract), reads=[tS, tup], writes=[tdg])
                            fw.op("dve", lambda e, S=S, UP=UP, dg=dg: e.scalar_tensor_tensor(
                                out=dg[:, 1024:1536], in0=S[:, 1096:1608], scalar=invw, in1=UP[:, 1096:1608],
                                op0=ALU.mult, op1=ALU.subtract), reads=[tS, tup], writes=[tdg])
                            for side in range(2):
                                o = 8 if side == 0 else 256
                                od = 0 if side == 0 else 248
                                ic = ICP[:, g, side, :]
                                icb = bass.AP(ic.tensor, ic.offset, [list(ic.ap[0]), [0, 4], [1, 8]])
                                fw.op("dve", lambda e, S=S, o=o, icb=icb: e.tensor_tensor(
                                    out=ETMP[:, 0:32].rearrange("p (s n) -> p s n", s=4), in0=pv_(S)[:, :, o:o + 8], in1=icb, op=ALU.mult),
                                    reads=[tS, tC], writes=[tETMP])
                                fw.op("dve", lambda e, UP=UP, dg=dg, o=o, od=od: e.tensor_tensor(
                                    out=dg[:, 0:1024].rearrange("p (s n) -> p s n", s=4)[:, :, od:od + 8],
                                    in0=ETMP[:, 0:32].rearrange("p (s n) -> p s n", s=4), in1=pv_(UP)[:, :, o:o + 8], op=ALU.subtract),
                                    reads=[tETMP, tup], writes=[tdg])
                                o2 = 1096 if side == 0 else 1600
                                od2 = 1024 if side == 0 else 1528
                                fw.op("dve", lambda e, S=S, o2=o2, side=side: e.tensor_tensor(
                                    out=ETMP[:, 32:40], in0=S[:, o2:o2 + 8], in1=ICS[:, g, side, :], op=ALU.mult),
                                    reads=[tS, tC], writes=[tETMP])
                                fw.op("dve", lambda e, UP=UP, dg=dg, o2=o2, od2=od2: e.tensor_tensor(
                                    out=dg[:, od2:od2 + 8], in0=ETMP[:, 32:40], in1=UP[:, o2:o2 + 8], op=ALU.subtract),
                                    reads=[tETMP, tup], writes=[tdg])
                            for t in range(NT):
                                ps, tps = next_dps()
                                fw.mm([lambda e, dg=dg, t=t, ps=ps: e.matmul(ps[:], lhsT=wpw[:, 0, g * 128:(g + 1) * 128], rhs=dg[:, tsl(t)],
                                                                              start=True, stop=True)], reads=[twpw, tdg], writes=[tps])
                                fw.op("act", lambda e, t=t, ps=ps: e.activation(out=YB[:, 0, g, tsl(t)], in_=ps[:], func=AF.Identity,
                                                                               scale=PV[:, l, 64 + g:65 + g]),
                                      reads=[tps, tC], writes=[tYB[0][g][t]])
                    fw.release(tUP + [tPB, tPC, tETMP, twpw] + tDG)
                chk("pool")

                with Arena() as a4:
                    LU = 1556
                    UX = sb("UX", [128, LU], F32, a4)
                    GY = sb("GY", [128, TT], BF16, a4)
                    XC = sb("XC", [128, TT], F32, a4)
                    XCB = sb("XCB", [128, TT], BF16, a4)
                    HS = [sb(f"HS{e_}", [128, TT], F32, a4) for e_ in range(2)]
                    GT = [sb(f"GT{j}", [128, 512], F32, a4) for j in range(8)]
                    CAR = sb("CAR", [128, 4], F32, a4)
                    CARG = sb("CARG", [128, 4, 4], F32, a4)
                    HINS = sb("HINS", [128, 16], F32, a4)
                    SRG = sb("SRG", [128, 2], F32, a4)
                    tUX, tGY = T("UX"), [T(f"GY{t}") for t in range(NT)]
                    tXC, tXCB = [T(f"XC{t}") for t in range(NT)], [T(f"XCB{t}") for t in range(NT)]
                    tHS = [[T(f"HS{e_}_{t}") for t in range(NT)] for e_ in range(2)]
                    tGT = [T(f"GT{j}") for j in range(8)]
                    tCAR, tCARG, tHINS, tSRG = T("CAR"), T("CARG"), T("HINS"), T("SRG")
                    gtc = {"i": 0}

                    def next_gt():
                        i = 4 + gtc["i"] % 4
                        gtc["i"] += 1
                        return GT[i], tGT[i]

                    def rev(ap, n):
                        return bass.AP(ap.tensor, ap.offset + (n - 1), [list(ap.ap[0]), [-1, n]])

                    wbd0, twbd0 = take(P["bd"])
                    BDW = sb("BDW", [128, 1, 2048], BF16, a4)
                    twbd = T("BDW")
                    fw.op("dve", lambda e: e.tensor_copy(out=BDW[:], in_=wbd0), reads=[twbd0], writes=[twbd])
                    wbd = BDW
                    for c in range(4):
                        w, tw = take(P["lru"][c])
                        fw.op("pool", lambda e: e.memset(UX[:], 0.0), writes=[tUX])
                        for t in range(NT):
                            ps, tps = next_dps()
                            dense(w, tw, 0, 8, hrhs, hrt, t, ps, tps)
                            if t < 2:
                                dstv = UX[:, t * 520:(t + 1) * 520].rearrange("p (s n) -> p s n", s=2)[:, :, 2:258]
                                srcv = ps[:].rearrange("p (s n) -> p s n", s=2)
                            else:
                                dstv = UX[:, 1042:1554]
                                srcv = ps[:]
                            fw.op("act", lambda e, dstv=dstv, srcv=srcv: e.activation(out=dstv, in_=srcv, func=AF.Identity),
                                  reads=[tps], writes=[tUX])
                        ps, tps = next_dps()
                        fw.mm([lambda e, k=k: e.matmul(ps[:, 0:16], lhsT=w[:, k, 0:128], rhs=HH[:, k, :], start=(k == 0), stop=(k == 7))
                               for k in range(8)], reads=[tw, tHH], writes=[tps])
                        fw.op("dve", lambda e, ps=ps: e.tensor_scalar(out=UX[:, 1040:1042], in0=ps[:, 6:8], scalar1=FLG[:, 0:1], scalar2=None,
                                                                     op0=ALU.mult), reads=[tps, tC], writes=[tUX])
                        fw.op("dve", lambda e, ps=ps: e.tensor_scalar(out=UX[:, 1554:1555], in0=ps[:, 8:9], scalar1=FLG[:, 1:2], scalar2=None,
                                                                     op0=ALU.mult), reads=[tps, tC], writes=[tUX])
                        for t in (2, 0, 1):
                            def tap(j, t=t):
                                if t < 2:
                                    return UX[:, t * 520:(t + 1) * 520].rearrange("p (s n) -> p s n", s=2)[:, :, j:j + 256]
                                return UX[:, 1040 + j:1040 + j + 512]
                            xo = XC[:, tsl(t)].rearrange("p (s n) -> p s n", s=2) if t < 2 else XC[:, tsl(t)]
                            fw.op("act", lambda e, xo=xo, tap=tap: e.activation(out=xo, in_=tap(0), func=AF.Identity, scale=PV[:, l, 68 + c:69 + c],
                                                                               bias=PV[:, l, 84 + c:85 + c]),
                                  reads=[tUX, tC], writes=[tXC[t]])
                            for j in range(1, 4):
                                fw.op("dve", lambda e, xo=xo, tap=tap, j=j: e.scalar_tensor_tensor(
                                    out=xo, in0=tap(j), scalar=PV[:, l, 68 + j * 4 + c:69 + j * 4 + c], in1=xo, op0=ALU.mult, op1=ALU.add),
                                    reads=[tUX, tC, tXC[t]], writes=[tXC[t]])
                            fw.op("dve", lambda e, t=t: e.tensor_copy(out=XCB[:, tsl(t)], in_=XC[:, tsl(t)]), reads=[tXC[t]], writes=[tXCB[t]])
                        AT = {}

                        def gate_scan(e_, t):
                            cl = CL[:, e_ * 4 + c:e_ * 4 + c + 1]
                            cl2 = CL2[:, e_ * 4 + c:e_ * 4 + c + 1]
                            if True:
                                ma = (e_ * 2 + 0) * 4 + c
                                mx = (e_ * 2 + 1) * 4 + c
                                idx = e_ * 4 + c
                                ps, tps = next_dps()
                                fw.mm([lambda e, ps=ps, t=t, ma=ma: e.matmul(ps[:], lhsT=wbd[:, 0, ma * 128:(ma + 1) * 128], rhs=XCB[:, tsl(t)],
                                                                              start=True, stop=True)], reads=[twbd, tXCB[t]], writes=[tps])
                                ps2, tps2 = next_dps()
                                fw.mm([lambda e, ps2=ps2, t=t, mx=mx: e.matmul(ps2[:], lhsT=wbd[:, 0, mx * 128:(mx + 1) * 128], rhs=XCB[:, tsl(t)],
                                                                                start=True, stop=True)], reads=[twbd, tXCB[t]], writes=[tps2])
                                rg, trg = next_gt()
                                if t == 2:
                                    aa, taa = GT[e_ * 2], tGT[e_ * 2]
                                    bb_, tbb = GT[e_ * 2 + 1], tGT[e_ * 2 + 1]
                                    fw.op("act", lambda e, rg=rg, ps=ps: e.activation(out=rg[:], in_=ps[:], func=AF.Tanh, scale=0.5,
                                                                                    bias=HBV[:, idx:idx + 1], accum_out=SRG[:, e_:e_ + 1]),
                                          reads=[tps, tAB], writes=[trg, tSRG])
                                else:
                                    aa, taa = next_gt()
                                    bb_, tbb = next_gt()
                                    fw.op("act", lambda e, rg=rg, ps=ps: e.activation(out=rg[:], in_=ps[:], func=AF.Tanh, scale=0.5,
                                                                                    bias=HBV[:, idx:idx + 1]),
                                          reads=[tps, tAB], writes=[trg])
                                fw.op("act", lambda e, bb_=bb_, ps2=ps2: e.activation(out=bb_[:], in_=ps2[:], func=AF.Tanh, scale=0.5,
                                                                                     bias=HBV[:, 8 + idx:9 + idx]),
                                      reads=[tps2, tAB], writes=[tbb])
                                fw.op("act", lambda e, rg=rg, aa=aa: e.activation(out=aa[:], in_=rg[:], func=AF.Exp, scale=CLH[:, idx:idx + 1],
                                                                                bias=CLH[:, idx:idx + 1]),
                                      reads=[trg, tAB], writes=[taa])
                                fw.op("dve", lambda e, rg=rg, aa=aa: e.tensor_tensor(out=rg[:], in0=aa[:], in1=aa[:], op=ALU.mult),
                                      reads=[taa], writes=[trg])
                                fw.op("act", lambda e, rg=rg: e.activation(out=rg[:], in_=rg[:], func=AF.Sqrt, scale=-1.0, bias=1.0),
                                      reads=[trg], writes=[trg])
                                fw.op("dve", lambda e, bb_=bb_, t=t: e.scalar_tensor_tensor(out=bb_[:], in0=bb_[:], scalar=1.0, in1=XC[:, tsl(t)],
                                                                                            op0=ALU.add, op1=ALU.mult),
                                      reads=[tbb, tXC[t]], writes=[tbb])
                                fw.op("dve", lambda e, bb_=bb_, rg=rg: e.scalar_tensor_tensor(out=bb_[:], in0=bb_[:], scalar=0.5, in1=rg[:],
                                                                                              op0=ALU.mult, op1=ALU.mult),
                                      reads=[tbb, trg], writes=[tbb])
                                if t < 2:
                                    for s2 in range(2):
                                        sl = slice(s2 * 256, (s2 + 1) * 256)
                                        o_ = HS[e_][:, t * 512 + s2 * 256:t * 512 + (s2 + 1) * 256]
                                        if e_ == 0:
                                            fw.op("dve", lambda e, o_=o_, aa=aa, bb_=bb_, sl=sl: e.tensor_tensor_scan(
                                                out=o_, data0=aa[:, sl], data1=bb_[:, sl], initial=0.0, op0=ALU.mult, op1=ALU.add),
                                                reads=[taa, tbb], writes=[tHS[e_][t]])
                                        else:
                                            fw.op("dve", lambda e, o_=o_, aa=aa, bb_=bb_, sl=sl: e.tensor_tensor_scan(
                                                out=rev(o_, 256), data0=rev(aa[:, sl], 256), data1=rev(bb_[:, sl], 256), initial=0.0,
                                                op0=ALU.mult, op1=ALU.add), reads=[taa, tbb], writes=[tHS[e_][t]])
                                else:
                                    AT[e_] = (aa, taa, bb_, tbb)
                                    o_ = HS[e_][:, 1024:1536]
                                    if e_ == 0:
                                        fw.op("dve", lambda e, o_=o_, aa=aa, bb_=bb_: e.tensor_tensor_scan(
                                            out=o_, data0=aa[:], data1=bb_[:], initial=0.0, op0=ALU.mult, op1=ALU.add),
                                            reads=[taa, tbb], writes=[tHS[e_][t]])
                                        fw.op("dve", lambda e: e.tensor_copy(out=CAR[:, 1:2], in_=HS[0][:, 1535:1536]), reads=[tHS[0][2]], writes=[tCAR])
                                    else:
                                        fw.op("dve", lambda e, o_=o_, aa=aa, bb_=bb_: e.tensor_tensor_scan(
                                            out=rev(o_, 512), data0=rev(aa[:], 512), data1=rev(bb_[:], 512), initial=0.0,
                                            op0=ALU.mult, op1=ALU.add), reads=[taa, tbb], writes=[tHS[e_][t]])
                                        fw.op("dve", lambda e: e.tensor_copy(out=CAR[:, 3:4], in_=HS[1][:, 1024:1025]), reads=[tHS[1][2]], writes=[tCAR])
                                    fw.op("act", lambda e, e_=e_: e.activation(out=CAR[:, 2 * e_:2 * e_ + 1], in_=SRG[:, e_:e_ + 1], func=AF.Exp,
                                                                               scale=CLH[:, e_ * 4 + c:e_ * 4 + c + 1], bias=CL256[:, e_ * 4 + c:e_ * 4 + c + 1]),
                                          reads=[tSRG, tAB], writes=[tCAR])
                        gate_scan(0, 2)
                        gate_scan(1, 2)
                        for t in range(NT):
                            ps, tps = next_dps()
                            dense(w, tw, 128, 8, hrhs, hrt, t, ps, tps)
                            fw.op("act", lambda e, t=t, ps=ps: e.activation(out=GY[:, tsl(t)], in_=ps[:], func=AF.Gelu_apprx_tanh),
                                  reads=[tps], writes=[tGY[t]])
                        cb = c % 2
                        fw.dma("pool", cs_in[cb], CAR[:], reads=[tCAR], writes=[tCSI[cb]])
                        fw.collective("AllGather", GROUPS, cs_in[cb], cs_out[cb].rearrange("r p n -> (r p) n"), reads=[tCSI[cb]], writes=[tCSO[cb]])
                        for e_ in range(2):
                            for t in range(2):
                                gate_scan(e_, t)
                        for t in range(2):
                            tm, ttm = next_gt()
                            fw.op("pool", lambda e, tm=tm, t=t: e.tensor_tensor(out=tm[:], in0=HS[0][:, tsl(t)], in1=HS[1][:, tsl(t)], op=ALU.add),
                                  reads=[tHS[0][t], tHS[1][t]], writes=[ttm])
                            fw.op("dve", lambda e, tm=tm, t=t: e.tensor_tensor(out=YB[:, 2, c, tsl(t)], in0=tm[:], in1=GY[:, tsl(t)], op=ALU.mult),
                                  reads=[ttm, tGY[t]], writes=[tYB[2][c][t]])
                        stv = STO[:].rearrange("p (s x) -> p s x", s=4)
                        fw.op("act", lambda e: e.activation(out=stv[:, :, c], in_=HS[0][:, 0:1024].rearrange("p (s n) -> p s n", s=4)[:, :, 255],
                                                            func=AF.Identity), reads=[tHS[0][0], tHS[0][1]], writes=[tSTO])
                        fw.op("act", lambda e: e.activation(out=stv[:, :, 4 + c], in_=HS[1][:, 0:1024].rearrange("p (s n) -> p s n", s=4)[:, :, 0],
                                                            func=AF.Identity), reads=[tHS[1][0], tHS[1][1]], writes=[tSTO])
                        fw.dma("pool", CARG[:], cs_out[cb].rearrange("r p n -> p r n"), reads=[tCSO[cb]], writes=[tCARG])
                        fw.op("dve", lambda e: e.tensor_copy(out=HINS[:, 0:1], in_=ST0[:, l * 8 + c:l * 8 + c + 1]), reads=[tC], writes=[tHINS])
                        for r in range(3):
                            fw.op("dve", lambda e, r=r: e.scalar_tensor_tensor(out=HINS[:, r + 1:r + 2], in0=HINS[:, r:r + 1], scalar=CARG[:, r, 0:1],
                                                                               in1=CARG[:, r, 1:2], op0=ALU.mult, op1=ALU.add),
                                  reads=[tHINS, tCARG], writes=[tHINS])
                        fw.op("dve", lambda e: e.tensor_copy(out=HINS[:, 7:8], in_=ST0[:, l * 8 + 4 + c:l * 8 + 4 + c + 1]), reads=[tC], writes=[tHINS])
                        for r in (3, 2, 1):
                            fw.op("dve", lambda e, r=r: e.scalar_tensor_tensor(out=HINS[:, 4 + r - 1:4 + r], in0=HINS[:, 4 + r:4 + r + 1],
                                                                               scalar=CARG[:, r, 2:3], in1=CARG[:, r, 3:4], op0=ALU.mult, op1=ALU.add),
                                  reads=[tHINS, tCARG], writes=[tHINS])
                        for e_ in range(2):
                            fw.op("dve", lambda e, e_=e_: e.tensor_scalar(out=HINS[:, 8 + e_:9 + e_], in0=HINS[:, 4 * e_:4 * e_ + 1], scalar1=SEL[:, 8:9],
                                                                         scalar2=None, op0=ALU.mult), reads=[tHINS, tC], writes=[tHINS])
                            for r in range(1, 4):
                                fw.op("dve", lambda e, e_=e_, r=r: e.scalar_tensor_tensor(
                                    out=HINS[:, 8 + e_:9 + e_], in0=HINS[:, 4 * e_ + r:4 * e_ + r + 1], scalar=SEL[:, 8 + r:9 + r],
                                    in1=HINS[:, 8 + e_:9 + e_], op0=ALU.mult, op1=ALU.add), reads=[tHINS, tC], writes=[tHINS])
                        for e_ in range(2):
                            aa, taa, bb_, tbb = AT[e_]
                            o_ = HS[e_][:, 1024:1536]
                            if e_ == 0:
                                fw.op("dve", lambda e, o_=o_, aa=aa, bb_=bb_: e.tensor_tensor_scan(
                                    out=o_, data0=aa[:], data1=bb_[:], initial=HINS[:, 8:9], op0=ALU.mult, op1=ALU.add),
                                    reads=[taa, tbb, tHINS], writes=[tHS[0][2]])
                            else:
                                fw.op("dve", lambda e, o_=o_, aa=aa, bb_=bb_: e.tensor_tensor_scan(
                                    out=rev(o_, 512), data0=rev(aa[:], 512), data1=rev(bb_[:], 512), initial=HINS[:, 9:10],
                                    op0=ALU.mult, op1=ALU.add), reads=[taa, tbb, tHINS], writes=[tHS[1][2]])
                        tm, ttm = next_gt()
                        fw.op("pool", lambda e, tm=tm: e.tensor_tensor(out=tm[:], in0=HS[0][:, 1024:1536], in1=HS[1][:, 1024:1536], op=ALU.add),
                              reads=[tHS[0][2], tHS[1][2]], writes=[ttm])
                        fw.op("dve", lambda e, tm=tm: e.tensor_tensor(out=YB[:, 2, c, 1024:1536], in0=tm[:], in1=GY[:, 1024:1536], op=ALU.mult),
                              reads=[ttm, tGY[2]], writes=[tYB[2][c][2]])
                        chk(f"lru{c}")
                    fw.dma("act", st_out[l], STO[:], reads=[tSTO])
                    fw.release([tUX] + tGY + tXC + tXCB + tHS[0] + tHS[1] + tGT + [tCAR, tCARG, tHINS, tSRG, twbd])
                if debug and l == 0:
                    dbg_list.append(("YB2", YB[:], [tYB[n][c][t] for n in range(3) for c in range(4) for t in range(NT)]))
                chk("lru")

                with Arena() as a5:
                    MG = sb("MG", [128, 8, TT], BF16, a5)
                    MF = sb("MF", [128, 2, TT], F32, a5)
                    GG = [sb(f"GG{j}", [128, 512], F32, a5) for j in range(2)]
                    GM = [sb(f"GM{j}", [128, 512], F32, a5) for j in range(2)]
                    tMG = [[T(f"MG{k}_{t}") for t in range(NT)] for k in range(8)]
                    tMF = [[T(f"MF{k}_{t}") for t in range(NT)] for k in range(2)]
                    tGG, tGM = [T("GG0"), T("GG1")], [T("GM0"), T("GM1")]
                    gi = 0
                    bi_ = 0
                    for jp in range(4):
                        for n in range(3):
                            bg, bb2 = P["gate"][bi_]
                            bi_ += 1
                            wg, twg = take(bg)
                            wb, twb = take(bb2)
                            for jj in range(2):
                                j = jp * 2 + jj
                                for t in range(NT):
                                    psg, tpsg = next_dps()
                                    dense(wg, twg, jj * 128, 8, hrhs, hrt, t, psg, tpsg)
                                    gq = gi % 2
                                    gi += 1
                                    fw.op("act", lambda e, gq=gq, psg=psg: e.activation(out=GG[gq][:], in_=psg[:], func=AF.Sigmoid),
                                          reads=[tpsg], writes=[tGG[gq]])
                                    psp, tpsp = next_dps()
                                    dense(wb, twb, jj * 128, 4, lambda k, t, n=n: YB[:, n, k, tsl(t)], lambda k, t, n=n: tYB[n][k][t], t, psp, tpsp)
                                    if n == 0:
                                        fw.op("dve", lambda e, gq=gq, psp=psp, jj=jj, t=t: e.tensor_tensor(out=MF[:, jj, tsl(t)], in0=GG[gq][:], in1=psp[:], op=ALU.mult),
                                              reads=[tGG[gq], tpsp], writes=[tMF[jj][t]])
                                    else:
                                        fw.op("dve", lambda e, gq=gq, psp=psp: e.tensor_tensor(out=GM[gq][:], in0=GG[gq][:], in1=psp[:], op=ALU.mult),
                                              reads=[tGG[gq], tpsp], writes=[tGM[gq]])
                                        if n == 1:
                                            fw.op("pool", lambda e, gq=gq, jj=jj, t=t: e.tensor_tensor(out=MF[:, jj, tsl(t)], in0=MF[:, jj, tsl(t)], in1=GM[gq][:], op=ALU.add),
                                                  reads=[tGM[gq], tMF[jj][t]], writes=[tMF[jj][t]])
                                        else:
                                            fw.op("pool", lambda e, gq=gq, jj=jj, t=t, j=j: e.tensor_tensor(out=MG[:, j, tsl(t)], in0=MF[:, jj, tsl(t)], in1=GM[gq][:], op=ALU.add),
                                                  reads=[tGM[gq], tMF[jj][t]], writes=[tMG[j][t]])
                    chk("merge")
                    if debug and l == 0:
                        pass
                    for mp in range(4):
                        w, tw = take(P["out"][mp])
                        for jj in range(2):
                            j = mp * 2 + jj
                            for t in range(NT):
                                g = grp_of_tile[t]
                                ps, tps = next_dps()
                                dense(w, tw, jj * 128, 8, lambda k, t: MG[:, k, tsl(t)], lambda k, t: tMG[k][t], t, ps, tps)
                                fw.op("dve", lambda e, ps=ps, j=j, t=t, g=g: e.scalar_tensor_tensor(
                                    out=X[:, j, tsl(t)], in0=ps[:], scalar=mv[:, 16 + j, g:g + 1], in1=X[:, j, tsl(t)], op0=ALU.mult, op1=ALU.add),
                                    reads=[tps, tmv, tX[j][t]], writes=[tX[j][t]])
                    fw.release([x for r_ in tMG for x in r_] + [x for r_ in tMF for x in r_] + tGG + tGM)
                chk("wout")
                if debug and l == 0:
                    dbg_list.append(("X1", X[:], [tX[k][t] for k in range(8) for t in range(NT)]))
                    if stop_after == "wout":
                        raise _Stop()

                with Arena() as a1:
                    SQ = [sb(f"SQ{j}", [128, 512], BF16, a1) for j in range(4)]
                    RS = [sb(f"RS{j}", [128, 512], F32, a1) for j in range(3)]
                    TMP = [sb(f"TMP{j}", [128, 512], F32, a1) for j in range(4)]
                    tSQ = [T(f"SQ{j}") for j in range(4)]
                    tRS = [T(f"RS{j}") for j in range(3)]
                    tTMP = [T(f"TMP{j}") for j in range(4)]
                    emit_norm(l, 1, (SQ, tSQ, RS, tRS, TMP, tTMP))
                    fw.release(tSQ + tRS + tTMP)

                with Arena() as a6:
                    AV = sb("AV", [128, 12, TT], BF16, a6)
                    SG = [sb(f"SG{j}", [128, 512], F32, a6) for j in range(2)]
                    tAV = [[T(f"AV{k}_{t}") for t in range(NT)] for k in range(12)]
                    tSG = [T("SG0"), T("SG1")]
                    si = 0
                    for (kind, idx, bid) in P["ffseq"]:
                        if kind == "ffin":
                            hf_, i = idx
                            w, tw = take(bid)
                            for t in range(NT):
                                psg, tpsg = next_dps()
                                dense(w, tw, 0, 8, hrhs, hrt, t, psg, tpsg)
                                sq_ = si % 2
                                si += 1
                                fw.op("act", lambda e, sq_=sq_, psg=psg: e.activation(out=SG[sq_][:], in_=psg[:], func=AF.Silu),
                                      reads=[tpsg], writes=[tSG[sq_]])
                                psu, tpsu = next_dps()
                                dense(w, tw, 128, 8, hrhs, hrt, t, psu, tpsu)
                                fw.op("dve", lambda e, sq_=sq_, psu=psu, i=i, t=t: e.tensor_tensor(out=AV[:, i, tsl(t)], in0=SG[sq_][:], in1=psu[:], op=ALU.mult),
                                      reads=[tSG[sq_], tpsu], writes=[tAV[i][t]])
                        elif kind == "mod":
                            assert bid == cons["i"]
                            cons["i"] += 1
                            emit_mod_block(l + 1, idx, bid)
                        else:
                            hf_, j, nk = idx
                            w, tw = take(bid)
                            for t in range(NT):
                                g = grp_of_tile[t]
                                ps, tps = next_dps()
                                fw.mm([lambda e, k=k, ps=ps, t=t: e.matmul(ps[:], lhsT=w[:, k, :], rhs=AV[:, k, tsl(t)], start=(k == 0), stop=(k == nk - 1))
                                       for k in range(nk)], reads=[tw] + [tAV[k][t] for k in range(nk)], writes=[tps])
                                fw.op("dve", lambda e, ps=ps, j=j, t=t, g=g: e.scalar_tensor_tensor(
                                    out=X[:, j, tsl(t)], in0=ps[:], scalar=mv[:, 40 + j, g:g + 1], in1=X[:, j, tsl(t)], op0=ALU.mult, op1=ALU.add),
                                    reads=[tps, tmv, tX[j][t]], writes=[tX[j][t]])
                    if l + 1 < depth:
                        finish_mod(l + 1)
                    fw.release([x for r_ in tAV for x in r_] + tSG)
                if debug and l == 0:
                    dbg_list.append(("X2", X[:], [tX[k][t] for k in range(8) for t in range(NT)]))
                chk(f"layer{l}")

                if l + 1 < depth:
                    with Arena() as a7:
                        XHO = sb("XHO", [128, 8, 16], F32, a7)
                        tXHO = T("XHO")
                        xrd = [tX[k][2] for k in range(8)]
                        fw.op("pool", lambda e: e.tensor_copy(out=XHO[:, :, 0:8], in_=X[:, :, 1024:1032]), reads=xrd, writes=[tXHO])
                        fw.op("pool", lambda e: e.tensor_copy(out=XHO[:, :, 8:16], in_=X[:, :, 1528:1536]), reads=xrd, writes=[tXHO])
                        fw.dma("pool", cx_in, XHO[:].rearrange("p k t -> p (k t)"), reads=[tXHO], writes=[tCXI])
                        fw.collective("AllGather", GROUPS, cx_in, cx_out.rearrange("r p n -> (r p) n"), reads=[tCXI], writes=[tCXO])
                        fw.release([tXHO])

            with Arena() as a8:
                SQ = [sb(f"SQ{j}", [128, 512], BF16, a8) for j in range(2)]
                RS = sb("RS", [128, 512], F32, a8)
                YO = [sb(f"YO{j}", [128, 512], F32, a8) for j in range(4)]
                tSQ, tRS, tYO = [T("SQ0"), T("SQ1")], T("RS"), [T(f"YO{j}") for j in range(4)]
                yv = yT.rearrange("(k p) t -> p k t", p=128)
                yi = 0
                for t in range(NT):
                    ps, tps = next_dps()
                    for k in range(8):
                        j = k % 2
                        fw.op("act", lambda e, k=k, j=j, t=t: e.activation(out=SQ[j][:], in_=X[:, k, tsl(t)], func=AF.Square),
                              reads=[tX[k][t]], writes=[tSQ[j]])
                        fw.mm([lambda e, k=k, j=j, ps=ps: e.matmul(ps[:], lhsT=ONES[:], rhs=SQ[j][:], start=(k == 0), stop=(k == 7))],
                              reads=[tSQ[j], tC], writes=[tps])
                    fw.op("act", lambda e, ps=ps: e.activation(out=RS[:], in_=ps[:], func=AF.Ln, bias=EPSC[:, 0:1]), reads=[tps, tC], writes=[tRS])
                    fw.op("act", lambda e: e.activation(out=RS[:], in_=RS[:], func=AF.Exp, scale=-0.5), reads=[tRS], writes=[tRS])
                    for k in range(8):
                        j = yi % 4
                        yi += 1
                        fw.op("dve", lambda e, k=k, j=j, t=t: e.scalar_tensor_tensor(out=YO[j][:], in0=X[:, k, tsl(t)], scalar=GF[:, k:k + 1], in1=RS[:],
                                                                                     op0=ALU.mult, op1=ALU.mult),
                              reads=[tX[k][t], tRS, tC], writes=[tYO[j]])
                        fw.dma("act", yv[:, k, tsl(t)], YO[j][:], reads=[tYO[j]])
                fw.release(tSQ + [tRS] + tYO)

        except _Stop:
            pass
        for (nm, ap_sb, tl) in dbg_list:
            dd = nc.dram_tensor("dbg_" + nm, list(ap_sb.shape), ap_sb.dtype, kind="ExternalOutput").ap()
            fw.dma("sp", dd, ap_sb, reads=tl)
        eng = fw.engs["sp"]
        for s_ in fw.sems:
            if s_.count > 0:
                eng.h.wait_ge(s_.h, s_.count)
    return nc


def _chunked(v):
    v = np.asarray(v, np.float32)
    n = v.shape[-1] // 128
    return np.ascontiguousarray(np.moveaxis(v.reshape(v.shape[:-1] + (n, 128)), -1, 0))


def prepare_inputs(inp, depth=DEPTH):
    f32 = np.float32
    LAYERED = ("w_mod", "b_mod", "g_norm1", "g_norm2", "w_in", "pool_w", "pool_scale", "na_rpb", "lru_conv_w", "lru_conv_b",
               "lru_wa", "lru_ba", "lru_wx", "lru_bx", "lru_lambda", "w_branch", "w_out", "w_ff_in", "w_ff_out")

    def g(k):
        a = np.asarray(inp[k], f32)
        if k in LAYERED:
            a = a[:depth]
        elif k in ("cache_k", "cache_v", "state_lru"):
            a = a[:, :depth]
        return a
    DEPTH = depth
    x_prompt, x_sample = g("x_prompt"), g("x_sample")
    cache_k, cache_v, state_lru = g("cache_k"), g("cache_v"), g("state_lru")
    c, c_ctx = g("c"), g("c_ctx")
    pv = np.zeros((128, DEPTH, NV), f32)
    pv[:, :, 0:8] = _chunked(g("g_norm1"))
    pv[:, :, 8:16] = _chunked(g("g_norm2"))
    pv[:, :, 16:64] = _chunked(g("b_mod"))
    pv[:, :, 64:68] = _chunked(g("pool_scale"))
    cw = _chunked(g("lru_conv_w"))
    pv[:, :, 68:84] = cw.reshape(128, DEPTH, 16)
    pv[:, :, 84:88] = _chunked(g("lru_conv_b"))
    pv[:, :, 88:96] = _chunked(g("lru_ba")).reshape(128, DEPTH, 8)
    pv[:, :, 96:104] = _chunked(g("lru_bx")).reshape(128, DEPTH, 8)
    pv[:, :, 104:112] = _chunked(g("lru_lambda")).reshape(128, DEPTH, 8)
    gf = _chunked(g("g_final"))
    poolw = np.ascontiguousarray(g("pool_w").transpose(0, 2, 1, 3).reshape(DEPTH, 128, 512))
    wa, wx = g("lru_wa"), g("lru_wx")
    bd = np.zeros((DEPTH, 128, 16, 128), f32)
    for e in range(2):
        for which, w in enumerate((wa, wx)):
            for ch in range(4):
                mat = (e * 2 + which) * 4 + ch
                for hf in range(2):
                    bd[:, hf * 64:(hf + 1) * 64, mat, hf * 64:(hf + 1) * 64] = w[:, e, ch * 2 + hf]
    bd = bd.reshape(DEPTH, 128, 2048)
    rpb = g("na_rpb")
    wk = np.arange(64)[:, None]
    wq = np.arange(64)[None, :]
    dc = np.clip(wk - wq, -15, 15) + 15
    col0 = np.clip(np.arange(64) - 8, 0, 48)
    col_ok = (wk >= col0[None, :]) & (wk < col0[None, :] + 16)
    G = np.empty((DEPTH, 64, 8, 15, 64), f32)
    for ep in range(15):
        gath = rpb[:, :, 14 - ep][:, :, dc]
        gath = np.where(col_ok[None, None], gath, f32(NEG))
        G[:, :, :, ep, :] = gath.transpose(0, 2, 1, 3)
    slot16 = (np.arange(1024)[None, :] // 64 == np.arange(16)[:, None]).astype(f32)
    slot = np.zeros((128, 1024), f32)
    slot[0:16] = slot16
    slot[64:80] = slot16
    icp = np.zeros((128, 4, 2, 8), f32)
    for gi in range(4):
        half = 2 ** gi
        for i in range(8):
            t = i
            icp[:, gi, 0, i] = 1.0 / (min(t + half, 256) - max(t - half, 0))
            t = 248 + i
            icp[:, gi, 1, i] = 1.0 / (min(t + half, 256) - max(t - half, 0))
    shared = dict(pvec=pv, gfin=gf, w_mod=g("w_mod"), w_in=g("w_in"), w_branch=g("w_branch"), w_out=g("w_out"),
                  w_ff_in=g("w_ff_in"), w_ff_out=g("w_ff_out"), pool_w=poolw, lru_bd=bd, rpbG=G, slotoh=slot, icp=icp)
    in_maps = []
    for core in range(NCORES):
        b, j = core // 4, core % 4
        xp = x_prompt[4 * core:4 * core + 4].reshape(TP, D)
        xs = x_sample[b, j * TS:(j + 1) * TS]
        xT = np.ascontiguousarray(np.concatenate([xp, xs], 0).T)
        halo = np.zeros((16, D), f32)
        if j > 0:
            halo[0:8] = x_sample[b, j * TS - 8:j * TS]
        if j < 3:
            halo[8:16] = x_sample[b, (j + 1) * TS:(j + 1) * TS + 8]
        xh0 = np.ascontiguousarray(halo.T.reshape(8, 128, 16).transpose(1, 0, 2))
        ckT = np.ascontiguousarray(cache_k[b].reshape(DEPTH, 512, 512).transpose(0, 2, 1))
        cvv = np.ascontiguousarray(cache_v[b].reshape(DEPTH, 512, 512))
        st0 = _chunked(state_lru[b]).reshape(128, DEPTH * 8)
        cvec = np.stack([_chunked(c_ctx), _chunked(c[b])], -1)
        R0 = 8 * j
        rowb16 = np.zeros((16, 512), f32)
        for i in range(8):
            r = R0 + i
            row0 = min(max(r - 4, 0), 24)
            for m in range(16):
                rr = R0 - 4 + m
                ok = (row0 <= rr < row0 + 8) and (0 <= rr < 32)
                if not ok:
                    rowb16[m, i * 64:(i + 1) * 64] = -240000.0
        rowb = np.zeros((128, 512), f32)
        rowb[0:16] = rowb16
        rowb[64:80] = rowb16
        sel = np.zeros((128, 12), f32)
        if j > 0:
            sel[:, j - 1] = 1
        if j < 3:
            sel[:, 4 + j + 1] = 1
        sel[:, 8 + j] = 1
        flags = np.zeros((128, 2), f32)
        flags[:, 0] = float(j > 0)
        flags[:, 1] = float(j < 3)
        ics = np.zeros((128, 4, 2, 8), f32)
        for gi in range(4):
            half = 2 ** gi
            for i in range(8):
                t = j * TS + i
                ics[:, gi, 0, i] = 1.0 / (min(t + half, 2048) - max(t - half, 0))
                t = j * TS + 504 + i
                ics[:, gi, 1, i] = 1.0 / (min(t + half, 2048) - max(t - half, 0))
        m = dict(shared)
        m.update(xT=xT, xh0=xh0, ckT=ckT, cv=cvv, st0=np.ascontiguousarray(st0), cvec=np.ascontiguousarray(cvec),
                 rowb=rowb, sel=sel, flags=flags, ics=ics)
        in_maps.append(m)
    return in_maps


def assemble_outputs(results):
    f32 = np.float32
    y_prompt = np.empty((32, 256, D), f32)
    y_sample = np.empty((2, 2048, D), f32)
    nk = np.empty((32, DEPTH, 256, 8, 64), f32)
    nv = np.empty((32, DEPTH, 256, 8, 64), f32)
    ns = np.empty((32, DEPTH, 2, 512), f32)
    for core in range(NCORES):
        r = results[core]
        b, j = core // 4, core % 4
        yT = np.asarray(r["yT"])
        for s in range(4):
            y_prompt[4 * core + s] = yT[:, s * 256:(s + 1) * 256].T
        y_sample[b, j * TS:(j + 1) * TS] = yT[:, TP:TT].T
        kT = np.asarray(r["kT_out"])
        vv = np.asarray(r["v_out"])
        so = np.asarray(r["st_out"])
        for s in range(4):
            nk[4 * core + s] = kT[:, :, s * 256:(s + 1) * 256].transpose(0, 2, 1).reshape(DEPTH, 256, 8, 64)
            nv[4 * core + s] = vv[:, s * 256:(s + 1) * 256, :].reshape(DEPTH, 256, 8, 64)
            blk = so[:, :, s * 8:(s + 1) * 8].reshape(DEPTH, 128, 2, 4)
            ns[4 * core + s] = blk.transpose(0, 2, 3, 1).reshape(DEPTH, 2, 512)
    return y_prompt, y_sample, nk, nv, ns


_NC_CACHE = {}


def kernel(**inputs):
    if "nc" not in _NC_CACHE:
        _NC_CACHE["nc"] = build_program()
    nc = _NC_CACHE["nc"]
    in_maps = prepare_inputs(inputs)
    res = run_bass_kernel_spmd(nc, in_maps, core_ids=list(range(NCORES)))
    return assemble_outputs(res.results)
```

```python
import numpy as np
from contextlib import ExitStack
import concourse.bass as bass
import concourse.mybir as mybir
from concourse.bass_utils import run_bass_kernel_spmd

F32 = mybir.dt.float32
BF16 = mybir.dt.bfloat16
AF = mybir.ActivationFunctionType
ALU = mybir.AluOpType

D = 1024
DEPTH = 4
NCORES = 8
TP = 1024
TS = 512
TT = TP + TS
NT = 3
INW = 6144
FFH = 2816
EPS = 1e-6
SCALE = 0.125
NV = 112
NEG = -30000.0
GROUPS = [[0, 1, 2, 3], [4, 5, 6, 7]]


class _Stop(Exception):
    pass


class Arena:
    def __enter__(self):
        self.es = ExitStack()
        return self.es

    def __exit__(self, et, ev, tb):
        if et is None or et is _Stop:
            self.es.close()
        return False


class Sem:
    def __init__(self, handle, name):
        self.h = handle
        self.name = name
        self.count = 0


class T:
    __slots__ = ("name", "w", "rs", "dsem_w", "dsem_r", "excl")

    def __init__(self, name, excl=False):
        self.name = name
        self.excl = excl
        self.w = None
        self.rs = []
        self.dsem_w = None
        self.dsem_r = None


class Eng:
    def __init__(self, name, h, sem):
        self.name = name
        self.h = h
        self.sem = sem
        self.waited = {}
        self.nwaits = 0
        self.nops = 0


class FW:
    def __init__(self, nc, stack, same_engine_sync=True):
        self.nc = nc
        self.stack = stack
        self.same_engine_sync = same_engine_sync
        self.engs = {}
        self.nsem = 0
        self.sems = []
        self.free_dma = []
        for name, h in (("pe", nc.tensor), ("act", nc.scalar), ("dve", nc.vector),
                        ("pool", nc.gpsimd), ("sp", nc.sync)):
            self.engs[name] = Eng(name, h, self.new_sem("e_" + name))

    def new_sem(self, name):
        self.nsem += 1
        sm = Sem(self.stack.enter_context(self.nc.semaphore(name)), name)
        self.sems.append(sm)
        return sm

    def dma_sem(self):
        if self.free_dma:
            return self.free_dma.pop()
        return self.new_sem(f"dq{self.nsem}")

    def _needs(self, reads, writes):
        needs = {}
        for t in reads:
            if t.w is not None:
                s, v = t.w
                if needs.get(s, 0) < v:
                    needs[s] = v
            if t.excl:
                for (s, v) in t.rs:
                    if needs.get(s, 0) < v:
                        needs[s] = v
        for t in writes:
            if t.w is not None:
                s, v = t.w
                if needs.get(s, 0) < v:
                    needs[s] = v
            for (s, v) in t.rs:
                if needs.get(s, 0) < v:
                    needs[s] = v
        return needs

    def _emit_waits(self, eng, needs):
        for s, v in needs.items():
            if s is eng.sem and not self.same_engine_sync:
                continue
            if eng.waited.get(s, 0) >= v:
                continue
            eng.h.wait_ge(s.h, v)
            eng.waited[s] = v
            eng.nwaits += 1

    def _record(self, sv, reads, writes):
        for t in reads:
            t.rs.append(sv)
            if len(t.rs) > 24:
                m = {}
                for (s, v) in t.rs:
                    if m.get(s, 0) < v:
                        m[s] = v
                t.rs = list(m.items())
        for t in writes:
            t.w = sv
            t.rs = []

    def op(self, ename, fn, reads=(), writes=()):
        eng = self.engs[ename]
        self._emit_waits(eng, self._needs(reads, writes))
        ins = fn(eng.h)
        eng.sem.count += 1
        ins.then_inc(eng.sem.h, 1)
        eng.nops += 1
        self._record((eng.sem, eng.sem.count), reads, writes)
        return ins

    def mm(self, fns, reads=(), writes=()):
        eng = self.engs["pe"]
        self._emit_waits(eng, self._needs(reads, writes))
        ins = None
        for fn in fns:
            ins = fn(eng.h)
            eng.nops += 1
        eng.sem.count += 1
        ins.then_inc(eng.sem.h, 1)
        self._record((eng.sem, eng.sem.count), reads, writes)

    def dma(self, qname, out_ap, in_ap, reads=(), writes=(), sem_tile=None):
        eng = self.engs[qname]
        self._emit_waits(eng, self._needs(reads, writes))
        if sem_tile is None:
            sem_tile = writes[0] if writes else reads[0]
        if writes:
            if sem_tile.dsem_w is None:
                sem_tile.dsem_w = self.dma_sem()
            s = sem_tile.dsem_w
        else:
            if sem_tile.dsem_r is None:
                sem_tile.dsem_r = self.dma_sem()
            s = sem_tile.dsem_r
        ins = eng.h.dma_start(out=out_ap, in_=in_ap)
        s.count += 16
        ins.then_inc(s.h, 16)
        eng.nops += 1
        self._record((s, s.count), reads, writes)
        return ins

    def collective(self, kind, groups, in_ap, out_ap, reads=(), writes=()):
        eng = self.engs["pool"]
        self._emit_waits(eng, self._needs(reads, writes))
        sem_tile = writes[0]
        if sem_tile.dsem_w is None:
            sem_tile.dsem_w = self.new_sem("cc_" + sem_tile.name)
        s = sem_tile.dsem_w
        ins = eng.h.collective_compute(kind, ALU.bypass, replica_groups=groups, ins=[in_ap], outs=[out_ap])
        s.count += 1
        ins.then_inc(s.h, 1)
        self._record((s, s.count), reads, writes)

    def release(self, tiles, engines=("pe", "act", "dve", "pool", "sp")):
        needs = self._needs((), tiles)
        for e in engines:
            self._emit_waits(self.engs[e], needs)
        if len(engines) == 5:
            for t in tiles:
                for s_ in (t.dsem_w, t.dsem_r):
                    if s_ is not None and s_.name.startswith("dq"):
                        self.free_dma.append(s_)
                t.dsem_w = None
                t.dsem_r = None
                t.w = None
                t.rs = []

    def stats(self):
        return {k: (e.nops, e.nwaits) for k, e in self.engs.items()}, self.nsem


def build_program(depth=DEPTH, same_engine_sync=True, debug=False, stop_after=None):
    nc = bass.Bass("TRN2", target_bir_lowering=False)
    dbg_list = []

    def din(name, shape, dt=F32):
        return nc.dram_tensor(name, list(shape), dt, kind="ExternalInput").ap()

    def dout(name, shape, dt=F32):
        return nc.dram_tensor(name, list(shape), dt, kind="ExternalOutput").ap()

    def dint(name, shape, dt=F32):
        return nc.dram_tensor(name, list(shape), dt, kind="Internal").ap()

    xT = din("xT", [D, TT])
    xh0 = din("xh0", [128, 8, 16])
    ckT = din("ckT", [depth, 512, 512])
    cv = din("cv", [depth, 512, 512])
    st0 = din("st0", [128, depth * 2 * 4])
    cvec = din("cvec", [128, 8, 2])
    pvec = din("pvec", [128, depth, NV])
    gfin = din("gfin", [128, 8])
    w_mod = din("w_mod", [depth, D, INW])
    w_in = din("w_in", [depth, D, INW])
    w_branch = din("w_branch", [depth, 3, 512, D])
    w_out = din("w_out", [depth, D, D])
    w_ff_in = din("w_ff_in", [depth, D, 2 * FFH])
    w_ff_out = din("w_ff_out", [depth, FFH, D])
    pool_w = din("pool_w", [depth, 128, 4 * 128])
    lru_bd = din("lru_bd", [depth, 128, 16 * 128])
    rpbG = din("rpbG", [depth, 64, 8, 15, 64])
    slotoh = din("slotoh", [128, 1024])
    rowb = din("rowb", [128, 512])
    sel = din("sel", [128, 12])
    flags = din("flags", [128, 2])
    icp = din("icp", [128, 4, 2, 8])
    ics = din("ics", [128, 4, 2, 8])

    yT = dout("yT", [D, TT])
    kT_out = dout("kT_out", [depth, 512, TP])
    v_out = dout("v_out", [depth, TP, 512])
    st_out = dout("st_out", [depth, 128, 32])

    ckv_in = [dint(f"ckv_in{i}", [2, 128, 512], BF16) for i in range(2)]
    ckv_out = [dint(f"ckv_out{i}", [4, 2, 128, 512], BF16) for i in range(2)]
    cx_in = dint("cx_in", [128, 128])
    cx_out = dint("cx_out", [4, 128, 128])
    cs_in = [dint(f"cs_in{i}", [128, 4]) for i in range(2)]
    cs_out = [dint(f"cs_out{i}", [4, 128, 4]) for i in range(2)]

    with ExitStack() as st:
        fw = FW(nc, st, same_engine_sync=same_engine_sync)

        uid = {"i": 0}

        def sb(name, shape, dt, stack=st):
            uid["i"] += 1
            return stack.enter_context(nc.sbuf_tensor(f"{name}_{uid['i']}", list(shape), dt))

        X = sb("X", [128, 8, TT], F32)
        H = sb("H", [128, 8, TT], BF16)
        YB = sb("YB", [128, 3, 4, TT], BF16)
        WST = [sb(f"WST{i}", [128, 2048], F32) for i in range(3)]
        WRG = [sb(f"WRG{i}", [128, 2048], BF16) for i in range(4)]
        PV = sb("PV", [128, depth, NV], F32)
        GF = sb("GF", [128, 8], F32)
        CV = sb("CV", [128, 8, 2], F32)
        SC = sb("SC", [128, 8, 2], BF16)
        MODV = [sb(f"MODV{i}", [128, 48, 2], F32) for i in range(2)]
        AB = sb("AB", [128, 2, 8, 2], F32)
        CL = sb("CL", [128, 8], F32)
        CL2 = sb("CL2", [128, 8], F32)
        CLH = sb("CLH", [128, 8], F32)
        CL256 = sb("CL256", [128, 8], F32)
        HBV = sb("HBV", [128, 16], F32)
        ONES = sb("ONES", [128, 128], BF16)
        ONE1 = sb("ONE1", [128, 64], BF16)
        EPSC = sb("EPSC", [128, 1], F32)
        SEL = sb("SEL", [128, 12], F32)
        FLG = sb("FLG", [128, 2], F32)
        ICP = sb("ICP", [128, 4, 2, 8], F32)
        ICS = sb("ICS", [128, 4, 2, 8], F32)
        ST0 = sb("ST0", [128, depth * 8], F32)
        SOH = sb("SOH", [128, 1024], BF16)
        RWB = sb("RWB", [128, 512], BF16)
        XH = sb("XH", [128, 8, 16], F32)
        HH = sb("HH", [128, 8, 16], BF16)
        STO = sb("STO", [128, 32], F32)

        PS = [st.enter_context(nc.psum_tensor(f"PS{i}", [128, 512], F32)) for i in range(8)]
        tPS = [T(f"PS{i}", excl=True) for i in range(8)]

        tX = [[T(f"X{k}_{t}") for t in range(NT)] for k in range(8)]
        tH = [[T(f"H{k}_{t}") for t in range(NT)] for k in range(8)]
        tYB = [[[T(f"Y{n}_{c}_{t}") for t in range(NT)] for c in range(4)] for n in range(3)]
        tWST = [T(f"WST{i}") for i in range(3)]
        tWRG = [T(f"WRG{i}") for i in range(4)]
        tC = T("consts")
        tMODV = [T("MODV0"), T("MODV1")]
        tAB = T("AB")
        tXH = T("XH")
        tHH = T("HH")
        tSTO = T("STO")
        tT2 = [T("T2r0"), T("T2r1")]

        def tsl(t):
            return slice(t * 512, (t + 1) * 512)

        def chk(name):
            if stop_after == name:
                raise _Stop()

        grp_of_tile = [0, 0, 1]

        fw.dma("sp", PV[:], pvec[:, :, :], writes=[tC])
        fw.dma("sp", GF[:], gfin[:, :], writes=[tC])
        fw.dma("sp", CV[:], cvec[:, :, :], writes=[tC])
        fw.dma("sp", SEL[:], sel[:, :], writes=[tC])
        fw.dma("sp", FLG[:], flags[:, :], writes=[tC])
        fw.dma("sp", ICP[:], icp[:, :, :, :], writes=[tC])
        fw.dma("sp", ICS[:], ics[:, :, :, :], writes=[tC])
        fw.dma("sp", ST0[:], st0[:, :], writes=[tC])
        fw.dma("sp", XH[:], xh0[:, :, :], writes=[tXH])
        tSF = T("SMALLF")
        a0 = ExitStack()
        SMALLF = sb("SMALLF", [128, 1024], F32, a0)
        fw.dma("sp", SMALLF[:], slotoh[:, :], writes=[tSF])
        fw.op("dve", lambda e: e.tensor_copy(out=SOH[:], in_=SMALLF[:]), reads=[tSF], writes=[tC])
        fw.dma("sp", SMALLF[:, 0:512], rowb[:, :], writes=[tSF])
        fw.op("dve", lambda e: e.tensor_copy(out=RWB[:], in_=SMALLF[:, 0:512]), reads=[tSF], writes=[tC])
        fw.release([tSF])
        a0.close()
        fw.op("dve", lambda e: e.memset(ONES[:], 1.0 / 1024.0), writes=[tC])
        fw.op("dve", lambda e: e.memset(ONE1[:], 1.0), writes=[tC])
        fw.op("dve", lambda e: e.memset(EPSC[:], EPS), writes=[tC])
        fw.op("act", lambda e: e.activation(out=SC[:], in_=CV[:], func=AF.Silu), reads=[tC], writes=[tC])
        xv = xT.rearrange("(k p) t -> p k t", p=128)
        for k in range(8):
            for t in range(NT):
                fw.dma("sp", X[:, k, tsl(t)], xv[:, k, tsl(t)], writes=[tX[k][t]])

        plan = []
        state = {"issued": 0}

        def add_block(segs, kc):
            ncols = sum(s[1] for s in segs)
            assert kc * ncols <= 2048
            plan.append(dict(segs=segs, kc=kc, ncols=ncols))
            return len(plan) - 1

        def issue_block(i):
            b = plan[i]
            s_i, r_i = i % 3, i % 4
            kc, ncols = b["kc"], b["ncols"]
            dst = WST[s_i][:, 0:kc * ncols].rearrange("p (k n) -> p k n", k=kc)
            c0 = 0
            for (ap, n) in b["segs"]:
                fw.dma("sp", dst[:, :, c0:c0 + n], ap, writes=[tWST[s_i]])
                c0 += n
            fw.op("act", lambda e: e.activation(out=WRG[r_i][:, 0:kc * ncols], in_=WST[s_i][:, 0:kc * ncols], func=AF.Identity),
                  reads=[tWST[s_i]], writes=[tWRG[r_i]])

        def get_block(i, lookahead=2):
            while state["issued"] <= min(i + lookahead, len(plan) - 1):
                issue_block(state["issued"])
                state["issued"] += 1
            b = plan[i]
            r_i = i % 4
            w = WRG[r_i][:, 0:b["kc"] * b["ncols"]].rearrange("p (k n) -> p k n", k=b["kc"])
            return w, tWRG[r_i]

        def wcols(wm, l, c0, n):
            return wm[l].rearrange("(k p) n -> p k n", p=128)[:, :, c0:c0 + n]

        dps = {"i": 0, "n": 4}

        def next_dps():
            i = dps["i"] % dps["n"]
            dps["i"] += 1
            return PS[i], tPS[i]

        def plan_mod(l):
            return [add_block([(wcols(w_mod, l, c * 256, 256), 256)], 8) for c in range(24)]

        def emit_mod_block(l, c, bi):
            w, tw = get_block(bi)
            for mm_ in range(2):
                m = c * 2 + mm_
                fns = []
                for k in range(8):
                    fns.append(lambda e, k=k, m=m, mm_=mm_: e.matmul(
                        PS[7][:, 2 * m:2 * m + 2], lhsT=w[:, k, mm_ * 128:(mm_ + 1) * 128], rhs=SC[:, k, :],
                        start=(k == 0), stop=(k == 7)))
                fw.mm(fns, reads=[tw, tC], writes=[tPS[7]])

        def finish_mod(l):
            mv = MODV[l % 2]
            psv = PS[7][:, 0:96].rearrange("p (m g) -> p m g", g=2)
            for g in range(2):
                fw.op("dve", lambda e, g=g: e.tensor_tensor(out=mv[:, :, g], in0=psv[:, :, g], in1=PV[:, l, 16:64], op=ALU.add),
                      reads=[tPS[7], tC], writes=[tMODV[l % 2]])

        def emit_ab(l):
            mv = MODV[l % 2]
            for which, (g0, sc0) in enumerate(((0, 8), (8, 32))):
                for g in range(2):
                    fw.op("dve", lambda e, which=which, g=g, g0=g0, sc0=sc0: e.scalar_tensor_tensor(
                        out=AB[:, which, :, g], in0=mv[:, sc0:sc0 + 8, g], scalar=1.0, in1=PV[:, l, g0:g0 + 8],
                        op0=ALU.add, op1=ALU.mult), reads=[tMODV[l % 2], tC], writes=[tAB])
            fw.op("act", lambda e: e.activation(out=CL[:], in_=PV[:, l, 104:112], func=AF.Exp, scale=-1.0), reads=[tC], writes=[tAB])
            fw.op("act", lambda e: e.activation(out=CL[:], in_=CL[:], func=AF.Ln, bias=1.0), reads=[tAB], writes=[tAB])
            fw.op("dve", lambda e: e.tensor_scalar(out=CL2[:], in0=CL[:], scalar1=-16.0, scalar2=None, op0=ALU.mult), reads=[tAB], writes=[tAB])
            fw.op("dve", lambda e: e.tensor_scalar(out=CL[:], in0=CL[:], scalar1=-8.0, scalar2=None, op0=ALU.mult), reads=[tAB], writes=[tAB])
            fw.op("dve", lambda e: e.tensor_scalar(out=CLH[:], in0=CL[:], scalar1=0.5, scalar2=None, op0=ALU.mult), reads=[tAB], writes=[tAB])
            fw.op("dve", lambda e: e.tensor_scalar(out=CL256[:], in0=CL[:], scalar1=256.0, scalar2=None, op0=ALU.mult), reads=[tAB], writes=[tAB])
            fw.op("dve", lambda e: e.tensor_scalar(out=HBV[:], in0=PV[:, l, 88:104], scalar1=0.5, scalar2=None, op0=ALU.mult), reads=[tC], writes=[tAB])

        def emit_norm(l, which, arena):
            SQ, tSQ, RS, tRS, TMP, tTMP = arena
            mv = MODV[l % 2]
            b0 = 0 if which == 0 else 24
            pss = []
            qi = 0
            for t in range(NT):
                ps, tps = next_dps()
                pss.append((ps, tps))
                for k in range(8):
                    j = qi % len(SQ)
                    qi += 1
                    fw.op("pool", lambda e, k=k, j=j, t=t: e.tensor_tensor(out=SQ[j][:], in0=X[:, k, tsl(t)], in1=X[:, k, tsl(t)], op=ALU.mult),
                          reads=[tX[k][t]], writes=[tSQ[j]])
                    fw.mm([lambda e, k=k, j=j, ps=ps: e.matmul(ps[:], lhsT=ONES[:], rhs=SQ[j][:], start=(k == 0), stop=(k == 7))],
                          reads=[tSQ[j], tC], writes=[tps])
            for t in range(NT):
                ps, tps = pss[t]
                fw.op("act", lambda e, t=t, ps=ps: e.activation(out=RS[t][:], in_=ps[:], func=AF.Ln, bias=EPSC[:, 0:1]), reads=[tps, tC], writes=[tRS[t]])
                fw.op("act", lambda e, t=t: e.activation(out=RS[t][:], in_=RS[t][:], func=AF.Exp, scale=-0.5), reads=[tRS[t]], writes=[tRS[t]])
            qi = 0
            for t in range(NT):
                g = grp_of_tile[t]
                for k in range(8):
                    j = qi % len(TMP)
                    qi += 1
                    fw.op("dve", lambda e, k=k, j=j, t=t: e.tensor_tensor(out=TMP[j][:], in0=X[:, k, tsl(t)], in1=RS[t][:], op=ALU.mult),
                          reads=[tX[k][t], tRS[t]], writes=[tTMP[j]])
                    fw.op("act", lambda e, k=k, j=j, g=g, t=t: e.activation(
                        out=H[:, k, tsl(t)], in_=TMP[j][:], func=AF.Identity,
                        scale=AB[:, which, k, g:g + 1], bias=mv[:, b0 + k, g:g + 1]),
                        reads=[tTMP[j], tAB, tMODV[l % 2]], writes=[tH[k][t]])

        def emit_norm_halo(l, arena):
            SQ, tSQ, RS, tRS, TMP, tTMP = arena
            mv = MODV[l % 2]
            ps, tps = next_dps()
            fw.op("act", lambda e: e.activation(out=SQ[0][:, 0:128], in_=XH[:].rearrange("p k t -> p (k t)"), func=AF.Square),
                  reads=[tXH], writes=[tSQ[0]])
            sqv = SQ[0][:, 0:128].rearrange("p (k t) -> p k t", k=8)
            fw.mm([lambda e, k=k: e.matmul(ps[:, 0:16], lhsT=ONES[:], rhs=sqv[:, k, :], start=(k == 0), stop=(k == 7)) for k in range(8)],
                  reads=[tSQ[0], tC], writes=[tps])
            fw.op("act", lambda e: e.activation(out=RS[0][:, 0:16], in_=ps[:, 0:16], func=AF.Ln, bias=EPSC[:, 0:1]), reads=[tps, tC], writes=[tRS[0]])
            fw.op("act", lambda e: e.activation(out=RS[0][:, 0:16], in_=RS[0][:, 0:16], func=AF.Exp, scale=-0.5), reads=[tRS[0]], writes=[tRS[0]])
            for k in range(8):
                fw.op("dve", lambda e, k=k: e.tensor_tensor(out=TMP[0][:, 0:16], in0=XH[:, k, :], in1=RS[0][:, 0:16], op=ALU.mult),
                      reads=[tXH, tRS[0]], writes=[tTMP[0]])
                fw.op("act", lambda e, k=k: e.activation(out=HH[:, k, :], in_=TMP[0][:, 0:16], func=AF.Identity,
                                                          scale=AB[:, 0, k, 1:2], bias=mv[:, k, 1:2]),
                      reads=[tTMP[0], tAB, tMODV[l % 2]], writes=[tHH])

        def dense(w, tw, mcol, kcn, rhs_fn, rhs_tiles_fn, t, ps, tps, pcols=slice(0, 512)):
            fns = []
            rt = []
            for k in range(kcn):
                fns.append(lambda e, k=k: e.matmul(ps[:, pcols], lhsT=w[:, k, mcol:mcol + 128], rhs=rhs_fn(k, t),
                                                   start=(k == 0), stop=(k == kcn - 1)))
                rt.append(rhs_tiles_fn(k, t))
            fw.mm(fns, reads=[tw] + rt, writes=[tps])

        hrhs = lambda k, t: H[:, k, tsl(t)]
        hrt = lambda k, t: tH[k][t]

        def plan_layer(l):
            P = {}
            P["qk"] = [add_block([(wcols(w_in, l, 512 + m * 128, 128), 128), (wcols(w_in, l, 1024 + m * 128, 128), 128)], 8) for m in range(4)]
            P["v"] = [add_block([(wcols(w_in, l, 1536 + m * 128, 128), 128)], 8) for m in range(4)]
            return P

        order = []

        LP = []
        mod_blocks = {}
        mod_blocks[0] = plan_mod(0)
        for l in range(depth):
            P = {}
            P["qkv"] = []
            for m in range(4):
                P["qkv"].append((
                    add_block([(wcols(w_in, l, 512 + m * 128, 128), 128), (wcols(w_in, l, 1024 + m * 128, 128), 128)], 8),
                    add_block([(wcols(w_in, l, 1536 + m * 128, 128), 128)], 8)))
            P["poolw"] = add_block([(pool_w[l].rearrange("c (o n) -> c o n", o=1), 512)], 1)
            P["pool"] = [add_block([(wcols(w_in, l, gp * 256, 256), 256)], 8) for gp in range(2)]
            P["bd"] = add_block([(lru_bd[l].rearrange("c (o n) -> c o n", o=1), 2048)], 1)
            P["lru"] = [add_block([(wcols(w_in, l, 2048 + c * 128, 128), 128), (wcols(w_in, l, 2560 + c * 128, 128), 128)], 8) for c in range(4)]
            P["gate"] = []
            for jp in range(4):
                for n in range(3):
                    P["gate"].append((
                        add_block([(wcols(w_in, l, 3072 + n * 1024 + jp * 256, 256), 256)], 8),
                        add_block([(w_branch[l, n].rearrange("(k p) n -> p k n", p=128)[:, :, jp * 256:(jp + 1) * 256], 256)], 4)))
            P["out"] = [add_block([(wcols(w_out, l, mp * 256, 256), 256)], 8) for mp in range(4)]
            P["ffseq"] = []
            halves = ((0, 12), (12, 22))
            nmod = 0
            for hf, (c0, c1) in enumerate(halves):
                for i in range(c0, c1):
                    P["ffseq"].append(("ffin", (hf, i - c0), add_block(
                        [(wcols(w_ff_in, l, i * 128, 128), 128), (wcols(w_ff_in, l, FFH + i * 128, 128), 128)], 8)))
                    if l + 1 < depth:
                        for _ in range(2 if i in (5, 16) else 1):
                            if nmod < 24:
                                P["ffseq"].append(("mod", nmod, add_block([(wcols(w_mod, l + 1, nmod * 256, 256), 256)], 8)))
                                nmod += 1
                nk = c1 - c0
                for j in range(8):
                    P["ffseq"].append(("ffout", (hf, j, nk), add_block(
                        [(w_ff_out[l].rearrange("(k p) n -> p k n", p=128)[:, c0:c1, j * 128:(j + 1) * 128], 128)], nk)))
            assert nmod == (24 if l + 1 < depth else 0)
            LP.append(P)

        cons = {"i": 0}

        def take(bi):
            assert bi == cons["i"], (bi, cons["i"])
            cons["i"] += 1
            return get_block(bi)

        try:
            for c in range(24):
                bi = mod_blocks[0][c]
                assert bi == cons["i"]
                cons["i"] += 1
                emit_mod_block(0, c, bi)
            finish_mod(0)
            chk("mod")

            tKVI = [T("kvin0"), T("kvin1")]
            tKVO = [T("kvout0"), T("kvout1")]
            tCXI, tCXO = T("cxin"), T("cxout")
            tCSI = [T("csin0"), T("csin1")]
            tCSO = [T("csout0"), T("csout1")]

            for l in range(depth):
                P = LP[l]
                emit_ab(l)
                mv = MODV[l % 2]
                tmv = tMODV[l % 2]

                with Arena() as a1:
                    SQ = [sb(f"SQ{j}", [128, 512], BF16, a1) for j in range(4)]
                    RS = [sb(f"RS{j}", [128, 512], F32, a1) for j in range(3)]
                    TMP = [sb(f"TMP{j}", [128, 512], F32, a1) for j in range(4)]
                    tSQ = [T(f"SQ{j}") for j in range(4)]
                    tRS = [T(f"RS{j}") for j in range(3)]
                    tTMP = [T(f"TMP{j}") for j in range(4)]
                    arena = (SQ, tSQ, RS, tRS, TMP, tTMP)
                    emit_norm(l, 0, arena)
                    if l >= 1:
                        XCAND = sb("XCAND", [128, 4, 128], F32, a1)
                        XSEL = sb("XSEL", [128, 8, 8], F32, a1)
                        tXCAND, tXSEL = T("XCAND"), T("XSEL")
                        fw.dma("sp", XCAND[:], cx_out.rearrange("r p n -> p r n"), reads=[tCXO], writes=[tXCAND])
                        for (d0, s0, c0) in ((0, 0, 8), (8, 4, 0)):
                            cv_ = lambda r, c0=c0: XCAND[:, r, :].rearrange("p (k t) -> p k t", k=8)[:, :, c0:c0 + 8]
                            fw.op("dve", lambda e, d0=d0, s0=s0, cv_=cv_: e.tensor_scalar(out=XH[:, :, d0:d0 + 8], in0=cv_(0), scalar1=SEL[:, s0:s0 + 1],
                                                                                         scalar2=None, op0=ALU.mult), reads=[tXCAND, tC], writes=[tXH])
                            for r in range(1, 4):
                                fw.op("dve", lambda e, d0=d0, s0=s0, cv_=cv_, r=r: e.scalar_tensor_tensor(
                                    out=XH[:, :, d0:d0 + 8], in0=cv_(r), scalar=SEL[:, s0 + r:s0 + r + 1], in1=XH[:, :, d0:d0 + 8],
                                    op0=ALU.mult, op1=ALU.add), reads=[tXCAND, tC, tXH], writes=[tXH])
                    emit_norm_halo(l, arena)
                    chk("norm")
                    fw.release(tSQ + tRS + tTMP + ([tXCAND, tXSEL] if l >= 1 else []))

                with Arena() as a2:
                    QT = sb("QT", [128, TT], BF16, a2)
                    KT = sb("KT", [128, TT], BF16, a2)
                    VT = sb("VT", [128, 12, 128], BF16, a2)
                    KWs = [sb(f"KW{j}", [128, 1024], BF16, a2) for j in range(2)]
                    VWs = [sb(f"VW{j}", [128, 8, 128], BF16, a2) for j in range(2)]
                    CKs = [sb(f"CK{j}", [128, 512], BF16, a2) for j in range(2)]
                    CVBs = [sb(f"CVB{j}", [128, 4, 128], BF16, a2) for j in range(2)]
                    QSs = [sb(f"QS{j}", [128, 512], BF16, a2) for j in range(2)]
                    CAND = sb("CAND", [128, 4, 512], BF16, a2)
                    CSEL = sb("CSEL", [128, 256], BF16, a2)
                    tCSEL = T("CSEL")
                    T2 = [sb(f"T2r{j}", [128, 22, 64], F32, a2) for j in range(2)]
                    FS = [sb(f"FS{j}", [128, 512], F32, a2) for j in range(3)]
                    PT = [sb(f"PT{j}", [128, 512], BF16, a2) for j in range(4)]
                    RC = sb("RC", [128, 512], F32, a2)
                    tQT = [T(f"QT{t}") for t in range(NT)]
                    tKT = [T(f"KT{t}") for t in range(NT)]
                    tVT = [T(f"VT{t}") for t in range(NT)]
                    tKWs, tVWs, tCKs, tCVBs, tQSs = ([T(f"{n_}{j}") for j in range(2)] for n_ in ("KW", "VW", "CK", "CVB", "QS"))
                    tCAND = T("CAND")
                    tFS = [T(f"FS{j}") for j in range(3)]
                    tPT = [T("PT0"), T("PT1"), T("PT2"), T("PT3")]
                    tRC = T("RC")
                    arena_tiles = tQT + tKT + tVT + tKWs + tVWs + tCKs + tCVBs + tQSs + [tCAND, tCSEL] + tFS + tPT + [tRC] + tT2
                    for j in range(2):
                        fw.op("pool", lambda e, j=j: e.memset(T2[j][:], 0.0), writes=[tT2[j]])
                    fsc = {"i": 0}

                    def next_fs():
                        i = fsc["i"] % 3
                        fsc["i"] += 1
                        return FS[i], tFS[i]

                    cnt = {"sb": 0, "pt": 0, "sps": 0, "t2": 0}
                    dps["n"] = 3

                    def exp_tile(src_ps, tsrc, width, bias_ap=None, bias_tiles=()):
                        j = cnt["pt"] % 4
                        cnt["pt"] += 1
                        if bias_ap is None:
                            fw.op("act", lambda e: e.activation(out=PT[j][:, 0:width], in_=src_ps, func=AF.Exp, scale=SCALE),
                                  reads=[tsrc], writes=[tPT[j]])
                        else:
                            sbf, tsbf = next_fs()
                            fw.op("dve", lambda e: e.scalar_tensor_tensor(out=sbf[:, 0:width], in0=src_ps, scalar=SCALE, in1=bias_ap,
                                                                           op0=ALU.mult, op1=ALU.add),
                                  reads=[tsrc] + list(bias_tiles), writes=[tsbf])
                            fw.op("act", lambda e: e.activation(out=PT[j][:, 0:width], in_=sbf[:, 0:width], func=AF.Exp),
                                  reads=[tsbf], writes=[tPT[j]])
                        return PT[j], tPT[j]

                    def next_sps():
                        i = 3 + cnt["sps"] % 3
                        cnt["sps"] += 1
                        return PS[i], tPS[i]

                    def run_pipeline(steps, depth_):
                        G = 2
                        n = len(steps)
                        outs = [None] * n
                        ngr = (n + G - 1) // G
                        for g in range(ngr + 1):
                            if g < ngr:
                                for i in range(g * G, min(n, (g + 1) * G)):
                                    outs[i] = steps[i][0]()
                            if g >= 1:
                                for i in range((g - 1) * G, min(n, g * G)):
                                    steps[i][1](outs[i])

                    QRANGE = {0: (0, 128), 1: (0, 256), 6: (320, 512), 7: (448, 512)}

                    def sample_attention(m):
                        pb = m % 2
                        KW, VW, CK, CVB, QS = KWs[pb], VWs[pb], CKs[pb], CVBs[pb], QSs[pb]
                        tKW, tVW, tCK, tCVB, tQS = tKWs[pb], tVWs[pb], tCKs[pb], tCVBs[pb], tQSs[pb]
                        steps = []
                        nchunks = 12
                        for hh in range(2):
                            h = 2 * m + hh
                            hs = slice(hh * 64, (hh + 1) * 64)
                            for c in range(nchunks):
                                def s1(hh=hh, h=h, hs=hs, c=c):
                                    if c == 0:
                                        j2 = cnt["t2"] % 2
                                        cnt["t2"] += 1
                                        src = rpbG[l, :, h, :, :]
                                        fw.dma("act", T2[j2][0:64, 3:18, :], src, writes=[tT2[j2]])
                                        fw.dma("act", T2[j2][64:128, 4:19, :], src, writes=[tT2[j2]])
                                        cnt["t2cur"] = j2
                                    j2 = cnt["t2cur"]
                                    t2flat = T2[j2][:].rearrange("p e w -> p (e w)")
                                    sps, tsps = next_sps()
                                    if c < 8:
                                        q0, q1 = QRANGE.get(c, (0, 512))
                                        fw.mm([
                                            lambda e: e.matmul(sps[:, q0:q1], lhsT=KW[hs, c * 128:(c + 1) * 128], rhs=QS[hs, q0:q1], start=True, stop=False),
                                            lambda e: e.matmul(sps[:, q0:q1], lhsT=SOH[hs, c * 128:(c + 1) * 128], rhs=RWB[hs, q0:q1], start=False, stop=True),
                                        ], reads=[tKW, tQS, tC], writes=[tsps])
                                        e0 = (14 - 2 * c) * 64
                                        pt, tpt = exp_tile(sps[:, q0:q1], tsps, q1 - q0, bias_ap=t2flat[:, e0 + q0:e0 + q1], bias_tiles=[tT2[j2]])
                                        return pt, tpt, q0, q1
                                    cc = c - 8
                                    fw.mm([lambda e: e.matmul(sps[:], lhsT=CK[hs, cc * 128:(cc + 1) * 128], rhs=QS[hs, :], start=True, stop=True)],
                                          reads=[tCK, tQS], writes=[tsps])
                                    pt, tpt = exp_tile(sps[:], tsps, 512)
                                    return pt, tpt, 0, 512

                                def s2(o, hh=hh, hs=hs, c=c):
                                    pt, tpt, q0, q1 = o
                                    if c < 8:
                                        vl, vt = VW[:, c, hs], tVW
                                    else:
                                        vl, vt = CVB[:, c - 8, hs], tCVB
                                    first, last = (c == 0), (c == nchunks - 1)
                                    fw.mm([
                                        lambda e: e.matmul(PS[6][hs, q0:q1], lhsT=vl, rhs=pt[:, 0:q1 - q0], start=first, stop=last),
                                        lambda e: e.matmul(PS[7][hs, q0:q1], lhsT=ONE1[:, :], rhs=pt[:, 0:q1 - q0], start=first, stop=last),
                                    ], reads=[vt, tpt, tC], writes=[tPS[6], tPS[7]])
                                    if hh == 1 and last:
                                        fw.op("act", lambda e: e.activation(out=RC[:], in_=PS[7][:], func=AF.Ln), reads=[tPS[7]], writes=[tRC])
                                        fw.op("act", lambda e: e.activation(out=RC[:], in_=RC[:], func=AF.Exp, scale=-1.0), reads=[tRC], writes=[tRC])
                                        fw.op("dve", lambda e: e.tensor_tensor(out=YB[:, 1, m, 1024:1536], in0=PS[6][:], in1=RC[:], op=ALU.mult),
                                              reads=[tPS[6], tRC], writes=[tYB[1][m][2]])
                                steps.append((s1, s2))
                        run_pipeline(steps, 2)

                    def prompt_attention(m, mid_hook=None):
                        steps = []
                        for tp in range(2):
                            for hh in range(2):
                                hs = slice(hh * 64, (hh + 1) * 64)
                                for s2_ in range(2):
                                    s = tp * 2 + s2_
                                    q0 = s * 256

                                    def s1(tp=tp, hs=hs, q0=q0):
                                        sps, tsps = next_sps()
                                        fw.mm([lambda e, kc=kc: e.matmul(sps[:, kc * 256:(kc + 1) * 256], lhsT=KT[hs, q0 + kc * 128:q0 + (kc + 1) * 128],
                                                                         rhs=QT[hs, q0:q0 + 256], start=True, stop=True) for kc in range(2)],
                                              reads=[tKT[tp], tQT[tp]], writes=[tsps])
                                        return exp_tile(sps[:], tsps, 512)

                                    def s2(o, tp=tp, hh=hh, hs=hs, s2_=s2_, s=s):
                                        pt, tpt = o
                                        oc = slice(s2_ * 256, (s2_ + 1) * 256)
                                        fns = []
                                        for kc in range(2):
                                            fns.append(lambda e, kc=kc: e.matmul(PS[6][hs, oc], lhsT=VT[:, s * 2 + kc, hs], rhs=pt[:, kc * 256:(kc + 1) * 256],
                                                                                 start=(kc == 0), stop=(kc == 1)))
                                            fns.append(lambda e, kc=kc: e.matmul(PS[7][hs, oc], lhsT=ONE1[:, :], rhs=pt[:, kc * 256:(kc + 1) * 256],
                                                                                 start=(kc == 0), stop=(kc == 1)))
                                        fw.mm(fns, reads=[tVT[tp], tpt, tC], writes=[tPS[6], tPS[7]])
                                        if hh == 1 and s2_ == 1:
                                            fw.op("act", lambda e: e.activation(out=RC[:], in_=PS[7][:], func=AF.Ln), reads=[tPS[7]], writes=[tRC])
                                            fw.op("act", lambda e: e.activation(out=RC[:], in_=RC[:], func=AF.Exp, scale=-1.0), reads=[tRC], writes=[tRC])
                                            fw.op("dve", lambda e: e.tensor_tensor(out=YB[:, 1, m, tsl(tp)], in0=PS[6][:], in1=RC[:], op=ALU.mult),
                                                  reads=[tPS[6], tRC], writes=[tYB[1][m][tp]])
                                            if tp == 0 and mid_hook is not None:
                                                mid_hook()
                                    steps.append((s1, s2))
                        run_pipeline(steps, 2)

                    for m in range(4):
                        bqk, bv = P["qkv"][m]
                        wqk, twqk = take(bqk)
                        for t in range(NT):
                            ps, tps = next_dps()
                            dense(wqk, twqk, 0, 8, hrhs, hrt, t, ps, tps)
                            fw.op("act", lambda e, t=t, ps=ps: e.activation(out=QT[:, tsl(t)], in_=ps[:], func=AF.Identity),
                                  reads=[tps], writes=[tQT[t]])
                            chk(f"q{m}_{t}")
                            ps, tps = next_dps()
                            dense(wqk, twqk, 128, 8, hrhs, hrt, t, ps, tps)
                            if t < 2:
                                kf, tkf = next_fs()
                                fw.op("act", lambda e, kf=kf, ps=ps: e.activation(out=kf[:], in_=ps[:], func=AF.Identity),
                                      reads=[tps], writes=[tkf])
                                fw.op("dve", lambda e, t=t, kf=kf: e.tensor_copy(out=KT[:, tsl(t)], in_=kf[:]), reads=[tkf], writes=[tKT[t]])
                                chk(f"kc{m}_{t}")
                                fw.dma("act", kT_out[l, m * 128:(m + 1) * 128, tsl(t)], kf[:], reads=[tkf])
                                chk(f"kd{m}_{t}")
                            else:
                                fw.op("dve", lambda e, t=t, ps=ps: e.tensor_copy(out=KT[:, tsl(t)], in_=ps[:]), reads=[tps], writes=[tKT[t]])
                        chk(f"qk{m}")
                        wv, twv = take(bv)
                        for t in range(NT):
                            ps, tps = next_dps()
                            for q4 in range(4):
                                tt = t * 4 + q4
                                fw.mm([lambda e, k=k, tt=tt, q4=q4: e.matmul(ps[:, q4 * 128:(q4 + 1) * 128], lhsT=H[:, k, tt * 128:(tt + 1) * 128],
                                                                             rhs=wv[:, k, :], start=(k == 0), stop=(k == 7)) for k in range(8)],
                                      reads=[twv] + [tH[k][t] for k in range(8)], writes=[tps])
                            vf, tvf = next_fs()
                            fw.op("dve", lambda e, vf=vf, ps=ps: e.tensor_copy(out=vf[:], in_=ps[:]), reads=[tps], writes=[tvf])
                            fw.op("dve", lambda e, ps=ps, t=t: e.tensor_copy(out=VT[:, t * 4:(t + 1) * 4, :].rearrange("p a b -> p (a b)"), in_=ps[:]),
                                  reads=[tps], writes=[tVT[t]])
                            if t < 2:
                                fw.dma("act", v_out[l, t * 512:(t + 1) * 512, m * 128:(m + 1) * 128].rearrange("(a p) n -> p a n", p=128),
                                       vf[:].rearrange("p (a n) -> p a n", a=4), reads=[tvf])
                        chk(f"v{m}")
                        bb = m % 2
                        fw.dma("pool", ckv_in[bb][0], KT[:, 1024:1536], reads=[tKT[2]], writes=[tKVI[bb]])
                        fw.dma("pool", ckv_in[bb][1], VT[:, 8:12, :].rearrange("p a b -> p (a b)"), reads=[tVT[2]], writes=[tKVI[bb]])
                        fw.collective("AllGather", GROUPS, ckv_in[bb].rearrange("a p n -> (a p) n"),
                                      ckv_out[bb].rearrange("r a p n -> (r a p) n"), reads=[tKVI[bb]], writes=[tKVO[bb]])
                        pb = m % 2
                        fw.op("pool", lambda e, pb=pb: e.tensor_copy(out=QSs[pb][:], in_=QT[:, 1024:1536]), reads=[tQT[2]], writes=[tQSs[pb]])
                        fw.op("pool", lambda e, pb=pb: e.tensor_copy(out=KWs[pb][:, 256:768], in_=KT[:, 1024:1536]), reads=[tKT[2]], writes=[tKWs[pb]])
                        fw.op("pool", lambda e, pb=pb: e.tensor_copy(out=VWs[pb][:, 2:6, :], in_=VT[:, 8:12, :]), reads=[tVT[2]], writes=[tVWs[pb]])
                        ckf, tckf = next_fs()
                        fw.dma("pool", ckf[:], ckT[l, m * 128:(m + 1) * 128, :], writes=[tckf])
                        fw.op("pool", lambda e, ckf=ckf, pb=pb: e.tensor_copy(out=CKs[pb][:], in_=ckf[:]), reads=[tckf], writes=[tCKs[pb]])
                        cvf, tcvf = next_fs()
                        fw.dma("pool", cvf[:].rearrange("p (a n) -> p a n", a=4), cv[l, :, m * 128:(m + 1) * 128].rearrange("(a p) n -> p a n", p=128), writes=[tcvf])
                        fw.op("pool", lambda e, cvf=cvf, pb=pb: e.tensor_copy(out=CVBs[pb][:].rearrange("p a n -> p (a n)"), in_=cvf[:]), reads=[tcvf], writes=[tCVBs[pb]])
                        chk(f"cc{m}")

                        def window(mm_):
                            pb_ = mm_ % 2
                            KW, VW = KWs[pb_], VWs[pb_]
                            for a in range(2):
                                fw.dma("sp", CAND[:], ckv_out[pb_][:, a].rearrange("r p n -> p r n"), reads=[tKVO[pb_]], writes=[tCAND])
                                if a == 0:
                                    dsts = ((KW[:, 0:256], 256, 0), (KW[:, 768:1024], 0, 4))
                                    tw_ = tKWs[pb_]
                                else:
                                    dsts = ((VW[:, 0:2, :].rearrange("p a b -> p (a b)"), 256, 0), (VW[:, 6:8, :].rearrange("p a b -> p (a b)"), 0, 4))
                                    tw_ = tVWs[pb_]
                                for (dst, c0, s0) in dsts:
                                    fw.op("dve", lambda e, dst=dst, c0=c0, s0=s0: e.tensor_scalar(
                                        out=dst, in0=CAND[:, 0, c0:c0 + 256], scalar1=SEL[:, s0:s0 + 1], scalar2=None, op0=ALU.mult),
                                        reads=[tCAND, tC], writes=[tw_])
                                    for r in range(1, 4):
                                        fw.op("dve", lambda e, dst=dst, c0=c0, s0=s0, r=r: e.scalar_tensor_tensor(
                                            out=dst, in0=CAND[:, r, c0:c0 + 256], scalar=SEL[:, s0 + r:s0 + r + 1], in1=dst,
                                            op0=ALU.mult, op1=ALU.add), reads=[tCAND, tC], writes=[tw_])

                        prompt_attention(m, mid_hook=(lambda: window(m - 1)) if m >= 1 else None)
                        chk(f"pattn{m}")
                        if m >= 1:
                            sample_attention(m - 1)
                        if m == 3:
                            window(3)
                            sample_attention(3)
                        chk(f"sattn{m}")
                    dps["n"] = 4
                    fw.release(arena_tiles)

                if debug and l == 0:
                    dbg_list.append(("H", H[:], [tH[k][t] for k in range(8) for t in range(NT)]))
                    dbg_list.append(("YB", YB[:], [tYB[n][c][t] for n in range(3) for c in range(4) for t in range(NT)]))
                    dbg_list.append(("MODV", MODV[0][:], [tMODV[0]]))
                chk("attn")

                with Arena() as a3:
                    LP_ = 1616
                    UPB = [sb(f"UP{j}", [128, LP_], F32, a3) for j in range(2)]
                    PB = sb("PB", [128, LP_], F32, a3)
                    PC = sb("PC", [128, LP_], F32, a3)
                    DG = [sb(f"DG{j}", [128, TT], BF16, a3) for j in range(2)]
                    ETMP = sb("ETMP", [128, 64], F32, a3)
                    tUP = [T("UP0"), T("UP1")]
                    tPB, tPC, tETMP = T("PB"), T("PC"), T("ETMP")
                    tDG = [T("DG0"), T("DG1")]
                    wpw0, twpw0 = take(P["poolw"])
                    POOLW = sb("POOLW", [128, 1, 512], BF16, a3)
                    twpw = T("POOLW")
                    fw.op("dve", lambda e: e.tensor_copy(out=POOLW[:], in_=wpw0), reads=[twpw0], writes=[twpw])
                    wpw = POOLW
                    for gp in range(2):
                        w, tw = take(P["pool"][gp])
                        for gg in range(2):
                            g = gp * 2 + gg
                            jb = g % 2
                            UP, tup = UPB[jb], tUP[jb]
                            fw.op("pool", lambda e, UP=UP: e.memset(UP[:], 0.0), writes=[tup])
                            for t in range(NT):
                                ps, tps = next_dps()
                                dense(w, tw, gg * 128, 8, hrhs, hrt, t, ps, tps)
                                if t < 2:
                                    dstv = UP[:, t * 544:(t + 1) * 544].rearrange("p (s n) -> p s n", s=2)[:, :, 8:264]
                                    srcv = ps[:].rearrange("p (s n) -> p s n", s=2)
                                else:
                                    dstv = UP[:, 1096:1608]
                                    srcv = ps[:]
                                fw.op("act", lambda e, dstv=dstv, srcv=srcv: e.activation(out=dstv, in_=srcv, func=AF.Identity),
                                      reads=[tps], writes=[tup])
                            ps, tps = next_dps()
                            fw.mm([lambda e, k=k: e.matmul(ps[:, 0:16], lhsT=w[:, k, gg * 128:(gg + 1) * 128], rhs=HH[:, k, :],
                                                           start=(k == 0), stop=(k == 7)) for k in range(8)],
                                  reads=[tw, tHH], writes=[tps])
                            fw.op("dve", lambda e, UP=UP, ps=ps: e.tensor_scalar(out=UP[:, 1088:1096], in0=ps[:, 0:8], scalar1=FLG[:, 0:1],
                                                                               scalar2=None, op0=ALU.mult), reads=[tps, tC], writes=[tup])
                            fw.op("dve", lambda e, UP=UP, ps=ps: e.tensor_scalar(out=UP[:, 1608:1616], in0=ps[:, 8:16], scalar1=FLG[:, 1:2],
                                                                               scalar2=None, op0=ALU.mult), reads=[tps, tC], writes=[tup])
                            L_ = LP_
                            fw.op("dve", lambda e, UP=UP: e.tensor_tensor(out=PB[:, 1:L_], in0=UP[:, 0:L_ - 1], in1=UP[:, 1:L_], op=ALU.add),
                                  reads=[tup], writes=[tPB])
                            S, tS = PB, tPB
                            if g >= 1:
                                fw.op("dve", lambda e: e.tensor_tensor(out=PC[:, 2:L_ - 1], in0=PB[:, 1:L_ - 2], in1=PB[:, 3:L_], op=ALU.add),
                                      reads=[tPB], writes=[tPC])
                                S, tS = PC, tPC
                            if g >= 2:
                                fw.op("dve", lambda e: e.tensor_tensor(out=PB[:, 4:L_ - 3], in0=PC[:, 2:L_ - 5], in1=PC[:, 6:L_ - 1], op=ALU.add),
                                      reads=[tPC], writes=[tPB])
                                S, tS = PB, tPB
                            if g >= 3:
                                fw.op("dve", lambda e: e.tensor_tensor(out=PC[:, 8:L_ - 7], in0=PB[:, 4:L_ - 11], in1=PB[:, 12:L_ - 3], op=ALU.add),
                                      reads=[tPB], writes=[tPC])
                                S, tS = PC, tPC
                            dg, tdg = DG[jb], tDG[jb]
                            invw = 1.0 / (2 ** (g + 1))

                            def pv_(buf):
                                return buf[:, 0:1088].rearrange("p (s n) -> p s n", s=4)
                            fw.op("dve", lambda e, S=S, UP=UP, dg=dg: e.scalar_tensor_tensor(
                                out=dg[:, 0:1024].rearrange("p (s n) -> p s n", s=4), in0=pv_(S)[:, :, 8:264], scalar=invw,
                                in1=pv_(UP)[:, :, 8:264], op0=ALU.mult, op1=ALU.subtract), reads=[tS, tup], writes=[tdg])
                            fw.op("dve", lambda e, S=S, UP=UP, dg=dg: e.scalar_tensor_tensor(
                                out=dg[:, 1024:1536], in0=S[:, 1096:1608], scalar=invw, in1=UP[:, 1096:1608],
                                op0=ALU.mult, op1=ALU.subtract), reads=[tS, tup], writes=[tdg])
                            for side in range(2):
                                o = 8 if side == 0 else 256
                                od = 0 if side == 0 else 248
                                ic = ICP[:, g, side, :]
                                icb = bass.AP(ic.tensor, ic.offset, [list(ic.ap[0]), [0, 4], [1, 8]])
                                fw.op("dve", lambda e, S=S, o=o, icb=icb: e.tensor_tensor(
                                    out=ETMP[:, 0:32].rearrange("p (s n) -> p s n", s=4), in0=pv_(S)[:, :, o:o + 8], in1=icb, op=ALU.mult),
                                    reads=[tS, tC], writes=[tETMP])
                                fw.op("dve", lambda e, UP=UP, dg=dg, o=o, od=od: e.tensor_tensor(
                                    out=dg[:, 0:1024].rearrange("p (s n) -> p s n", s=4)[:, :, od:od + 8],
                                    in0=ETMP[:, 0:32].rearrange("p (s n) -> p s n", s=4), in1=pv_(UP)[:, :, o:o + 8], op=ALU.subtract),
                                    reads=[tETMP, tup], writes=[tdg])
                                o2 = 1096 if side == 0 else 1600
                                od2 = 1024 if side == 0 else 1528
                                fw.op("dve", lambda e, S=S, o2=o2, side=side: e.tensor_tensor(
                                    out=ETMP[:, 32:40], in0=S[:, o2:o2 + 8], in1=ICS[:, g, side, :], op=ALU.mult),
                                    reads=[tS, tC], writes=[tETMP])
                                fw.op("dve", lambda e, UP=UP, dg=dg, o2=o2, od2=od2: e.tensor_tensor(
                                    out=dg[:, od2:od2 + 8], in0=ETMP[:, 32:40], in1=UP[:, o2:o2 + 8], op=ALU.subtract),
                                    reads=[tETMP, tup], writes=[tdg])
                            for t in range(NT):
                                ps, tps = next_dps()
                                fw.mm([lambda e, dg=dg, t=t, ps=ps: e.matmul(ps[:], lhsT=wpw[:, 0, g * 128:(g + 1) * 128], rhs=dg[:, tsl(t)],
                                                                              start=True, stop=True)], reads=[twpw, tdg], writes=[tps])
                                fw.op("act", lambda e, t=t, ps=ps: e.activation(out=YB[:, 0, g, tsl(t)], in_=ps[:], func=AF.Identity,
                                                                               scale=PV[:, l, 64 + g:65 + g]),
                                      reads=[tps, tC], writes=[tYB[0][g][t]])
                    fw.release(tUP + [tPB, tPC, tETMP, twpw] + tDG)
                chk("pool")

                with Arena() as a4:
                    LU = 1556
                    UX = sb("UX", [128, LU], F32, a4)
                    GY = sb("GY", [128, TT], BF16, a4)
                    XC = sb("XC", [128, TT], F32, a4)
                    XCB = sb("XCB", [128, TT], BF16, a4)
                    HS = [sb(f"HS{e_}", [128, TT], F32, a4) for e_ in range(2)]
                    GT = [sb(f"GT{j}", [128, 512], F32, a4) for j in range(8)]
                    CAR = sb("CAR", [128, 4], F32, a4)
                    CARG = sb("CARG", [128, 4, 4], F32, a4)
                    HINS = sb("HINS", [128, 16], F32, a4)
                    SRG = sb("SRG", [128, 2], F32, a4)
                    tUX, tGY = T("UX"), [T(f"GY{t}") for t in range(NT)]
                    tXC, tXCB = [T(f"XC{t}") for t in range(NT)], [T(f"XCB{t}") for t in range(NT)]
                    tHS = [[T(f"HS{e_}_{t}") for t in range(NT)] for e_ in range(2)]
                    tGT = [T(f"GT{j}") for j in range(8)]
                    tCAR, tCARG, tHINS, tSRG = T("CAR"), T("CARG"), T("HINS"), T("SRG")
                    gtc = {"i": 0}

                    def next_gt():
                        i = 4 + gtc["i"] % 4
                        gtc["i"] += 1
                        return GT[i], tGT[i]

                    def rev(ap, n):
                        return bass.AP(ap.tensor, ap.offset + (n - 1), [list(ap.ap[0]), [-1, n]])

                    wbd0, twbd0 = take(P["bd"])
                    BDW = sb("BDW", [128, 1, 2048], BF16, a4)
                    twbd = T("BDW")
                    fw.op("dve", lambda e: e.tensor_copy(out=BDW[:], in_=wbd0), reads=[twbd0], writes=[twbd])
                    wbd = BDW
                    for c in range(4):
                        w, tw = take(P["lru"][c])
                        fw.op("dve", lambda e: e.memset(UX[:], 0.0), writes=[tUX])
                        for t in range(NT):
                            ps, tps = next_dps()
                            dense(w, tw, 0, 8, hrhs, hrt, t, ps, tps)
                            if t < 2:
                                dstv = UX[:, t * 520:(t + 1) * 520].rearrange("p (s n) -> p s n", s=2)[:, :, 2:258]
                                srcv = ps[:].rearrange("p (s n) -> p s n", s=2)
                            else:
                                dstv = UX[:, 1042:1554]
                                srcv = ps[:]
                            fw.op("act", lambda e, dstv=dstv, srcv=srcv: e.activation(out=dstv, in_=srcv, func=AF.Identity),
                                  reads=[tps], writes=[tUX])
                        ps, tps = next_dps()
                        fw.mm([lambda e, k=k: e.matmul(ps[:, 0:16], lhsT=w[:, k, 0:128], rhs=HH[:, k, :], start=(k == 0), stop=(k == 7))
                               for k in range(8)], reads=[tw, tHH], writes=[tps])
                        fw.op("dve", lambda e, ps=ps: e.tensor_scalar(out=UX[:, 1040:1042], in0=ps[:, 6:8], scalar1=FLG[:, 0:1], scalar2=None,
                                                                     op0=ALU.mult), reads=[tps, tC], writes=[tUX])
                        fw.op("dve", lambda e, ps=ps: e.tensor_scalar(out=UX[:, 1554:1555], in0=ps[:, 8:9], scalar1=FLG[:, 1:2], scalar2=None,
                                                                     op0=ALU.mult), reads=[tps, tC], writes=[tUX])
                        for t in (2, 0, 1):
                            def tap(j, t=t):
                                if t < 2:
                                    return UX[:, t * 520:(t + 1) * 520].rearrange("p (s n) -> p s n", s=2)[:, :, j:j + 256]
                                return UX[:, 1040 + j:1040 + j + 512]
                            xo = XC[:, tsl(t)].rearrange("p (s n) -> p s n", s=2) if t < 2 else XC[:, tsl(t)]
                            fw.op("act", lambda e, xo=xo, tap=tap: e.activation(out=xo, in_=tap(0), func=AF.Identity, scale=PV[:, l, 68 + c:69 + c],
                                                                               bias=PV[:, l, 84 + c:85 + c]),
                                  reads=[tUX, tC], writes=[tXC[t]])
                            for j in range(1, 4):
                                fw.op("dve", lambda e, xo=xo, tap=tap, j=j: e.scalar_tensor_tensor(
                                    out=xo, in0=tap(j), scalar=PV[:, l, 68 + j * 4 + c:69 + j * 4 + c], in1=xo, op0=ALU.mult, op1=ALU.add),
                                    reads=[tUX, tC, tXC[t]], writes=[tXC[t]])
                            fw.op("dve", lambda e, t=t: e.tensor_copy(out=XCB[:, tsl(t)], in_=XC[:, tsl(t)]), reads=[tXC[t]], writes=[tXCB[t]])
                        AT = {}

                        def gate_scan(e_, t):
                            cl = CL[:, e_ * 4 + c:e_ * 4 + c + 1]
                            cl2 = CL2[:, e_ * 4 + c:e_ * 4 + c + 1]
                            if True:
                                ma = (e_ * 2 + 0) * 4 + c
                                mx = (e_ * 2 + 1) * 4 + c
                                idx = e_ * 4 + c
                                ps, tps = next_dps()
                                fw.mm([lambda e, ps=ps, t=t, ma=ma: e.matmul(ps[:], lhsT=wbd[:, 0, ma * 128:(ma + 1) * 128], rhs=XCB[:, tsl(t)],
                                                                              start=True, stop=True)], reads=[twbd, tXCB[t]], writes=[tps])
                                ps2, tps2 = next_dps()
                                fw.mm([lambda e, ps2=ps2, t=t, mx=mx: e.matmul(ps2[:], lhsT=wbd[:, 0, mx * 128:(mx + 1) * 128], rhs=XCB[:, tsl(t)],
                                                                                start=True, stop=True)], reads=[twbd, tXCB[t]], writes=[tps2])
                                rg, trg = next_gt()
                                if t == 2:
                                    aa, taa = GT[e_ * 2], tGT[e_ * 2]
                                    bb_, tbb = GT[e_ * 2 + 1], tGT[e_ * 2 + 1]
                                    fw.op("act", lambda e, rg=rg, ps=ps: e.activation(out=rg[:], in_=ps[:], func=AF.Tanh, scale=0.5,
                                                                                    bias=HBV[:, idx:idx + 1], accum_out=SRG[:, e_:e_ + 1]),
                                          reads=[tps, tAB], writes=[trg, tSRG])
                                else:
                                    aa, taa = next_gt()
                                    bb_, tbb = next_gt()
                                    fw.op("act", lambda e, rg=rg, ps=ps: e.activation(out=rg[:], in_=ps[:], func=AF.Tanh, scale=0.5,
                                                                                    bias=HBV[:, idx:idx + 1]),
                                          reads=[tps, tAB], writes=[trg])
                                fw.op("act", lambda e, bb_=bb_, ps2=ps2: e.activation(out=bb_[:], in_=ps2[:], func=AF.Tanh, scale=0.5,
                                                                                     bias=HBV[:, 8 + idx:9 + idx]),
                                      reads=[tps2, tAB], writes=[tbb])
                                fw.op("act", lambda e, rg=rg, aa=aa: e.activation(out=aa[:], in_=rg[:], func=AF.Exp, scale=CLH[:, idx:idx + 1],
                                                                                bias=CLH[:, idx:idx + 1]),
                                      reads=[trg, tAB], writes=[taa])
                                fw.op("dve", lambda e, rg=rg, aa=aa: e.tensor_tensor(out=rg[:], in0=aa[:], in1=aa[:], op=ALU.mult),
                                      reads=[taa], writes=[trg])
                                fw.op("act", lambda e, rg=rg: e.activation(out=rg[:], in_=rg[:], func=AF.Sqrt, scale=-1.0, bias=1.0),
                                      reads=[trg], writes=[trg])
                                fw.op("dve", lambda e, bb_=bb_, t=t: e.scalar_tensor_tensor(out=bb_[:], in0=bb_[:], scalar=1.0, in1=XC[:, tsl(t)],
                                                                                            op0=ALU.add, op1=ALU.mult),
                                      reads=[tbb, tXC[t]], writes=[tbb])
                                fw.op("dve", lambda e, bb_=bb_, rg=rg: e.scalar_tensor_tensor(out=bb_[:], in0=bb_[:], scalar=0.5, in1=rg[:],
                                                                                              op0=ALU.mult, op1=ALU.mult),
                                      reads=[tbb, trg], writes=[tbb])
                                if t < 2:
                                    for s2 in range(2):
                                        sl = slice(s2 * 256, (s2 + 1) * 256)
                                        o_ = HS[e_][:, t * 512 + s2 * 256:t * 512 + (s2 + 1) * 256]
                                        if e_ == 0:
                                            fw.op("dve", lambda e, o_=o_, aa=aa, bb_=bb_, sl=sl: e.tensor_tensor_scan(
                                                out=o_, data0=aa[:, sl], data1=bb_[:, sl], initial=0.0, op0=ALU.mult, op1=ALU.add),
                                                reads=[taa, tbb], writes=[tHS[e_][t]])
                                        else:
                                            fw.op("dve", lambda e, o_=o_, aa=aa, bb_=bb_, sl=sl: e.tensor_tensor_scan(
                                                out=rev(o_, 256), data0=rev(aa[:, sl], 256), data1=rev(bb_[:, sl], 256), initial=0.0,
                                                op0=ALU.mult, op1=ALU.add), reads=[taa, tbb], writes=[tHS[e_][t]])
                                else:
                                    AT[e_] = (aa, taa, bb_, tbb)
                                    o_ = HS[e_][:, 1024:1536]
                                    if e_ == 0:
                                        fw.op("dve", lambda e, o_=o_, aa=aa, bb_=bb_: e.tensor_tensor_scan(
                                            out=o_, data0=aa[:], data1=bb_[:], initial=0.0, op0=ALU.mult, op1=ALU.add),
                                            reads=[taa, tbb], writes=[tHS[e_][t]])
                                        fw.op("dve", lambda e: e.tensor_copy(out=CAR[:, 1:2], in_=HS[0][:, 1535:1536]), reads=[tHS[0][2]], writes=[tCAR])
                                    else:
                                        fw.op("dve", lambda e, o_=o_, aa=aa, bb_=bb_: e.tensor_tensor_scan(
                                            out=rev(o_, 512), data0=rev(aa[:], 512), data1=rev(bb_[:], 512), initial=0.0,
                                            op0=ALU.mult, op1=ALU.add), reads=[taa, tbb], writes=[tHS[e_][t]])
                                        fw.op("dve", lambda e: e.tensor_copy(out=CAR[:, 3:4], in_=HS[1][:, 1024:1025]), reads=[tHS[1][2]], writes=[tCAR])
                                    fw.op("act", lambda e, e_=e_: e.activation(out=CAR[:, 2 * e_:2 * e_ + 1], in_=SRG[:, e_:e_ + 1], func=AF.Exp,
                                                                               scale=CLH[:, e_ * 4 + c:e_ * 4 + c + 1], bias=CL256[:, e_ * 4 + c:e_ * 4 + c + 1]),
                                          reads=[tSRG, tAB], writes=[tCAR])
                        gate_scan(0, 2)
                        gate_scan(1, 2)
                        for t in range(NT):
                            ps, tps = next_dps()
                            dense(w, tw, 128, 8, hrhs, hrt, t, ps, tps)
                            fw.op("act", lambda e, t=t, ps=ps: e.activation(out=GY[:, tsl(t)], in_=ps[:], func=AF.Gelu_apprx_tanh),
                                  reads=[tps], writes=[tGY[t]])
                        cb = c % 2
                        fw.dma("pool", cs_in[cb], CAR[:], reads=[tCAR], writes=[tCSI[cb]])
                        fw.collective("AllGather", GROUPS, cs_in[cb], cs_out[cb].rearrange("r p n -> (r p) n"), reads=[tCSI[cb]], writes=[tCSO[cb]])
                        for e_ in range(2):
                            for t in range(2):
                                gate_scan(e_, t)
                        for t in range(2):
                            tm, ttm = next_gt()
                            fw.op("dve", lambda e, tm=tm, t=t: e.tensor_tensor(out=tm[:], in0=HS[0][:, tsl(t)], in1=HS[1][:, tsl(t)], op=ALU.add),
                                  reads=[tHS[0][t], tHS[1][t]], writes=[ttm])
                            fw.op("dve", lambda e, tm=tm, t=t: e.tensor_tensor(out=YB[:, 2, c, tsl(t)], in0=tm[:], in1=GY[:, tsl(t)], op=ALU.mult),
                                  reads=[ttm, tGY[t]], writes=[tYB[2][c][t]])
                        stv = STO[:].rearrange("p (s x) -> p s x", s=4)
                        fw.op("act", lambda e: e.activation(out=stv[:, :, c], in_=HS[0][:, 0:1024].rearrange("p (s n) -> p s n", s=4)[:, :, 255],
                                                            func=AF.Identity), reads=[tHS[0][0], tHS[0][1]], writes=[tSTO])
                        fw.op("act", lambda e: e.activation(out=stv[:, :, 4 + c], in_=HS[1][:, 0:1024].rearrange("p (s n) -> p s n", s=4)[:, :, 0],
                                                            func=AF.Identity), reads=[tHS[1][0], tHS[1][1]], writes=[tSTO])
                        fw.dma("pool", CARG[:], cs_out[cb].rearrange("r p n -> p r n"), reads=[tCSO[cb]], writes=[tCARG])
                        fw.op("dve", lambda e: e.tensor_copy(out=HINS[:, 0:1], in_=ST0[:, l * 8 + c:l * 8 + c + 1]), reads=[tC], writes=[tHINS])
                        for r in range(3):
                            fw.op("dve", lambda e, r=r: e.scalar_tensor_tensor(out=HINS[:, r + 1:r + 2], in0=HINS[:, r:r + 1], scalar=CARG[:, r, 0:1],
                                                                               in1=CARG[:, r, 1:2], op0=ALU.mult, op1=ALU.add),
                                  reads=[tHINS, tCARG], writes=[tHINS])
                        fw.op("dve", lambda e: e.tensor_copy(out=HINS[:, 7:8], in_=ST0[:, l * 8 + 4 + c:l * 8 + 4 + c + 1]), reads=[tC], writes=[tHINS])
                        for r in (3, 2, 1):
                            fw.op("dve", lambda e, r=r: e.scalar_tensor_tensor(out=HINS[:, 4 + r - 1:4 + r], in0=HINS[:, 4 + r:4 + r + 1],
                                                                               scalar=CARG[:, r, 2:3], in1=CARG[:, r, 3:4], op0=ALU.mult, op1=ALU.add),
                                  reads=[tHINS, tCARG], writes=[tHINS])
                        for e_ in range(2):
                            fw.op("dve", lambda e, e_=e_: e.tensor_scalar(out=HINS[:, 8 + e_:9 + e_], in0=HINS[:, 4 * e_:4 * e_ + 1], scalar1=SEL[:, 8:9],
                                                                         scalar2=None, op0=ALU.mult), reads=[tHINS, tC], writes=[tHINS])
                            for r in range(1, 4):
                                fw.op("dve", lambda e, e_=e_, r=r: e.scalar_tensor_tensor(
                                    out=HINS[:, 8 + e_:9 + e_], in0=HINS[:, 4 * e_ + r:4 * e_ + r + 1], scalar=SEL[:, 8 + r:9 + r],
                                    in1=HINS[:, 8 + e_:9 + e_], op0=ALU.mult, op1=ALU.add), reads=[tHINS, tC], writes=[tHINS])
                        for e_ in range(2):
                            aa, taa, bb_, tbb = AT[e_]
                            o_ = HS[e_][:, 1024:1536]
                            if e_ == 0:
                                fw.op("dve", lambda e, o_=o_, aa=aa, bb_=bb_: e.tensor_tensor_scan(
                                    out=o_, data0=aa[:], data1=bb_[:], initial=HINS[:, 8:9], op0=ALU.mult, op1=ALU.add),
                                    reads=[taa, tbb, tHINS], writes=[tHS[0][2]])
                            else:
                                fw.op("dve", lambda e, o_=o_, aa=aa, bb_=bb_: e.tensor_tensor_scan(
                                    out=rev(o_, 512), data0=rev(aa[:], 512), data1=rev(bb_[:], 512), initial=HINS[:, 9:10],
                                    op0=ALU.mult, op1=ALU.add), reads=[taa, tbb, tHINS], writes=[tHS[1][2]])
                        tm, ttm = next_gt()
                        fw.op("dve", lambda e, tm=tm: e.tensor_tensor(out=tm[:], in0=HS[0][:, 1024:1536], in1=HS[1][:, 1024:1536], op=ALU.add),
                              reads=[tHS[0][2], tHS[1][2]], writes=[ttm])
                        fw.op("dve", lambda e, tm=tm: e.tensor_tensor(out=YB[:, 2, c, 1024:1536], in0=tm[:], in1=GY[:, 1024:1536], op=ALU.mult),
                              reads=[ttm, tGY[2]], writes=[tYB[2][c][2]])
                        chk(f"lru{c}")
                    fw.dma("act", st_out[l], STO[:], reads=[tSTO])
                    fw.release([tUX] + tGY + tXC + tXCB + tHS[0] + tHS[1] + tGT + [tCAR, tCARG, tHINS, tSRG, twbd])
                if debug and l == 0:
                    dbg_list.append(("YB2", YB[:], [tYB[n][c][t] for n in range(3) for c in range(4) for t in range(NT)]))
                chk("lru")

                with Arena() as a5:
                    MG = sb("MG", [128, 8, TT], BF16, a5)
                    MF = sb("MF", [128, 2, TT], F32, a5)
                    GG = [sb(f"GG{j}", [128, 512], F32, a5) for j in range(2)]
                    GM = [sb(f"GM{j}", [128, 512], F32, a5) for j in range(2)]
                    tMG = [[T(f"MG{k}_{t}") for t in range(NT)] for k in range(8)]
                    tMF = [[T(f"MF{k}_{t}") for t in range(NT)] for k in range(2)]
                    tGG, tGM = [T("GG0"), T("GG1")], [T("GM0"), T("GM1")]
                    gi = 0
                    bi_ = 0
                    for jp in range(4):
                        for n in range(3):
                            bg, bb2 = P["gate"][bi_]
                            bi_ += 1
                            wg, twg = take(bg)
                            wb, twb = take(bb2)
                            for jj in range(2):
                                j = jp * 2 + jj
                                for t in range(NT):
                                    psg, tpsg = next_dps()
                                    dense(wg, twg, jj * 128, 8, hrhs, hrt, t, psg, tpsg)
                                    gq = gi % 2
                                    gi += 1
                                    fw.op("act", lambda e, gq=gq, psg=psg: e.activation(out=GG[gq][:], in_=psg[:], func=AF.Sigmoid),
                                          reads=[tpsg], writes=[tGG[gq]])
                                    psp, tpsp = next_dps()
                                    dense(wb, twb, jj * 128, 4, lambda k, t, n=n: YB[:, n, k, tsl(t)], lambda k, t, n=n: tYB[n][k][t], t, psp, tpsp)
                                    if n == 0:
                                        fw.op("dve", lambda e, gq=gq, psp=psp, jj=jj, t=t: e.tensor_tensor(out=MF[:, jj, tsl(t)], in0=GG[gq][:], in1=psp[:], op=ALU.mult),
                                              reads=[tGG[gq], tpsp], writes=[tMF[jj][t]])
                                    else:
                                        fw.op("dve", lambda e, gq=gq, psp=psp: e.tensor_tensor(out=GM[gq][:], in0=GG[gq][:], in1=psp[:], op=ALU.mult),
                                              reads=[tGG[gq], tpsp], writes=[tGM[gq]])
                                        if n == 1:
                                            fw.op("pool", lambda e, gq=gq, jj=jj, t=t: e.tensor_tensor(out=MF[:, jj, tsl(t)], in0=MF[:, jj, tsl(t)], in1=GM[gq][:], op=ALU.add),
                                                  reads=[tGM[gq], tMF[jj][t]], writes=[tMF[jj][t]])
                                        else:
                                            fw.op("pool", lambda e, gq=gq, jj=jj, t=t, j=j: e.tensor_tensor(out=MG[:, j, tsl(t)], in0=MF[:, jj, tsl(t)], in1=GM[gq][:], op=ALU.add),
                                                  reads=[tGM[gq], tMF[jj][t]], writes=[tMG[j][t]])
                    chk("merge")
                    if debug and l == 0:
                        pass
                    for mp in range(4):
                        w, tw = take(P["out"][mp])
                        for jj in range(2):
                            j = mp * 2 + jj
                            for t in range(NT):
                                g = grp_of_tile[t]
                                ps, tps = next_dps()
                                dense(w, tw, jj * 128, 8, lambda k, t: MG[:, k, tsl(t)], lambda k, t: tMG[k][t], t, ps, tps)
                                fw.op("dve", lambda e, ps=ps, j=j, t=t, g=g: e.scalar_tensor_tensor(
                                    out=X[:, j, tsl(t)], in0=ps[:], scalar=mv[:, 16 + j, g:g + 1], in1=X[:, j, tsl(t)], op0=ALU.mult, op1=ALU.add),
                                    reads=[tps, tmv, tX[j][t]], writes=[tX[j][t]])
                    fw.release([x for r_ in tMG for x in r_] + [x for r_ in tMF for x in r_] + tGG + tGM)
                chk("wout")
                if debug and l == 0:
                    dbg_list.append(("X1", X[:], [tX[k][t] for k in range(8) for t in range(NT)]))
                    if stop_after == "wout":
                        raise _Stop()

                with Arena() as a1:
                    SQ = [sb(f"SQ{j}", [128, 512], BF16, a1) for j in range(4)]
                    RS = [sb(f"RS{j}", [128, 512], F32, a1) for j in range(3)]
                    TMP = [sb(f"TMP{j}", [128, 512], F32, a1) for j in range(4)]
                    tSQ = [T(f"SQ{j}") for j in range(4)]
                    tRS = [T(f"RS{j}") for j in range(3)]
                    tTMP = [T(f"TMP{j}") for j in range(4)]
                    emit_norm(l, 1, (SQ, tSQ, RS, tRS, TMP, tTMP))
                    fw.release(tSQ + tRS + tTMP)

                with Arena() as a6:
                    AV = sb("AV", [128, 12, TT], BF16, a6)
                    SG = [sb(f"SG{j}", [128, 512], F32, a6) for j in range(2)]
                    tAV = [[T(f"AV{k}_{t}") for t in range(NT)] for k in range(12)]
                    tSG = [T("SG0"), T("SG1")]
                    si = 0
                    for (kind, idx, bid) in P["ffseq"]:
                        if kind == "ffin":
                            hf_, i = idx
                            w, tw = take(bid)
                            for t in range(NT):
                                psg, tpsg = next_dps()
                                dense(w, tw, 0, 8, hrhs, hrt, t, psg, tpsg)
                                sq_ = si % 2
                                si += 1
                                fw.op("act", lambda e, sq_=sq_, psg=psg: e.activation(out=SG[sq_][:], in_=psg[:], func=AF.Silu),
                                      reads=[tpsg], writes=[tSG[sq_]])
                                psu, tpsu = next_dps()
                                dense(w, tw, 128, 8, hrhs, hrt, t, psu, tpsu)
                                fw.op("dve", lambda e, sq_=sq_, psu=psu, i=i, t=t: e.tensor_tensor(out=AV[:, i, tsl(t)], in0=SG[sq_][:], in1=psu[:], op=ALU.mult),
                                      reads=[tSG[sq_], tpsu], writes=[tAV[i][t]])
                        elif kind == "mod":
                            assert bid == cons["i"]
                            cons["i"] += 1
                            emit_mod_block(l + 1, idx, bid)
                        else:
                            hf_, j, nk = idx
                            w, tw = take(bid)
                            for t in range(NT):
                                g = grp_of_tile[t]
                                ps, tps = next_dps()
                                fw.mm([lambda e, k=k, ps=ps, t=t: e.matmul(ps[:], lhsT=w[:, k, :], rhs=AV[:, k, tsl(t)], start=(k == 0), stop=(k == nk - 1))
                                       for k in range(nk)], reads=[tw] + [tAV[k][t] for k in range(nk)], writes=[tps])
                                fw.op("dve", lambda e, ps=ps, j=j, t=t, g=g: e.scalar_tensor_tensor(
                                    out=X[:, j, tsl(t)], in0=ps[:], scalar=mv[:, 40 + j, g:g + 1], in1=X[:, j, tsl(t)], op0=ALU.mult, op1=ALU.add),
                                    reads=[tps, tmv, tX[j][t]], writes=[tX[j][t]])
                    if l + 1 < depth:
                        finish_mod(l + 1)
                    fw.release([x for r_ in tAV for x in r_] + tSG)
                if debug and l == 0:
                    dbg_list.append(("X2", X[:], [tX[k][t] for k in range(8) for t in range(NT)]))
                chk(f"layer{l}")

                if l + 1 < depth:
                    with Arena() as a7:
                        XHO = sb("XHO", [128, 8, 16], F32, a7)
                        tXHO = T("XHO")
                        xrd = [tX[k][2] for k in range(8)]
                        fw.op("pool", lambda e: e.tensor_copy(out=XHO[:, :, 0:8], in_=X[:, :, 1024:1032]), reads=xrd, writes=[tXHO])
                        fw.op("pool", lambda e: e.tensor_copy(out=XHO[:, :, 8:16], in_=X[:, :, 1528:1536]), reads=xrd, writes=[tXHO])
                        fw.dma("pool", cx_in, XHO[:].rearrange("p k t -> p (k t)"), reads=[tXHO], writes=[tCXI])
                        fw.collective("AllGather", GROUPS, cx_in, cx_out.rearrange("r p n -> (r p) n"), reads=[tCXI], writes=[tCXO])
                        fw.release([tXHO])

            with Arena() as a8:
                SQ = [sb(f"SQ{j}", [128, 512], BF16, a8) for j in range(2)]
                RS = sb("RS", [128, 512], F32, a8)
                YO = [sb(f"YO{j}", [128, 512], F32, a8) for j in range(4)]
                tSQ, tRS, tYO = [T("SQ0"), T("SQ1")], T("RS"), [T(f"YO{j}") for j in range(4)]
                yv = yT.rearrange("(k p) t -> p k t", p=128)
                yi = 0
                for t in range(NT):
                    ps, tps = next_dps()
                    for k in range(8):
                        j = k % 2
                        fw.op("act", lambda e, k=k, j=j, t=t: e.activation(out=SQ[j][:], in_=X[:, k, tsl(t)], func=AF.Square),
                              reads=[tX[k][t]], writes=[tSQ[j]])
                        fw.mm([lambda e, k=k, j=j, ps=ps: e.matmul(ps[:], lhsT=ONES[:], rhs=SQ[j][:], start=(k == 0), stop=(k == 7))],
                              reads=[tSQ[j], tC], writes=[tps])
                    fw.op("act", lambda e, ps=ps: e.activation(out=RS[:], in_=ps[:], func=AF.Ln, bias=EPSC[:, 0:1]), reads=[tps, tC], writes=[tRS])
                    fw.op("act", lambda e: e.activation(out=RS[:], in_=RS[:], func=AF.Exp, scale=-0.5), reads=[tRS], writes=[tRS])
                    for k in range(8):
                        j = yi % 4
                        yi += 1
                        fw.op("dve", lambda e, k=k, j=j, t=t: e.scalar_tensor_tensor(out=YO[j][:], in0=X[:, k, tsl(t)], scalar=GF[:, k:k + 1], in1=RS[:],
                                                                                     op0=ALU.mult, op1=ALU.mult),
                              reads=[tX[k][t], tRS, tC], writes=[tYO[j]])
                        fw.dma("act", yv[:, k, tsl(t)], YO[j][:], reads=[tYO[j]])
                fw.release(tSQ + [tRS] + tYO)

        except _Stop:
            pass
        for (nm, ap_sb, tl) in dbg_list:
            dd = nc.dram_tensor("dbg_" + nm, list(ap_sb.shape), ap_sb.dtype, kind="ExternalOutput").ap()
            fw.dma("sp", dd, ap_sb, reads=tl)
        eng = fw.engs["sp"]
        for s_ in fw.sems:
            if s_.count > 0:
                eng.h.wait_ge(s_.h, s_.count)
    return nc


def _chunked(v):
    v = np.asarray(v, np.float32)
    n = v.shape[-1] // 128
    return np.ascontiguousarray(np.moveaxis(v.reshape(v.shape[:-1] + (n, 128)), -1, 0))


def prepare_inputs(inp, depth=DEPTH):
    f32 = np.float32
    LAYERED = ("w_mod", "b_mod", "g_norm1", "g_norm2", "w_in", "pool_w", "pool_scale", "na_rpb", "lru_conv_w", "lru_conv_b",
               "lru_wa", "lru_ba", "lru_wx", "lru_bx", "lru_lambda", "w_branch", "w_out", "w_ff_in", "w_ff_out")

    def g(k):
        a = np.asarray(inp[k], f32)
        if k in LAYERED:
            a = a[:depth]
        elif k in ("cache_k", "cache_v", "state_lru"):
            a = a[:, :depth]
        return a
    DEPTH = depth
    x_prompt, x_sample = g("x_prompt"), g("x_sample")
    cache_k, cache_v, state_lru = g("cache_k"), g("cache_v"), g("state_lru")
    c, c_ctx = g("c"), g("c_ctx")
    pv = np.zeros((128, DEPTH, NV), f32)
    pv[:, :, 0:8] = _chunked(g("g_norm1"))
    pv[:, :, 8:16] = _chunked(g("g_norm2"))
    pv[:, :, 16:64] = _chunked(g("b_mod"))
    pv[:, :, 64:68] = _chunked(g("pool_scale"))
    cw = _chunked(g("lru_conv_w"))
    pv[:, :, 68:84] = cw.reshape(128, DEPTH, 16)
    pv[:, :, 84:88] = _chunked(g("lru_conv_b"))
    pv[:, :, 88:96] = _chunked(g("lru_ba")).reshape(128, DEPTH, 8)
    pv[:, :, 96:104] = _chunked(g("lru_bx")).reshape(128, DEPTH, 8)
    pv[:, :, 104:112] = _chunked(g("lru_lambda")).reshape(128, DEPTH, 8)
    gf = _chunked(g("g_final"))
    poolw = np.ascontiguousarray(g("pool_w").transpose(0, 2, 1, 3).reshape(DEPTH, 128, 512))
    wa, wx = g("lru_wa"), g("lru_wx")
    bd = np.zeros((DEPTH, 128, 16, 128), f32)
    for e in range(2):
        for which, w in enumerate((wa, wx)):
            for ch in range(4):
                mat = (e * 2 + which) * 4 + ch
                for hf in range(2):
                    bd[:, hf * 64:(hf + 1) * 64, mat, hf * 64:(hf + 1) * 64] = w[:, e, ch * 2 + hf]
    bd = bd.reshape(DEPTH, 128, 2048)
    rpb = g("na_rpb")
    wk = np.arange(64)[:, None]
    wq = np.arange(64)[None, :]
    dc = np.clip(wk - wq, -15, 15) + 15
    col0 = np.clip(np.arange(64) - 8, 0, 48)
    col_ok = (wk >= col0[None, :]) & (wk < col0[None, :] + 16)
    G = np.empty((DEPTH, 64, 8, 15, 64), f32)
    for ep in range(15):
        gath = rpb[:, :, 14 - ep][:, :, dc]
        gath = np.where(col_ok[None, None], gath, f32(NEG))
        G[:, :, :, ep, :] = gath.transpose(0, 2, 1, 3)
    slot16 = (np.arange(1024)[None, :] // 64 == np.arange(16)[:, None]).astype(f32)
    slot = np.zeros((128, 1024), f32)
    slot[0:16] = slot16
    slot[64:80] = slot16
    icp = np.zeros((128, 4, 2, 8), f32)
    for gi in range(4):
        half = 2 ** gi
        for i in range(8):
            t = i
            icp[:, gi, 0, i] = 1.0 / (min(t + half, 256) - max(t - half, 0))
            t = 248 + i
            icp[:, gi, 1, i] = 1.0 / (min(t + half, 256) - max(t - half, 0))
    shared = dict(pvec=pv, gfin=gf, w_mod=g("w_mod"), w_in=g("w_in"), w_branch=g("w_branch"), w_out=g("w_out"),
                  w_ff_in=g("w_ff_in"), w_ff_out=g("w_ff_out"), pool_w=poolw, lru_bd=bd, rpbG=G, slotoh=slot, icp=icp)
    in_maps = []
    for core in range(NCORES):
        b, j = core // 4, core % 4
        xp = x_prompt[4 * core:4 * core + 4].reshape(TP, D)
        xs = x_sample[b, j * TS:(j + 1) * TS]
        xT = np.ascontiguousarray(np.concatenate([xp, xs], 0).T)
        halo = np.zeros((16, D), f32)
        if j > 0:
            halo[0:8] = x_sample[b, j * TS - 8:j * TS]
        if j < 3:
            halo[8:16] = x_sample[b, (j + 1) * TS:(j + 1) * TS + 8]
        xh0 = np.ascontiguousarray(halo.T.reshape(8, 128, 16).transpose(1, 0, 2))
        ckT = np.ascontiguousarray(cache_k[b].reshape(DEPTH, 512, 512).transpose(0, 2, 1))
        cvv = np.ascontiguousarray(cache_v[b].reshape(DEPTH, 512, 512))
        st0 = _chunked(state_lru[b]).reshape(128, DEPTH * 8)
        cvec = np.stack([_chunked(c_ctx), _chunked(c[b])], -1)
        R0 = 8 * j
        rowb16 = np.zeros((16, 512), f32)
        for i in range(8):
            r = R0 + i
            row0 = min(max(r - 4, 0), 24)
            for m in range(16):
                rr = R0 - 4 + m
                ok = (row0 <= rr < row0 + 8) and (0 <= rr < 32)
                if not ok:
                    rowb16[m, i * 64:(i + 1) * 64] = -240000.0
        rowb = np.zeros((128, 512), f32)
        rowb[0:16] = rowb16
        rowb[64:80] = rowb16
        sel = np.zeros((128, 12), f32)
        if j > 0:
            sel[:, j - 1] = 1
        if j < 3:
            sel[:, 4 + j + 1] = 1
        sel[:, 8 + j] = 1
        flags = np.zeros((128, 2), f32)
        flags[:, 0] = float(j > 0)
        flags[:, 1] = float(j < 3)
        ics = np.zeros((128, 4, 2, 8), f32)
        for gi in range(4):
            half = 2 ** gi
            for i in range(8):
                t = j * TS + i
                ics[:, gi, 0, i] = 1.0 / (min(t + half, 2048) - max(t - half, 0))
                t = j * TS + 504 + i
                ics[:, gi, 1, i] = 1.0 / (min(t + half, 2048) - max(t - half, 0))
        m = dict(shared)
        m.update(xT=xT, xh0=xh0, ckT=ckT, cv=cvv, st0=np.ascontiguousarray(st0), cvec=np.ascontiguousarray(cvec),
                 rowb=rowb, sel=sel, flags=flags, ics=ics)
        in_maps.append(m)
    return in_maps


def assemble_outputs(results):
    f32 = np.float32
    y_prompt = np.empty((32, 256, D), f32)
    y_sample = np.empty((2, 2048, D), f32)
    nk = np.empty((32, DEPTH, 256, 8, 64), f32)
    nv = np.empty((32, DEPTH, 256, 8, 64), f32)
    ns = np.empty((32, DEPTH, 2, 512), f32)
    for core in range(NCORES):
        r = results[core]
        b, j = core // 4, core % 4
        yT = np.asarray(r["yT"])
        for s in range(4):
            y_prompt[4 * core + s] = yT[:, s * 256:(s + 1) * 256].T
        y_sample[b, j * TS:(j + 1) * TS] = yT[:, TP:TT].T
        kT = np.asarray(r["kT_out"])
        vv = np.asarray(r["v_out"])
        so = np.asarray(r["st_out"])
        for s in range(4):
            nk[4 * core + s] = kT[:, :, s * 256:(s + 1) * 256].transpose(0, 2, 1).reshape(DEPTH, 256, 8, 64)
            nv[4 * core + s] = vv[:, s * 256:(s + 1) * 256, :].reshape(DEPTH, 256, 8, 64)
            blk = so[:, :, s * 8:(s + 1) * 8].reshape(DEPTH, 128, 2, 4)
            ns[4 * core + s] = blk.transpose(0, 2, 3, 1).reshape(DEPTH, 2, 512)
    return y_prompt, y_sample, nk, nv, ns


_NC_CACHE = {}


def kernel(**inputs):
    if "nc" not in _NC_CACHE:
        _NC_CACHE["nc"] = build_program()
    nc = _NC_CACHE["nc"]
    in_maps = prepare_inputs(inputs)
    res = run_bass_kernel_spmd(nc, in_maps, core_ids=list(range(NCORES)))
    return assemble_outputs(res.results)
```
